# Optimizing a Trainium2 kernel written in Bass

```python
import math
import jax, jax.numpy as jnp
from jax import lax
import numpy as np

D_MODEL = 1024
BATCH = 8
SEQ = 2048
DEPTH = 4
DEC_BATCH = 32
DEC_SEQ = 1
PAST_LEN = 8192
PAGE_SIZE = 128

H_A = 8
DH_QK = D_MODEL // 32
DH_V = 2 * DH_QK
D_QK = 2 * H_A * DH_QK
D_A = H_A * DH_V
D_B = D_MODEL // 2
W_B = 3
D_C = D_MODEL
W_C = 31
D_FF = 2816
N_BUCKETS = 32
MAX_EXACT = N_BUCKETS // 2
MAX_DISTANCE = 128
Q_BLOCK = 128
N_EVEN = (DEPTH + 1) // 2
N_ODD = DEPTH // 2
D_IN_EVEN = 2 * D_QK + D_A + 3 * D_B
EPS = 1e-6

kernel_name = "diffattn_shortconv_conformer_macaron_step"


def rmsnorm(x, g):
    xf = x.astype(jnp.float32)
    y = xf * lax.rsqrt(jnp.mean(xf * xf, axis=-1, keepdims=True) + EPS) * g.astype(jnp.float32)
    return y.astype(x.dtype)


def layernorm(x, g, b):
    xf = x.astype(jnp.float32)
    mu = jnp.mean(xf, axis=-1, keepdims=True)
    xc = xf - mu
    y = xc * lax.rsqrt(jnp.mean(xc * xc, axis=-1, keepdims=True) + EPS) * g.astype(jnp.float32) + b.astype(jnp.float32)
    return y.astype(x.dtype)


def half_ffn(x, g, w_gate, w_up, w_down):
    h = rmsnorm(x, g)
    return 0.5 * ((jax.nn.silu(h @ w_gate) * (h @ w_up)) @ w_down)


def t5_bucket(rel):
    n = jnp.maximum(rel, 0)
    nf = jnp.maximum(n, 1).astype(jnp.float32)
    large = MAX_EXACT + (jnp.log(nf / MAX_EXACT) / math.log(MAX_DISTANCE / MAX_EXACT)
                         * (N_BUCKETS - MAX_EXACT)).astype(jnp.int32)
    large = jnp.minimum(large, N_BUCKETS - 1)
    return jnp.where(n < MAX_EXACT, n, large)


def causal_dwconv(u, prev, w):
    width = w.shape[0]
    u_ext = jnp.concatenate([prev.astype(u.dtype), u], axis=1)
    out = lax.conv_general_dilated(u_ext, w[:, None, :].astype(u.dtype), window_strides=(1,), padding='VALID',
                                   dimension_numbers=('NWC', 'WIO', 'NWC'), feature_group_count=u.shape[-1])
    return out, u_ext[:, u_ext.shape[1] - (width - 1):]


def diff_attn_core(q, k, v, q_pos, k_pos, lam, rel_bias):
    s = jnp.einsum('bqhcd,bkhcd->bhcqk', q, k).astype(jnp.float32) * (DH_QK ** -0.5)
    bias = rel_bias[t5_bucket(q_pos[:, None] - k_pos[None, :])].astype(jnp.float32)
    s = s + jnp.transpose(bias, (2, 0, 1))[None, :, None]
    mask = k_pos[None, :] <= q_pos[:, None]
    s = jnp.where(mask, s, -1e30)
    p = jax.nn.softmax(s, axis=-1)
    a = p[:, :, 0] - lam * p[:, :, 1]
    return jnp.einsum('bhqk,bkhe->bqhe', a.astype(v.dtype), v)


def even_mixer(h, k_past, v_past, conv_prev, pos0, w_in, w_out, q_gain, k_gain,
               lq1, lk1, lq2, lk2, subln_gain, conv_w, rel_bias, lam_init):
    b, t, _ = h.shape
    z = h @ w_in
    q = rmsnorm(z[..., :D_QK].reshape(b, t, H_A, 2, DH_QK), q_gain)
    k = rmsnorm(z[..., D_QK:2 * D_QK].reshape(b, t, H_A, 2, DH_QK), k_gain)
    v = z[..., 2 * D_QK:2 * D_QK + D_A].reshape(b, t, H_A, DH_V)
    o = 2 * D_QK + D_A
    gate_b = z[..., o:o + D_B]
    gate_c = z[..., o + D_B:o + 2 * D_B]
    x_in = z[..., o + 2 * D_B:]
    f32 = jnp.float32
    lam = (jnp.exp(jnp.sum(lq1.astype(f32) * lk1.astype(f32)))
           - jnp.exp(jnp.sum(lq2.astype(f32) * lk2.astype(f32))) + lam_init)
    if k_past is None:
        k_all, v_all = k, v
        k_pos = jnp.arange(t)
        nb = t // Q_BLOCK
        q_blocks = q.reshape(b, nb, Q_BLOCK, H_A, 2, DH_QK).swapaxes(0, 1)
        starts = pos0 + jnp.arange(nb) * Q_BLOCK

        def blk(args):
            qb, s0 = args
            return diff_attn_core(qb, k_all, v_all, s0 + jnp.arange(Q_BLOCK), k_pos, lam, rel_bias)

        att = lax.map(blk, (q_blocks, starts)).swapaxes(0, 1).reshape(b, t, H_A, DH_V)
    else:
        k_all = jnp.concatenate([k_past.astype(k.dtype), k], axis=1)
        v_all = jnp.concatenate([v_past.astype(v.dtype), v], axis=1)
        k_pos = jnp.arange(k_all.shape[1])
        att = diff_attn_core(q, k_all, v_all, pos0 + jnp.arange(t), k_pos, lam, rel_bias)
    att = rmsnorm(att, subln_gain) * (1.0 - lam_init)
    conv_out, conv_state = causal_dwconv(gate_c * x_in, conv_prev, conv_w)
    y = jnp.concatenate([att.reshape(b, t, D_A), gate_b * conv_out], axis=-1) @ w_out
    return y, k.reshape(b, t, 2 * H_A, DH_QK), v, conv_state


def conformer_conv(h, conv_prev, w_pw1, b_pw1, conv_w, conv_b, ln_g, ln_b, w_pw2, b_pw2):
    z = h @ w_pw1 + b_pw1
    u = z[..., :D_C] * jax.nn.sigmoid(z[..., D_C:])
    c, st = causal_dwconv(u, conv_prev, conv_w)
    c = layernorm(c + conv_b, ln_g, ln_b)
    return jax.nn.silu(c) @ w_pw2 + b_pw2, st


def setup_inputs(seed: int = 0) -> dict:
    key = jax.random.key(seed)
    ks = jax.random.split(key, 40)
    n_pages = PAST_LEN // PAGE_SIZE
    n_used = DEC_BATCH * n_pages
    n_phys = n_used + n_used // 4
    nrm = lambda k, shape, s=1.0: jax.random.normal(k, shape, jnp.float32) * s
    gain = lambda k, shape: 1.0 + 0.02 * jax.random.normal(k, shape, jnp.float32)
    page_table = jax.random.permutation(ks[6], n_phys)[:n_used].reshape(DEC_BATCH, n_pages).astype(jnp.int32)
    return {
        'x_prompt': nrm(ks[0], (BATCH, SEQ, D_MODEL)),
        'x_sample': nrm(ks[1], (DEC_BATCH, DEC_SEQ, D_MODEL)),
        'cache_k': nrm(ks[2], (N_EVEN, n_phys, PAGE_SIZE, 2 * H_A, DH_QK)),
        'cache_v': nrm(ks[3], (N_EVEN, n_phys, PAGE_SIZE, H_A, DH_V)),
        'state_conv_b': nrm(ks[4], (N_EVEN, DEC_BATCH, W_B - 1, D_B)),
        'state_conv_c': nrm(ks[5], (N_ODD, DEC_BATCH, W_C - 1, D_C), 0.5),
        'page_table': page_table,
        'rel_bias': nrm(ks[7], (N_BUCKETS, H_A), 0.5),
        'norm_ffn1': gain(ks[8], (DEPTH, D_MODEL)),
        'ffn1_w_gate': nrm(ks[9], (DEPTH, D_MODEL, D_FF), D_MODEL ** -0.5),
        'ffn1_w_up': nrm(ks[10], (DEPTH, D_MODEL, D_FF), D_MODEL ** -0.5),
        'ffn1_w_down': nrm(ks[11], (DEPTH, D_FF, D_MODEL), D_FF ** -0.5),
        'norm_mix': gain(ks[12], (DEPTH, D_MODEL)),
        'norm_ffn2': gain(ks[13], (DEPTH, D_MODEL)),
        'ffn2_w_gate': nrm(ks[14], (DEPTH, D_MODEL, D_FF), D_MODEL ** -0.5),
        'ffn2_w_up': nrm(ks[15], (DEPTH, D_MODEL, D_FF), D_MODEL ** -0.5),
        'ffn2_w_down': nrm(ks[16], (DEPTH, D_FF, D_MODEL), D_FF ** -0.5),
        'w_in_even': nrm(ks[17], (N_EVEN, D_MODEL, D_IN_EVEN), D_MODEL ** -0.5),
        'w_out_even': nrm(ks[18], (N_EVEN, D_A + D_B, D_MODEL), (D_A + D_B) ** -0.5),
        'q_norm_gain': gain(ks[19], (N_EVEN, DH_QK)),
        'k_norm_gain': gain(ks[20], (N_EVEN, DH_QK)),
        'lambda_q1': nrm(ks[21], (N_EVEN, DH_QK), 0.1),
        'lambda_k1': nrm(ks[22], (N_EVEN, DH_QK), 0.1),
        'lambda_q2': nrm(ks[23], (N_EVEN, DH_QK), 0.1),
        'lambda_k2': nrm(ks[24], (N_EVEN, DH_QK), 0.1),
        'subln_gain': gain(ks[25], (N_EVEN, DH_V)),
        'conv_b_w': nrm(ks[26], (N_EVEN, W_B, D_B), W_B ** -0.5),
        'w_pw1': nrm(ks[27], (N_ODD, D_MODEL, 2 * D_C), D_MODEL ** -0.5),
        'b_pw1': nrm(ks[28], (N_ODD, 2 * D_C), 0.02),
        'conv_c_w': nrm(ks[29], (N_ODD, W_C, D_C), W_C ** -0.5),
        'conv_c_b': nrm(ks[30], (N_ODD, D_C), 0.02),
        'ln_c_gain': gain(ks[31], (N_ODD, D_C)),
        'ln_c_bias': nrm(ks[32], (N_ODD, D_C), 0.02),
        'w_pw2': nrm(ks[33], (N_ODD, D_C, D_MODEL), D_C ** -0.5),
        'b_pw2': nrm(ks[34], (N_ODD, D_MODEL), 0.02),
    }


def reference(x_prompt, x_sample, cache_k, cache_v, state_conv_b, state_conv_c, page_table, rel_bias,
              norm_ffn1, ffn1_w_gate, ffn1_w_up, ffn1_w_down, norm_mix, norm_ffn2,
              ffn2_w_gate, ffn2_w_up, ffn2_w_down, w_in_even, w_out_even, q_norm_gain, k_norm_gain,
              lambda_q1, lambda_k1, lambda_q2, lambda_k2, subln_gain, conv_b_w,
              w_pw1, b_pw1, conv_c_w, conv_c_b, ln_c_gain, ln_c_bias, w_pw2, b_pw2):
    bp, tp, _ = x_prompt.shape
    bs, ts, _ = x_sample.shape
    past_len = page_table.shape[1] * cache_k.shape[2]
    xp, xs = x_prompt, x_sample
    kp_l, vp_l, ks_l, vs_l, cbp_l, cbs_l, ccp_l, ccs_l = [], [], [], [], [], [], [], []
    for li in range(DEPTH):
        xp = xp + half_ffn(xp, norm_ffn1[li], ffn1_w_gate[li], ffn1_w_up[li], ffn1_w_down[li])
        xs = xs + half_ffn(xs, norm_ffn1[li], ffn1_w_gate[li], ffn1_w_up[li], ffn1_w_down[li])
        hp = rmsnorm(xp, norm_mix[li])
        hs = rmsnorm(xs, norm_mix[li])
        if li % 2 == 0:
            e = li // 2
            lam_init = 0.8 - 0.6 * math.exp(-0.3 * li)
            prm = (w_in_even[e], w_out_even[e], q_norm_gain[e], k_norm_gain[e], lambda_q1[e], lambda_k1[e],
                   lambda_q2[e], lambda_k2[e], subln_gain[e], conv_b_w[e], rel_bias, lam_init)
            yp, kp, vp, cbp = even_mixer(hp, None, None, jnp.zeros((bp, W_B - 1, D_B), hp.dtype), 0, *prm)
            k_past = cache_k[e][page_table].reshape(bs, past_len, H_A, 2, DH_QK)
            v_past = cache_v[e][page_table].reshape(bs, past_len, H_A, DH_V)
            ys, ks_, vs_, cbs = even_mixer(hs, k_past, v_past, state_conv_b[e], past_len, *prm)
            kp_l.append(kp); vp_l.append(vp); ks_l.append(ks_); vs_l.append(vs_)
            cbp_l.append(cbp); cbs_l.append(cbs)
        else:
            o = li // 2
            prm = (w_pw1[o], b_pw1[o], conv_c_w[o], conv_c_b[o], ln_c_gain[o], ln_c_bias[o], w_pw2[o], b_pw2[o])
            yp, ccp = conformer_conv(hp, jnp.zeros((bp, W_C - 1, D_C), hp.dtype), *prm)
            ys, ccs = conformer_conv(hs, state_conv_c[o], *prm)
            ccp_l.append(ccp); ccs_l.append(ccs)
        xp = xp + yp
        xs = xs + ys
        xp = xp + half_ffn(xp, norm_ffn2[li], ffn2_w_gate[li], ffn2_w_up[li], ffn2_w_down[li])
        xs = xs + half_ffn(xs, norm_ffn2[li], ffn2_w_gate[li], ffn2_w_up[li], ffn2_w_down[li])
    return (xp, xs, jnp.stack(kp_l), jnp.stack(vp_l), jnp.stack(ks_l), jnp.stack(vs_l),
            jnp.stack(cbp_l), jnp.stack(cbs_l), jnp.stack(ccp_l), jnp.stack(ccs_l))
```

```python
import bisect
import contextlib
import math
import numpy as np
import concourse.bass as bass
import concourse.mybir as mybir
from concourse.bass_utils import run_bass_kernel_spmd

F32 = mybir.dt.float32
BF16 = mybir.dt.bfloat16
I32 = mybir.dt.int32
U8 = mybir.dt.uint8
AF = mybir.ActivationFunctionType
ALU = mybir.AluOpType
AX = mybir.AxisListType

D = 1024
NPROMPT = 2048
NSAMP = 4
NT = NPROMPT + NSAMP
DEPTH = 4
DFF = 2816
NFF = DFF // 128
GFF = 11
EPS = 1e-6
FT = [(342 * i, 342 * (i + 1)) for i in range(6)]
MT = [(0, 512), (512, 1024), (1024, 1536), (1536, 2048), (2048, 2052)]


class IMap:
    def __init__(self):
        self.starts = [0]
        self.data = {0: [1 << 40, None, []]}

    def _split(self, x):
        i = bisect.bisect_right(self.starts, x) - 1
        s = self.starts[i]
        e, w, r = self.data[s]
        if s == x or x >= e:
            return
        self.data[s] = [x, w, list(r)]
        self.data[x] = [e, w, list(r)]
        bisect.insort(self.starts, x)

    def access(self, a, b, oid, is_write, deps):
        self._split(a)
        self._split(b)
        i = bisect.bisect_left(self.starts, a)
        n = len(self.starts)
        while i < n and self.starts[i] < b:
            seg = self.data[self.starts[i]]
            if is_write:
                if seg[1] is not None:
                    deps.add((seg[1], 'waw'))
                for r in seg[2]:
                    deps.add((r, 'war'))
                seg[1] = oid
                seg[2] = []
            else:
                if seg[1] is not None:
                    deps.add((seg[1], 'raw'))
                seg[2].append(oid)
            i += 1


class Prog:
    COMPUTE = ('pe', 'act', 'dve', 'pool')
    ENGS = ('pe', 'act', 'dve', 'pool', 'sp')

    def __init__(self, nc):
        self.nc = nc
        self.ops = []
        self.buf = {}
        self.imaps = {}
        self.chan_last = {}
        self.chan_cnt = {}

    def _acc(self, k, oid, is_write, deps):
        if isinstance(k, tuple) and k and k[0] == 'iv':
            _, space, a, b = k
            self.imaps.setdefault(space, IMap()).access(a, b, oid, is_write, deps)
            return
        st = self.buf.setdefault(k, [None, []])
        if is_write:
            if st[0] is not None:
                deps.add((st[0], 'waw'))
            for r in st[1]:
                deps.add((r, 'war'))
            self.buf[k] = [oid, []]
        else:
            if st[0] is not None:
                deps.add((st[0], 'raw'))
            st[1].append(oid)

    def op(self, eng, fn, reads=(), writes=(), chan=None):
        oid = len(self.ops)
        deps = set()
        is_dma = chan is not None
        for k in reads:
            self._acc(k, oid, False, deps)
        for k in writes:
            self._acc(k, oid, True, deps)
        if is_dma and chan in self.chan_last:
            deps.add((self.chan_last[chan], 'chan'))
        dma_idx = None
        if is_dma:
            self.chan_last[chan] = oid
            dma_idx = self.chan_cnt.get(chan, 0) + 1
            self.chan_cnt[chan] = dma_idx
        fdeps = set()
        for (p, kind) in deps:
            if p == oid:
                continue
            po = self.ops[p]
            if po['chan'] is None and not is_dma and po['eng'] == eng:
                if eng == 'pe' or kind != 'raw':
                    continue
            fdeps.add(p)
        self.ops.append(dict(eng=eng, fn=fn, deps=fdeps, chan=chan, dma_idx=dma_idx, sig=False, sig_idx=None))
        return oid

    def emit(self):
        nc = self.nc
        ops = self.ops
        for o in ops:
            for p in o['deps']:
                ops[p]['sig'] = True
        cnt = {e: 0 for e in self.COMPUTE}
        for o in ops:
            if o['chan'] is None and o['sig']:
                cnt[o['eng']] += 1
                o['sig_idx'] = cnt[o['eng']]
        chans = sorted(self.chan_cnt.keys(), key=str)
        with contextlib.ExitStack() as es:
            sem = {}
            for e in self.COMPUTE:
                sem[('e', e)] = es.enter_context(nc.semaphore('s_' + e))
            for ci, c in enumerate(chans):
                sem[('c', c)] = es.enter_context(nc.semaphore('c%d' % ci))
            block = es.enter_context(nc.Block())
            by_eng = {e: [] for e in self.ENGS}
            for i, o in enumerate(ops):
                by_eng[o['eng']].append(i)

            def run(engname, engobj):
                waited = {}
                for i in by_eng[engname]:
                    o = ops[i]
                    need = {}
                    for p in o['deps']:
                        po = ops[p]
                        if po['chan'] is None:
                            k = ('e', po['eng'])
                            v = po['sig_idx']
                        else:
                            k = ('c', po['chan'])
                            v = 16 * po['dma_idx']
                        if need.get(k, 0) < v:
                            need[k] = v
                    for k, v in need.items():
                        if waited.get(k, 0) >= v:
                            continue
                        engobj.wait_ge(sem[k], v)
                        waited[k] = v
                    ins = o['fn'](engobj)
                    if o['chan'] is not None:
                        ins.then_inc(sem[('c', o['chan'])], 16)
                    elif o['sig']:
                        ins.then_inc(sem[('e', engname)], 1)
                if engname == 'sp':
                    for c in chans:
                        engobj.wait_ge(sem[('c', c)], 16 * self.chan_cnt[c])

            block.tensor(lambda e: run('pe', e))
            block.scalar(lambda e: run('act', e))
            block.vector(lambda e: run('dve', e))
            block.gpsimd(lambda e: run('pool', e))
            block.sync(lambda e: run('sp', e))


class Ten:
    def __init__(self, big, off, n0, n1, dt, esz):
        assert off % 32 == 0
        self.off, self.n0, self.n1, self.esz = off, n0, n1, esz
        self.nbytes = n0 * n1 * esz
        self.v = big[:, off:off + self.nbytes].bitcast(dt).rearrange("p (a b) -> p a b", b=n1)

    def r(self, i=None, lo=0, hi=None, i1=None):
        if hi is None:
            hi = self.n1
        if i is None:
            i, i1 = 0, self.n0
        elif i1 is None:
            i1 = i + 1
        if lo == 0 and hi == self.n1:
            return [('iv', 'sb', self.off + i * self.n1 * self.esz, self.off + i1 * self.n1 * self.esz)]
        return [('iv', 'sb', self.off + (j * self.n1 + lo) * self.esz, self.off + (j * self.n1 + hi) * self.esz)
                for j in range(i, i1)]


class Alloc:
    def __init__(self, big, limit):
        self.big, self.limit, self.cur = big, limit, 0

    def ten(self, n0, n1, dt):
        esz = {F32: 4, BF16: 2, I32: 4}[dt]
        t = Ten(self.big, self.cur, n0, n1, dt, esz)
        self.cur += (t.nbytes + 31) // 32 * 32
        assert self.cur <= self.limit, (self.cur, self.limit)
        return t

    def at(self, off, n0, n1, dt):
        esz = {F32: 4, BF16: 2, I32: 4}[dt]
        t = Ten(self.big, off, n0, n1, dt, esz)
        assert off + t.nbytes <= self.limit
        return t


NEG = -30000.0
SCALE = 32 ** -0.5
NTAB = 639
TS = 4
NPAGES = 64


def t5_bucket_np(n):
    n = np.asarray(n, np.int64)
    nn = np.maximum(n, 0)
    nf = np.maximum(nn, 1).astype(np.float32)
    large = 16 + (np.log(nf / np.float32(16)) / np.float32(math.log(128 / 16)) * np.float32(16)).astype(np.int32)
    large = np.minimum(large, 31)
    return np.where(nn < 16, nn, large).astype(np.int64)


def static_tables():
    consts = np.zeros((128, 7, 128), np.float32)
    consts[:, 0] = np.eye(128)
    consts[:, 1] = 1.0 / 1024
    consts[:, 2] = np.kron(np.eye(4), np.ones((32, 32))) / 32
    consts[:, 3] = np.eye(128)[::-1]
    consts[:, 4] = 1.0
    consts[:, 5, 0:64] = 1.0
    consts[:, 6, 64:128] = 1.0
    lay = {}
    cur = 0

    def add(name, n):
        nonlocal cur
        lay[name] = (cur, n)
        cur += n
    add('iota', 1); add('sgn0', 1); add('sgn1', 1); add('maskd', 512); add('pair', 8); add('maskp', 128); add('selc', 4); add('iota32', 32); add('negm', 1)
    c2 = np.zeros((128, cur), np.float32)
    c2[:, lay['iota'][0]] = np.arange(128)
    c2[:, lay['iota32'][0]:lay['iota32'][0] + 32] = np.arange(32)[None, :]
    c2[63, lay['negm'][0]] = NEG
    c2[127, lay['negm'][0]] = NEG
    for base in (0, 32):
        for hs in range(16):
            c2[base + hs, lay['sgn0'][0]] = 1.0 if hs % 2 == 0 else 0.0
            c2[base + hs, lay['sgn1'][0]] = -1.0 if hs % 2 == 1 else 0.0
            h = hs // 2
            c2[base + hs, lay['maskd'][0] + h * 64: lay['maskd'][0] + (h + 1) * 64] = 1.0
            c2[base + hs, lay['pair'][0] + h] = 1.0
    for h in range(8):
        r = h % 2
        c2[h, lay['maskp'][0] + r * 64: lay['maskp'][0] + (r + 1) * 64] = 1.0
        c2[h, lay['selc'][0] + h // 2] = 1.0
    ds_ = np.zeros((33, NTAB + 256), np.float32)
    for i in range(NTAB):
        n = i - 255
        if n < 0:
            ds_[32, i] = NEG
        else:
            ds_[int(t5_bucket_np(n)), i] += 1.0
            ds_[31, i] -= 1.0
    for p in range(128):
        ds_[int(t5_bucket_np(128 - p)), NTAB + p] += 1.0
        ds_[31, NTAB + p] -= 1.0
    ds_[0, NTAB + 128] += 1.0
    ds_[31, NTAB + 128] -= 1.0
    ds_[32, NTAB + 129:NTAB + 256] = NEG
    return consts, c2, lay, ds_


def vec_layout():
    lay = {}
    cur = 0

    def add(name, n):
        nonlocal cur
        lay[name] = (cur, n)
        cur += n
    for l in range(DEPTH):
        add(('n1', l), 8)
        add(('nm', l), 8)
        add(('n2', l), 8)
    for o in range(2):
        add(('bpw1', o), 16)
        add(('ccb', o), 8)
        add(('lng', o), 8)
        add(('lnb', o), 8)
        add(('bpw2', o), 8)
        add(('ccw', o), 31 * 8)
    for e in range(2):
        add(('cbw', e), 3 * 4)
        add(('qg', e), 1)
        add(('kg', e), 1)
    return lay, cur


def build_vecs(inp):
    lay, ncol = vec_layout()
    V = np.zeros((128, ncol), np.float32)

    def put(name, vec):
        c0, n = lay[name]
        V[:, c0:c0 + n] = np.asarray(vec, np.float32).reshape(n, 128).T
    for l in range(DEPTH):
        put(('n1', l), inp['norm_ffn1'][l])
        put(('nm', l), inp['norm_mix'][l])
        put(('n2', l), inp['norm_ffn2'][l])
    for o in range(2):
        put(('bpw1', o), inp['b_pw1'][o])
        put(('ccb', o), inp['conv_c_b'][o])
        put(('lng', o), inp['ln_c_gain'][o])
        put(('lnb', o), inp['ln_c_bias'][o])
        put(('bpw2', o), inp['b_pw2'][o])
        put(('ccw', o), np.asarray(inp['conv_c_w'][o]).reshape(-1))
    for e in range(2):
        put(('cbw', e), np.asarray(inp['conv_b_w'][e]).reshape(-1))
        put(('qg', e), np.tile(np.asarray(inp['q_norm_gain'][e]), 4))
        put(('kg', e), np.tile(np.asarray(inp['k_norm_gain'][e]), 4))
    return V


def build_nc(depth=DEPTH):
    nc = bass.Bass("TRN2", target_bir_lowering=False)
    lay, ncol = vec_layout()
    _, _, lay2, _ = static_tables()
    nc2 = sum(v[1] for v in lay2.values())
    n_even = (depth + 1) // 2
    n_odd = depth // 2

    def din(name, shape, dtype=F32):
        return nc.dram_tensor(name, list(shape), dtype, kind="ExternalInput").ap()

    def dout(name, shape, dtype=F32):
        return nc.dram_tensor(name, list(shape), dtype, kind="ExternalOutput").ap()

    xT = din("xT", [128, 8, NT])
    vecsT = din("vecsT", [128, ncol])
    consts = din("consts", [128, 7, 128])
    cst2 = din("cst2", [128, nc2])
    dstat = din("dstat", [33, NTAB + 256])
    relb = din("relb", [33, 8])
    lamv = din("lamv", [2, 4, 32])
    sgrow = din("sgrow", [2, 64])
    scb = din("scb", [128, 2, 4, 2, 4])
    scc = din("scc", [128, 2, 8, 4, 30])
    ptab = din("ptab", [1, NSAMP * NPAGES], I32)
    cache_k = [din("cache_k%d" % i, [2560 * 128, 512]) for i in range(2)]
    cache_v = [din("cache_v%d" % i, [2560 * 128, 512]) for i in range(2)]
    NSTEP = 128 // TS
    wg = [din("ffn1_w_gate", [DEPTH, D, DFF]), din("ffn2_w_gate", [DEPTH, D, DFF])]
    wu = [din("ffn1_w_up", [DEPTH, D, DFF]), din("ffn2_w_up", [DEPTH, D, DFF])]
    wd = [din("ffn1_w_down", [DEPTH, DFF, D]), din("ffn2_w_down", [DEPTH, DFF, D])]
    w_in = din("w_in_even", [2, D, 3072])
    w_out = din("w_out_even", [2, D, D])
    w_pw1 = din("w_pw1", [2, D, 2048])
    w_pw2 = din("w_pw2", [2, D, D])
    yT = dout("yT", [128, 8, NT])
    nkT = dout("nkT", [2, 128, 4, NT])
    nvT = dout("nvT", [2, 128, 4, NT])
    cbpT = dout("cbpT", [2, 128, 4, 2])
    cbs0T = dout("cbs0T", [2, 128, 4, 4])
    cbs1T = dout("cbs1T", [2, 128, 4, 4])
    ccpT = dout("ccpT", [2, 128, 8, 30])
    ccsoT = dout("ccsoT", [2, 128, 8, 4, 29])
    ccsnT = dout("ccsnT", [2, 128, 8, 4])
    tabd = nc.dram_tensor("tabd", [8, NTAB], F32)

    P = Prog(nc)
    with contextlib.ExitStack() as es:
        LIMIT = 212800
        big = es.enter_context(nc.sbuf_tensor("big", [128, LIMIT], U8))
        PSA = es.enter_context(nc.psum_tensor("psa", [128, 4096], F32))
        A = Alloc(big, LIMIT)
        X = A.ten(8, NT, F32)
        XN = A.ten(8, NT, BF16)
        VEC = A.ten(1, ncol, F32)
        CF = A.ten(7, 128, F32)
        CB = A.ten(5, 128, BF16)
        C2 = A.ten(1, nc2, F32)
        RSTD = A.ten(1, NT, F32)
        EPSC = A.ten(1, 8, F32)
        LAMIN = A.ten(2, 128, F32)
        LAM = A.ten(2, 8, F32)
        SGR = A.ten(2, 64, F32)
        BDEC = A.ten(2, 8, F32)
        LAMC = A.ten(2, 1, F32)
        SCB = A.ten(2, 32, F32)
        IDX63 = A.ten(1, 8, I32)
        QS = A.ten(4, 4, F32)
        KS = A.ten(4, 4, F32)
        VS = A.ten(4, 4, F32)
        IDX2 = A.ten(2, 32, I32)
        NHS = 8
        HS = [A.ten(8, 128, BF16) for _ in range(NHS)]
        SG = [A.ten(1, 342, F32) for _ in range(3)]
        R0 = A.cur
        WD = A.ten(GFF, 1024, BF16)
        H = A.ten(GFF, NT, BF16)
        assert A.cur <= LIMIT
        RSIZE = LIMIT - R0
        SQH = A.at(H.off, 8, 512, BF16)

        def bk(b, n0=0, n1=512):
            return PSA[:, b * 512 + n0: b * 512 + n1]

        def PS(*bs):
            return [('ps', b) for b in bs]

        def vcol(name, j=0):
            return VEC.v[:, 0, lay[name][0] + j: lay[name][0] + j + 1]

        def c2v(name, p0, p1):
            c0, n = lay2[name]
            return C2.v[p0:p1, 0, c0:c0 + n]

        ident = CF.v[:, 0, :]
        ones_f = CF.v[:, 4, :]
        Jf = CF.v[:, 3, :]
        mean_b = CB.v[:, 1, :]
        blk_b = CB.v[:, 2, :]
        ident_b = CB.v[:, 0, :]
        ones_b = CB.v[:, 4, :]

        P.op('sp', lambda e: e.dma_start(out=X.v, in_=xT), writes=X.r(), chan='ld0')
        P.op('sp', lambda e: e.dma_start(out=VEC.v[:, 0, :], in_=vecsT), writes=VEC.r(), chan='ld')
        P.op('sp', lambda e: e.dma_start(out=CF.v, in_=consts), writes=CF.r(), chan='ld')
        P.op('sp', lambda e: e.dma_start(out=C2.v[:, 0, :], in_=cst2), writes=C2.r(), chan='ld')
        P.op('dve', lambda e: e.tensor_copy(CB.v, CF.v[:, 0:5, :]), reads=CF.r(), writes=CB.r())
        P.op('dve', lambda e: e.memset(EPSC.v[:, 0, :], EPS), writes=EPSC.r())
        epsc = EPSC.v[:, 0, 0:1]

        hs_ctr = [0]

        def load_col(w2d, col0):
            i = hs_ctr[0] % NHS
            hs_ctr[0] += 1
            src = w2d.rearrange("(kc p) f -> p kc f", p=128)[:, :, col0:col0 + 128]
            t = HS[i]
            P.op('pool', lambda e: e.dma_start(out=t.v, in_=src), writes=t.r(), chan=('hs', i))
            return t

        class WStream:
            def __init__(self, items, depth):
                self.items, self.i, self.q, self.depth = list(items), 0, [], depth

            def fill(self):
                while self.i < len(self.items) and len(self.q) < self.depth:
                    self.q.append(load_col(*self.items[self.i]))
                    self.i += 1

            def get(self):
                self.fill()
                h = self.q.pop(0)
                self.fill()
                return h

        def proj(hs, srcs, b, lo, hi):
            n = hi - lo
            for kc in range(8):
                t, row = srcs[kc]
                P.op('pe', lambda e, kc=kc, t=t, row=row: e.matmul(
                    bk(b, 0, n), hs.v[:, kc, :], t.v[:, row, lo:hi], start=(kc == 0), stop=(kc == 7)),
                    reads=hs.r(kc) + t.r(row, lo, hi), writes=PS(b))

        XNS = [(XN, kc) for kc in range(8)]

        nrm_ctr = [0]

        def rmsnorm(gname, tiles):
            SQ = SQH
            for (lo, hi) in tiles:
                n = hi - lo
                b = 6 + nrm_ctr[0] % 2
                nrm_ctr[0] += 1
                P.op('act', lambda e, lo=lo, hi=hi, n=n: e.activation(SQ.v[:, :, 0:n], X.v[:, :, lo:hi], AF.Square),
                     reads=X.r(None, lo, hi), writes=SQ.r(None, 0, n))
                for kc in range(8):
                    P.op('pe', lambda e, kc=kc, n=n, b=b: e.matmul(bk(b, 0, n), mean_b, SQ.v[:, kc, 0:n],
                                                                  start=(kc == 0), stop=(kc == 7)),
                         reads=SQ.r(kc, 0, n) + CB.r(1), writes=PS(b))
                P.op('act', lambda e, lo=lo, hi=hi, n=n, b=b: e.activation(
                    RSTD.v[:, 0, lo:hi], bk(b, 0, n), AF.Sqrt, bias=epsc),
                    reads=PS(b) + EPSC.r(), writes=RSTD.r(0, lo, hi))
                P.op('dve', lambda e, lo=lo, hi=hi: e.reciprocal(RSTD.v[:, 0, lo:hi], RSTD.v[:, 0, lo:hi]),
                     reads=RSTD.r(0, lo, hi), writes=RSTD.r(0, lo, hi))
                for c in range(8):
                    P.op('dve', lambda e, c=c, lo=lo, hi=hi: e.scalar_tensor_tensor(
                        XN.v[:, c, lo:hi], X.v[:, c, lo:hi], vcol(gname, c), RSTD.v[:, 0, lo:hi],
                        ALU.mult, ALU.mult),
                        reads=X.r(c, lo, hi) + RSTD.r(0, lo, hi) + VEC.r(), writes=XN.r(c, lo, hi))

        def ffn(which, l):
            gu_ctr = 0
            sgc = 0
            dn_ctr = 0
            pend = []
            nxt = [0]

            def prefetch(upto):
                while nxt[0] < min(upto, NFF):
                    f = nxt[0]
                    pend.append((load_col(wg[which][l], f * 128), load_col(wu[which][l], f * 128)))
                    nxt[0] += 1
            for g in range(2):
                prefetch(g * GFF + 2)
                for j in range(GFF):
                    f = g * GFF + j
                    src = wd[which][l][f * 128:(f + 1) * 128, :]
                    P.op('pool', lambda e, j=j, src=src: e.dma_start(out=WD.v[:, j, :], in_=src),
                         writes=WD.r(j), chan=('wd', j % 4))
                for j in range(GFF):
                    f = g * GFF + j
                    prefetch(f + 3)
                    hg, hu = pend.pop(0)
                    for (lo, hi) in FT:
                        n = hi - lo
                        bg = (gu_ctr % 3) * 2
                        bu = bg + 1
                        gu_ctr += 1
                        proj(hg, XNS, bg, lo, hi)
                        proj(hu, XNS, bu, lo, hi)
                        sg = SG[sgc % 3]
                        sgc += 1
                        P.op('act', lambda e, n=n, bg=bg, sg=sg: e.activation(sg.v[:, 0, 0:n], bk(bg, 0, n), AF.Silu),
                             reads=PS(bg), writes=sg.r())
                        P.op('dve', lambda e, n=n, bu=bu, sg=sg, j=j, lo=lo, hi=hi: e.tensor_tensor(
                            H.v[:, j, lo:hi], sg.v[:, 0, 0:n], bk(bu, 0, n), ALU.mult),
                            reads=sg.r() + PS(bu), writes=H.r(j, lo, hi))
                for (lo, hi) in FT:
                    n = hi - lo
                    for dch in range(8):
                        b = 6 + (dn_ctr % 2)
                        dn_ctr += 1
                        for j in range(GFF):
                            P.op('pe', lambda e, j=j, dch=dch, lo=lo, hi=hi, n=n, b=b: e.matmul(
                                bk(b, 0, n), WD.v[:, j, dch * 128:(dch + 1) * 128], H.v[:, j, lo:hi],
                                start=(j == 0), stop=(j == GFF - 1)),
                                reads=WD.r(j) + H.r(j, lo, hi), writes=PS(b))
                        P.op('dve', lambda e, dch=dch, lo=lo, hi=hi, n=n, b=b: e.scalar_tensor_tensor(
                            X.v[:, dch, lo:hi], bk(b, 0, n), 0.5, X.v[:, dch, lo:hi], ALU.mult, ALU.add),
                            reads=PS(b) + X.r(dch, lo, hi), writes=X.r(dch, lo, hi))

        def even_setup():
            B0 = Alloc(big, LIMIT)
            B0.cur = R0
            RB = B0.ten(1, 8, F32)
            DS = B0.ten(1, NTAB + 256, F32)
            TB = B0.ten(1, NTAB, F32)
            PTI = B0.ten(1, NSAMP * NPAGES, I32)
            PTF = B0.ten(1, NSAMP * NPAGES, F32)
            PRD = B0.ten(2, 32, F32)
            P63 = B0.ten(1, 8, F32)
            PT2I = B0.ten(1, 8, I32)
            PT2F = B0.ten(1, 8, F32)
            P2F = B0.ten(1, 32, F32)
            P.op('sp', lambda e: e.dma_start(out=RB.v[0:33, 0, :], in_=relb), writes=RB.r(), chan='ld')
            P.op('sp', lambda e: e.dma_start(out=DS.v[0:33, 0, :], in_=dstat), writes=DS.r(), chan='ld')
            for (c0, c1) in ((0, 512), (512, NTAB)):
                P.op('pe', lambda e, c0=c0, c1=c1: e.matmul(bk(0, 0, c1 - c0)[0:8, :], RB.v[0:33, 0, :], DS.v[0:33, 0, c0:c1],
                                                           start=True, stop=True),
                     reads=RB.r() + DS.r(), writes=PS(0))
                P.op('dve', lambda e, c0=c0, c1=c1: e.tensor_copy(TB.v[0:8, 0, c0:c1], bk(0, 0, c1 - c0)[0:8, :]),
                     reads=PS(0), writes=TB.r(0, c0, c1))
            P.op('sp', lambda e: e.dma_start(out=tabd.ap(), in_=TB.v[0:8, 0, :]), reads=TB.r(), writes=['tabd'], chan='ld')
            for k in range(2):
                P.op('pe', lambda e, k=k: e.matmul(bk(1, 0, 8), DS.v[0:33, 0, NTAB + 128 * k: NTAB + 128 * (k + 1)],
                                                  RB.v[0:33, 0, :], start=True, stop=True),
                     reads=RB.r() + DS.r(), writes=PS(1))
                P.op('dve', lambda e, k=k: e.tensor_copy(BDEC.v[:, k, :], bk(1, 0, 8)), reads=PS(1), writes=BDEC.r(k))
            P.op('sp', lambda e: e.dma_start(
                out=LAMIN.v, in_=bass.AP(lamv.tensor, 0, [[0, 128], [128, 2], [1, 128]])), writes=LAMIN.r(), chan='ld')
            P.op('sp', lambda e: e.dma_start(
                out=SGR.v, in_=bass.AP(sgrow.tensor, 0, [[0, 128], [64, 2], [1, 64]])), writes=SGR.r(), chan='ld')
            P.op('sp', lambda e: e.dma_start(out=SCB.v, in_=scb.rearrange("p e c k s -> p e (c k s)")),
                 writes=SCB.r(), chan='ld')
            P.op('sp', lambda e: e.dma_start(
                out=PTI.v[:, 0, :], in_=bass.AP(ptab.tensor, 0, [[0, 128], [1, NSAMP * NPAGES]])),
                writes=PTI.r(), chan='ld')
            P.op('dve', lambda e: e.tensor_copy(PTF.v, PTI.v), reads=PTI.r(), writes=PTF.r())
            P.op('dve', lambda e: e.tensor_scalar(
                P63.v[:, 0, 0:4], PTF.v[:, 0, NPAGES - 1:NSAMP * NPAGES:NPAGES], 128.0, c2v('iota', 0, 128), ALU.mult, ALU.add),
                reads=PTF.r() + C2.r(), writes=P63.r())
            P.op('dve', lambda e: e.tensor_copy(IDX63.v[:, 0, 0:4], P63.v[:, 0, 0:4]), reads=P63.r(), writes=IDX63.r())
            for a in range(2):
                P.op('sp', lambda e, a=a: e.dma_start(out=PT2I.v[:, 0, a:a + 1], in_=bass.AP(ptab.tensor, a * 128, [[1, 128], [1, 1]])),
                     writes=PT2I.r(), chan='ld')
            P.op('dve', lambda e: e.tensor_copy(PT2F.v[:, 0, 0:2], PT2I.v[:, 0, 0:2]), reads=PT2I.r(), writes=PT2F.r())
            for a in range(2):
                P.op('dve', lambda e, a=a: e.scalar_tensor_tensor(
                    P2F.v[:, 0, :], c2v('iota32', 0, 128), 1.0 / NSTEP, PT2F.v[:, 0, a:a + 1].to_broadcast([128, 32]), ALU.mult, ALU.add),
                    reads=PT2F.r() + C2.r(), writes=P2F.r())
                P.op('dve', lambda e: e.tensor_scalar(P2F.v[:, 0, :], P2F.v[:, 0, :], float(NSTEP), None, ALU.mult),
                     reads=P2F.r(), writes=P2F.r())
                P.op('dve', lambda e, a=a: e.tensor_copy(IDX2.v[:, a, :], P2F.v[:, 0, :]), reads=P2F.r(), writes=IDX2.r(a))
            for e_ in range(n_even):
                lam_init = 0.8 - 0.6 * math.exp(-0.3 * (2 * e_))
                for k in range(2):
                    P.op('dve', lambda e, e_=e_, k=k: e.tensor_tensor(
                        PRD.v[:, k, :], LAMIN.v[:, e_, 64 * k:64 * k + 32], LAMIN.v[:, e_, 64 * k + 32:64 * k + 64],
                        ALU.mult), reads=LAMIN.r(), writes=PRD.r(k))
                P.op('dve', lambda e, e_=e_: e.tensor_reduce(LAM.v[:, e_, 2:4], PRD.v, AX.X, ALU.add),
                     reads=PRD.r(), writes=LAM.r(e_))
                P.op('act', lambda e, e_=e_: e.activation(LAM.v[:, e_, 2:4], LAM.v[:, e_, 2:4], AF.Exp),
                     reads=LAM.r(e_), writes=LAM.r(e_))
                P.op('dve', lambda e, e_=e_: e.tensor_tensor(LAM.v[:, e_, 0:1], LAM.v[:, e_, 2:3], LAM.v[:, e_, 3:4],
                                                            ALU.subtract), reads=LAM.r(e_), writes=LAM.r(e_))
                P.op('dve', lambda e, e_=e_, li=lam_init: e.tensor_scalar(
                    LAM.v[:, e_, 0:1], LAM.v[:, e_, 0:1], li, None, ALU.add), reads=LAM.r(e_), writes=LAM.r(e_))
                P.op('dve', lambda e, e_=e_: e.tensor_scalar(LAM.v[:, e_, 1:2], LAM.v[:, e_, 0:1], -1.0, None, ALU.mult),
                     reads=LAM.r(e_), writes=LAM.r(e_))
                P.op('dve', lambda e, e_=e_, li=lam_init: e.tensor_scalar(SGR.v[:, e_, :], SGR.v[:, e_, :], 1.0 - li, None, ALU.mult),
                     reads=SGR.r(e_), writes=SGR.r(e_))
                P.op('dve', lambda e, e_=e_: e.scalar_tensor_tensor(
                    LAMC.v[0:48, e_, :], c2v('sgn1', 0, 48), LAM.v[0:48, e_, 0:1], c2v('sgn0', 0, 48), ALU.mult, ALU.add),
                    reads=LAM.r(e_) + C2.r(), writes=LAMC.r(e_))

        def even_mixer(l):
            e_ = l // 2
            lam_init = 0.8 - 0.6 * math.exp(-0.3 * l)
            B = Alloc(big, LIMIT)
            B.cur = R0
            ATT = B.ten(4, NT, BF16)
            CATB = B.ten(4, NT, BF16)
            S0 = B.cur
            QN = B.ten(1, NT, BF16)
            KN = B.ten(1, NT, BF16)
            VA = B.ten(16, 130, BF16)
            ECB = B.ten(2, 512, BF16)
            ZS = [B.ten(1, 512, F32) for _ in range(2)]
            SQB = [B.ten(1, 512, BF16) for _ in range(2)]
            RQ = [B.ten(1, 512, F32) for _ in range(2)]
            KF = [B.ten(1, 512, F32) for _ in range(2)]
            PT = [B.ten(2, 256, BF16) for _ in range(3)]
            E2 = B.at(KF[0].off, 2, 512, F32)
            assert KF[1].off == KF[0].off + 2048
            B2 = Alloc(big, SG[2].off + 1376)
            B2.cur = SG[0].off
            ATOK = [B2.ten(1, 256, F32) for _ in range(2)]
            OS = B2.ten(4, 130, F32)
            A1 = B.ten(2, 128, F32)
            RR = B.ten(1, 16, F32)
            S1 = B.cur
            w2d = w_in[e_]
            qg, kg = vcol(('qg', e_)), vcol(('kg', e_))
            nlam = LAM.v[:, e_, 1:2]

            items = []
            for c in range(4):
                items += [(w2d, c * 128), (w2d, (4 + c) * 128), (w2d, (8 + c) * 128)]
            for i in range(4):
                items += [(w2d, (16 + i) * 128), (w2d, (20 + i) * 128), (w2d, (12 + i) * 128)]
            for dch in range(8):
                items += [(w_out[e_], dch * 128)]
            ws = WStream(items, 4)
            ws.fill()
            rmsnorm(('nm', l), MT)
            P.op('pool', lambda e: e.memset(VA.v, 1.0), writes=VA.r())

            zc = [0]
            tc_ = [0]
            for c in range(4):
                hq = ws.get()
                hk = ws.get()
                hv = ws.get()
                srcE = bass.AP(tabd, 2 * c * NTAB, [[1, 128], [NTAB, 2], [1, 512]])
                P.op('sp', lambda e, srcE=srcE: e.dma_start(out=E2.v, in_=srcE), reads=['tabd'], writes=E2.r(), chan='e2')
                for hh in range(2):
                    P.op('pe', lambda e, hh=hh: e.matmul(bk(7), Jf, E2.v[:, hh, :], start=True, stop=True),
                         reads=E2.r(hh) + CF.r(3), writes=PS(7))
                    P.op('act', lambda e, hh=hh: e.activation(ECB.v[:, hh, :], bk(7), AF.Exp), reads=PS(7), writes=ECB.r(hh))
                tl = [(kind, hs, ti, lo, hi) for kind, hs in (('q', hq), ('k', hk), ('v', hv)) for ti, (lo, hi) in enumerate(MT)]
                tinfo = {}

                def st1(t):
                    kind, hs, ti, lo, hi = tl[t]
                    n = hi - lo
                    b = zc[0] % 4
                    zi = zc[0] % 2
                    zc[0] += 1
                    tinfo[t] = (b, zi)
                    proj(hs, XNS, b, lo, hi)
                    if kind in 'qk':
                        zs, sqb = ZS[zi], SQB[zi]
                        P.op('act', lambda e, n=n, b=b, zs=zs: e.activation(zs.v[:, 0, 0:n], bk(b, 0, n), AF.Copy),
                             reads=PS(b), writes=zs.r(0, 0, n))
                        P.op('act', lambda e, n=n, b=b, sqb=sqb: e.activation(sqb.v[:, 0, 0:n], bk(b, 0, n), AF.Square),
                             reads=PS(b), writes=sqb.r(0, 0, n))
                    else:
                        vf = KF[zi]
                        P.op('act', lambda e, n=n, b=b, vf=vf: e.activation(vf.v[:, 0, 0:n], bk(b, 0, n), AF.Copy),
                             reads=PS(b), writes=vf.r(0, 0, n))
                        P.op('sp', lambda e, n=n, lo=lo, hi=hi, vf=vf, c=c: e.dma_start(
                            out=nvT[e_, :, c, lo:hi], in_=vf.v[:, 0, 0:n]), reads=vf.r(0, 0, n), chan=('ok', zi))

                def st2(t):
                    kind, hs, ti, lo, hi = tl[t]
                    n = hi - lo
                    b, zi = tinfo[t]
                    if kind in 'qk':
                        zs, sqb, rq = ZS[zi], SQB[zi], RQ[zi]
                        b2 = 4 + zi
                        P.op('pe', lambda e, n=n, b2=b2, sqb=sqb: e.matmul(bk(b2, 0, n), blk_b, sqb.v[:, 0, 0:n],
                                                                          start=True, stop=True),
                             reads=sqb.r(0, 0, n) + CB.r(2), writes=PS(b2))
                        P.op('act', lambda e, n=n, b2=b2, rq=rq: e.activation(rq.v[:, 0, 0:n], bk(b2, 0, n), AF.Ln, bias=epsc),
                             reads=PS(b2) + EPSC.r(), writes=rq.r(0, 0, n))
                        P.op('act', lambda e, n=n, rq=rq: e.activation(rq.v[:, 0, 0:n], rq.v[:, 0, 0:n], AF.Exp, scale=-0.5),
                             reads=rq.r(0, 0, n), writes=rq.r(0, 0, n))
                        if kind == 'q':
                            P.op('dve', lambda e, n=n, lo=lo, hi=hi, zs=zs, rq=rq: e.scalar_tensor_tensor(
                                QN.v[:, 0, lo:hi], zs.v[:, 0, 0:n], qg, rq.v[:, 0, 0:n], ALU.mult, ALU.mult),
                                reads=zs.r(0, 0, n) + rq.r(0, 0, n) + VEC.r(), writes=QN.r(0, lo, hi))
                            if ti == 4:
                                P.op('dve', lambda e, n=n, zs=zs, rq=rq, c=c: e.scalar_tensor_tensor(
                                    QS.v[:, c, :], zs.v[:, 0, 0:n], qg, rq.v[:, 0, 0:n], ALU.mult, ALU.mult),
                                    reads=zs.r(0, 0, n) + rq.r(0, 0, n) + VEC.r(), writes=QS.r(c))
                        else:
                            kf = KF[zi]
                            P.op('dve', lambda e, n=n, zs=zs, rq=rq, kf=kf: e.scalar_tensor_tensor(
                                kf.v[:, 0, 0:n], zs.v[:, 0, 0:n], kg, rq.v[:, 0, 0:n], ALU.mult, ALU.mult),
                                reads=zs.r(0, 0, n) + rq.r(0, 0, n) + VEC.r(), writes=kf.r(0, 0, n))
                            P.op('pool', lambda e, n=n, lo=lo, hi=hi, kf=kf: e.tensor_copy(KN.v[:, 0, lo:hi], kf.v[:, 0, 0:n]),
                                 reads=kf.r(0, 0, n), writes=KN.r(0, lo, hi))
                            P.op('sp', lambda e, n=n, lo=lo, hi=hi, kf=kf, c=c: e.dma_start(
                                out=nkT[e_, :, c, lo:hi], in_=kf.v[:, 0, 0:n]), reads=kf.r(0, 0, n), chan=('ok', zi))
                            if ti == 4:
                                P.op('pool', lambda e, n=n, kf=kf, c=c: e.tensor_copy(KS.v[:, c, :], kf.v[:, 0, 0:n]),
                                     reads=kf.r(0, 0, n), writes=KS.r(c))
                    else:
                        vf = KF[zi]
                        if ti == 4:
                            P.op('pool', lambda e, n=n, vf=vf, c=c: e.tensor_copy(VS.v[:, c, :], vf.v[:, 0, 0:n]),
                                 reads=vf.r(0, 0, n), writes=VS.r(c))
                        else:
                            for bi in range(4):
                                kb = ti * 4 + bi
                                tb = 6 + tc_[0] % 2
                                tc_[0] += 1
                                P.op('pe', lambda e, bi=bi, tb=tb, vf=vf: e.transpose(
                                    bk(tb, 0, 128), vf.v[:, 0, bi * 128:(bi + 1) * 128], ident),
                                    reads=vf.r(0, bi * 128, (bi + 1) * 128) + CF.r(0), writes=PS(tb))
                                P.op('dve', lambda e, kb=kb, tb=tb: e.tensor_copy(
                                    VA.v[:, kb, :].rearrange("p (h x) -> p h x", x=65)[:, :, 0:64],
                                    bk(tb, 0, 128).rearrange("p (h x) -> p h x", x=64)),
                                    reads=PS(tb), writes=VA.r(kb))

                st1(0)
                for t in range(len(tl)):
                    if t + 1 < len(tl):
                        st1(t + 1)
                    st2(t)
                steps = [(qb, hh, kb) for qb in range(8) for hh in range(2) for kb in range(2 * qb + 2)]
                nst = len(steps)
                first_flag = {}

                def emit_S(i):
                    qb, hh, kb = steps[i]
                    q0, k0 = qb * 256, kb * 128
                    d = q0 - k0
                    col_lo = max(0, -d)
                    ncol = 256 - col_lo
                    sb = 2 * (i % 3)
                    pt = PT[i % 3]
                    for sub in range(2):
                        j = 2 * hh + sub
                        P.op('pe', lambda e, j=j, sb=sb, sub=sub, k0=k0, q0=q0, col_lo=col_lo, ncol=ncol: e.matmul(
                            bk(sb + sub, 0, ncol), KN.v[32 * j:32 * j + 32, 0, k0:k0 + 128],
                            QN.v[32 * j:32 * j + 32, 0, q0 + col_lo:q0 + 256], start=True, stop=True,
                            tile_position=(32 * j, 0)),
                            reads=KN.r(0, k0, k0 + 128) + QN.r(0, q0 + col_lo, q0 + 256), writes=PS(sb + sub))
                    pspair = PSA[:, sb * 512:(sb + 2) * 512].rearrange("p (s n) -> p s n", n=512)[:, :, 0:ncol]
                    P.op('act', lambda e, pspair=pspair, pt=pt, ncol=ncol: e.activation(
                        pt.v[:, :, 0:ncol], pspair, AF.Exp, scale=SCALE),
                        reads=PS(sb, sb + 1), writes=pt.r(None, 0, ncol))
                    if d < 256:
                        w0 = col_lo + d + 128
                        P.op('dve', lambda e, pt=pt, hh=hh, w0=w0, ncol=ncol: e.tensor_tensor(
                            pt.v[:, :, 0:ncol], pt.v[:, :, 0:ncol],
                            ECB.v[:, hh:hh + 1, w0:w0 + ncol].to_broadcast([128, 2, ncol]), ALU.mult),
                            reads=pt.r(None, 0, ncol) + ECB.r(hh), writes=pt.r(None, 0, ncol))

                def emit_PV(i):
                    qb, hh, kb = steps[i]
                    q0, k0 = qb * 256, kb * 128
                    col_lo = max(0, k0 - q0)
                    pt = PT[i % 3]
                    ob = 6 + hh
                    for sub in range(2):
                        for qs in range(2):
                            if 128 * qs < col_lo:
                                continue
                            c0 = 128 * qs - col_lo
                            last_kb = 2 * qb + qs
                            st = not first_flag.get((qb, hh), False)
                            first_flag[(qb, hh)] = True
                            gi_ = sub * 2 + qs
                            P.op('pe', lambda e, sub=sub, gi_=gi_, c0=c0, pt=pt, kb=kb, hh=hh, ob=ob, st=st, last_kb=last_kb: e.matmul(
                                bk(ob, gi_ * 65, (gi_ + 1) * 65), pt.v[:, sub, c0:c0 + 128],
                                VA.v[:, kb, hh * 65:(hh + 1) * 65], start=st, stop=(kb == last_kb),
                                skip_group_check=True),
                                reads=pt.r(sub, c0, c0 + 128) + VA.r(kb), writes=PS(ob))

                def norm_A(qb):
                    osv = OS.v[:, :, :].rearrange("p a (q x) -> p a q x", x=65)
                    P.op('dve', lambda e: e.tensor_copy(
                        OS.v.rearrange("p (h s) x -> p h (s x)", s=2),
                        PSA[:, 6 * 512:8 * 512].rearrange("p (a n) -> p a n", n=512)[:, :, 0:260]),
                        reads=PS(6, 7), writes=OS.r())
                    P.op('dve', lambda e: e.reciprocal(RR.v[:, 0, 0:8].rearrange("p (a q) -> p a q", q=2), osv[:, :, :, 64]),
                         reads=OS.r(), writes=RR.r())
                    P.op('dve', lambda e: e.tensor_tensor(
                        osv[:, :, :, 0:64], osv[:, :, :, 0:64],
                        RR.v[:, 0, 0:8].rearrange("p (a q) -> p a q", q=2).unsqueeze(3).to_broadcast([128, 4, 2, 64]), ALU.mult),
                        reads=OS.r() + RR.r(), writes=OS.r())
                    os5 = OS.v[:, :, :].rearrange("p (h s) (q x) -> p h s q x", s=2, x=65)
                    a1v = A1.v[:, :, :].rearrange("p h (q x) -> p h q x", x=64)
                    for h2 in range(2):
                        P.op('dve', lambda e, h2=h2: e.scalar_tensor_tensor(
                            a1v[:, h2], os5[:, h2, 1, :, 0:64], nlam, os5[:, h2, 0, :, 0:64], ALU.mult, ALU.add),
                            reads=OS.r() + LAM.r(e_), writes=A1.r(h2))
                    sqv = OS.v[:, 0:2, 0:128].rearrange("p h (q x) -> p h q x", x=64)
                    P.op('dve', lambda e: e.tensor_tensor(sqv, a1v, a1v, ALU.mult), reads=A1.r(), writes=OS.r())
                    P.op('dve', lambda e: e.tensor_reduce(RR.v[:, 0, 8:12].rearrange("p (h q) -> p h q", q=2), sqv, AX.X, ALU.add),
                         reads=OS.r(), writes=RR.r())

                def norm_B(qb):
                    atok = ATOK[qb % 2]
                    a1v = A1.v[:, :, :].rearrange("p h (q x) -> p h q x", x=64)
                    P.op('act', lambda e: e.activation(RR.v[:, 0, 8:12], RR.v[:, 0, 8:12], AF.Ln, bias=epsc, scale=1.0 / 64),
                         reads=RR.r() + EPSC.r(), writes=RR.r())
                    P.op('act', lambda e: e.activation(RR.v[:, 0, 8:12], RR.v[:, 0, 8:12], AF.Exp, scale=-0.5),
                         reads=RR.r(), writes=RR.r())
                    P.op('dve', lambda e: e.tensor_tensor(
                        a1v, a1v, RR.v[:, 0, 8:12].rearrange("p (h q) -> p h q", q=2).unsqueeze(3).to_broadcast([128, 2, 2, 64]),
                        ALU.mult), reads=A1.r() + RR.r(), writes=A1.r())
                    av = atok.v[:, 0, :].rearrange("p (q h x) -> p h q x", q=2, h=2)
                    P.op('dve', lambda e, av=av: e.tensor_tensor(
                        av, a1v, SGR.v[:, e_:e_ + 1, :].unsqueeze(1).to_broadcast([128, 2, 2, 64]), ALU.mult),
                        reads=A1.r() + SGR.r(e_), writes=atok.r())

                def trans(qb):
                    atok = ATOK[qb % 2]
                    q0 = qb * 256
                    for qs in range(2):
                        tb = 2 * (qb % 3) + qs
                        P.op('pe', lambda e, qs=qs, tb=tb, atok=atok: e.transpose(
                            bk(tb, 0, 128), atok.v[:, 0, qs * 128:(qs + 1) * 128], ident),
                            reads=atok.r() + CF.r(0), writes=PS(tb))
                        P.op('act', lambda e, qs=qs, tb=tb, c=c, q0=q0: e.activation(
                            ATT.v[:, c, q0 + 128 * qs:q0 + 128 * qs + 128], bk(tb, 0, 128), AF.Copy),
                            reads=PS(tb), writes=ATT.r(c, q0 + 128 * qs, q0 + 128 * qs + 128))

                deferred = []
                emit_S(0)
                emit_S(1)
                for i in range(nst):
                    if i + 2 < nst:
                        emit_S(i + 2)
                    emit_PV(i)
                    qb, hh, kb = steps[i]
                    if hh == 1 and kb == 2 * qb + 1:
                        norm_A(qb)
                        deferred.append((i + 2, norm_B, qb))
                        deferred.append((i + 5, trans, qb))
                    for item in list(deferred):
                        if item[0] <= i:
                            item[1](item[2])
                            deferred.remove(item)
                for item in sorted(deferred, key=lambda t: t[0]):
                    item[1](item[2])

            Dd = Alloc(big, LIMIT)
            Dd.cur = S0
            KP2 = [Dd.ten(TS, 512, BF16) for _ in range(3)]
            VP2 = [Dd.ten(TS, 512, BF16) for _ in range(3)]
            PRD = Dd.ten(TS, 512, BF16)
            SS2 = Dd.ten(TS, 16, F32)
            PD2 = [Dd.ten(TS, 48, BF16) for _ in range(3)]
            KP1 = Dd.at(KP2[0].off, 1, 512, BF16)
            VP1 = Dd.at(VP2[0].off, 1, 512, BF16)
            SS1 = Dd.ten(2, 16, F32)
            PD1 = Dd.ten(2, 16, BF16)
            DG = Dd.ten(4, 128, F32)
            DGB = Dd.ten(4, 128, F32)
            QB2 = Dd.ten(1, 512, BF16)
            QB = Dd.ten(1, 512, BF16)
            KSF = Dd.ten(1, 512, BF16)
            VSF = Dd.ten(1, 512, BF16)
            OM = Dd.at(DGB.off, 1, 512, F32)
            R1 = Dd.ten(1, 64, F32)
            AS = Dd.ten(1, 64, F32)
            AS2 = Dd.ten(1, 128, F32)
            RD = Dd.ten(1, 8, F32)
            assert Dd.cur <= LIMIT, (Dd.cur, LIMIT)
            ck1, cv1 = cache_k[e_], cache_v[e_]
            ck2 = cache_k[e_].rearrange("(r t) f -> r (t f)", t=TS)
            cv2 = cache_v[e_].rearrange("(r t) f -> r (t f)", t=TS)
            negm = c2v('negm', 0, 128)
            LA, LB = CF.v[:, 5, :], CF.v[:, 6, :]
            gctr = [0]

            def diag(dst, src, s):
                P.op('dve', lambda e: e.tensor_tensor(
                    dst.v, CF.v[:, 0:1, :].to_broadcast([128, 4, 128]),
                    src.v[:, :, s:s + 1].to_broadcast([128, 4, 128]), ALU.mult),
                    reads=CF.r(0) + src.r(), writes=dst.r())

            for pd_ in PD2:
                P.op('pool', lambda e, pd_=pd_: e.memset(pd_.v, 0.0), writes=pd_.r())
            for pair in range(2):
                sA, sB = 2 * pair, 2 * pair + 1
                diag(DG, QS, sA)
                diag(DGB, QS, sB)
                P.op('pe', lambda e: e.matmul(bk(0), LA, DG.v.rearrange("p a b -> p (a b)"), start=True, stop=False),
                     reads=DG.r() + CF.r(5), writes=PS(0))
                P.op('pe', lambda e: e.matmul(bk(0), LB, DGB.v.rearrange("p a b -> p (a b)"), start=False, stop=True),
                     reads=DGB.r() + CF.r(6), writes=PS(0))
                P.op('act', lambda e: e.activation(QB2.v[:, 0, :], bk(0), AF.Copy), reads=PS(0), writes=QB2.r())
                for g in range(NSTEP):
                    st_ = gctr[0] % 3
                    gctr[0] += 1
                    kp, vp, pd = KP2[st_], VP2[st_], PD2[st_]
                    P.op('pool', lambda e, g=g, kp=kp, pair=pair: e.indirect_dma_start(
                        out=kp.v.rearrange("p a b -> p (a b)"), out_offset=None, in_=ck2,
                        in_offset=bass.IndirectOffsetOnAxis(ap=IDX2.v[:, pair, g:g + 1], axis=0)),
                        reads=IDX2.r(pair), writes=kp.r(), chan=('kp', st_))
                    P.op('pool', lambda e, g=g, vp=vp, pair=pair: e.indirect_dma_start(
                        out=vp.v.rearrange("p a b -> p (a b)"), out_offset=None, in_=cv2,
                        in_offset=bass.IndirectOffsetOnAxis(ap=IDX2.v[:, pair, g:g + 1], axis=0)),
                        reads=IDX2.r(pair), writes=vp.r(), chan=('vp', st_))
                    P.op('dve', lambda e, kp=kp: e.tensor_tensor(
                        PRD.v, kp.v, QB2.v.to_broadcast([128, TS, 512]), ALU.mult),
                        reads=kp.r() + QB2.r(), writes=PRD.r())
                    P.op('dve', lambda e: e.tensor_reduce(
                        SS2.v.rearrange("p a b -> p (a b)"), PRD.v.rearrange("p a (h x) -> p (a h) x", x=32), AX.X, ALU.add),
                        reads=PRD.r(), writes=SS2.r())
                    for half in range(2):
                        p0, p1 = 64 * half, 64 * half + 64
                        P.op('act', lambda e, pd=pd, p0=p0, p1=p1, half=half: e.activation(
                            pd.v[p0:p1, :, 32 * half:32 * half + 16], SS2.v[p0:p1, :, :], AF.Exp,
                            bias=c2v('negm', p0, p1), scale=SCALE),
                            reads=SS2.r() + C2.r(), writes=pd.r())
                    for t in range(TS):
                        first = (g == 0 and t == 0)
                        P.op('pe', lambda e, t=t, pd=pd, vp=vp, first=first: e.matmul(
                            bk(3)[0:48, :], pd.v[:, t, :], vp.v[:, t, :], start=first, stop=False),
                            reads=pd.r(t) + vp.r(t), writes=PS(3))
                        P.op('pe', lambda e, t=t, pd=pd, first=first: e.matmul(
                            bk(5, 0, 1)[0:48, :], pd.v[:, t, :], ones_b[:, 0:1], start=first, stop=False),
                            reads=pd.r(t) + CB.r(4), writes=PS(5))
                for half, s in enumerate((sA, sB)):
                    ob_, db_ = 3, 5
                    r0, r1 = 32 * half, 32 * half + 16
                    for (src, dst, b) in ((QS, QB, 0), (KS, KSF, 1), (VS, VSF, 2)):
                        diag(DG, src, s)
                        P.op('pe', lambda e, b=b: e.matmul(bk(b), ones_f, DG.v.rearrange("p a b -> p (a b)"), start=True, stop=True),
                             reads=DG.r() + CF.r(4), writes=PS(b))
                        P.op('act', lambda e, dst=dst, b=b: e.activation(dst.v[:, 0, :], bk(b), AF.Copy), reads=PS(b), writes=dst.r())
                    P.op('pool', lambda e, s=s: e.indirect_dma_start(
                        out=KP1.v[:, 0, :], out_offset=None, in_=ck1,
                        in_offset=bass.IndirectOffsetOnAxis(ap=IDX63.v[:, 0, s:s + 1], axis=0)),
                        reads=IDX63.r(), writes=KP1.r(), chan='kp1')
                    P.op('pool', lambda e, s=s: e.indirect_dma_start(
                        out=VP1.v[:, 0, :], out_offset=None, in_=cv1,
                        in_offset=bass.IndirectOffsetOnAxis(ap=IDX63.v[:, 0, s:s + 1], axis=0)),
                        reads=IDX63.r(), writes=VP1.r(), chan='vp1')
                    for k, ksrc in enumerate((KP1, KSF)):
                        P.op('dve', lambda e, ksrc=ksrc: e.tensor_tensor(PRD.v[:, 0, :], ksrc.v[:, 0, :], QB.v[:, 0, :], ALU.mult),
                             reads=ksrc.r() + QB.r(), writes=PRD.r(0))
                        P.op('dve', lambda e, k=k: e.tensor_reduce(
                            SS1.v[:, k, :], PRD.v[:, 0, :].rearrange("p (h x) -> p h x", x=32), AX.X, ALU.add),
                            reads=PRD.r(0), writes=SS1.r(k))
                    P.op('dve', lambda e: e.scalar_tensor_tensor(
                        SS1.v.rearrange("p k (h x) -> p k h x", x=2)[:, 0], SS1.v.rearrange("p k (h x) -> p k h x", x=2)[:, 0], SCALE,
                        BDEC.v[:, 0, :].unsqueeze(2).to_broadcast([128, 8, 2]), ALU.mult, ALU.add),
                        reads=SS1.r(0) + BDEC.r(0), writes=SS1.r(0))
                    P.op('dve', lambda e: e.scalar_tensor_tensor(
                        SS1.v.rearrange("p k (h x) -> p k h x", x=2)[:, 1], SS1.v.rearrange("p k (h x) -> p k h x", x=2)[:, 1], SCALE,
                        BDEC.v[:, 1, :].unsqueeze(2).to_broadcast([128, 8, 2]), ALU.mult, ALU.add),
                        reads=SS1.r(1) + BDEC.r(1), writes=SS1.r(1))
                    P.op('act', lambda e: e.activation(PD1.v, SS1.v, AF.Exp), reads=SS1.r(), writes=PD1.r())
                    for k, vsrc in enumerate((VP1, VSF)):
                        P.op('pe', lambda e, k=k, vsrc=vsrc, ob_=ob_, r0=r0, r1=r1: e.matmul(
                            bk(ob_)[r0:r1, :], PD1.v[:, k, :], vsrc.v[:, 0, :], start=False, stop=(k == 1),
                            tile_position=(0, r0), skip_group_check=True),
                            reads=PD1.r(k) + vsrc.r(), writes=PS(ob_))
                        P.op('pe', lambda e, k=k, db_=db_, r0=r0, r1=r1: e.matmul(
                            bk(db_, 0, 1)[r0:r1, :], PD1.v[:, k, :], ones_b[:, 0:1], start=False, stop=(k == 1),
                            tile_position=(0, r0), skip_group_check=True),
                            reads=PD1.r(k) + CB.r(4), writes=PS(db_))
                    P.op('dve', lambda e, db_=db_, r0=r0, r1=r1: e.reciprocal(RD.v[r0:r1, 0, 0:1], bk(db_, 0, 1)[r0:r1, :]), reads=PS(db_), writes=RD.r())
                    P.op('dve', lambda e, r0=r0, r1=r1: e.tensor_tensor(RD.v[r0:r1, 0, 0:1], RD.v[r0:r1, 0, 0:1], LAMC.v[r0:r1, e_, :], ALU.mult),
                         reads=RD.r() + LAMC.r(e_), writes=RD.r())
                    P.op('dve', lambda e, ob_=ob_, r0=r0, r1=r1: e.tensor_tensor(OM.v[r0:r1, 0, :], bk(ob_)[r0:r1, :], c2v('maskd', r0, r1), ALU.mult),
                         reads=PS(ob_) + C2.r(), writes=OM.r())
                    P.op('dve', lambda e, r0=r0, r1=r1: e.tensor_reduce(
                        R1.v[r0:r1, 0, :], OM.v[r0:r1, 0, :].rearrange("p (h x) -> p x h", x=64), AX.X, ALU.add),
                        reads=OM.r(), writes=R1.r())
                    P.op('dve', lambda e, r0=r0, r1=r1: e.tensor_scalar(R1.v[r0:r1, 0, :], R1.v[r0:r1, 0, :], RD.v[r0:r1, 0, 0:1], None, ALU.mult),
                         reads=R1.r() + RD.r(), writes=R1.r())
                    P.op('pe', lambda e, r0=r0, r1=r1: e.matmul(bk(0, 0, 64)[0:8, :], c2v('pair', r0, r1), R1.v[r0:r1, 0, :], start=True, stop=True),
                         reads=R1.r() + C2.r(), writes=PS(0))
                    P.op('dve', lambda e: e.tensor_copy(AS.v[0:8, 0, :], bk(0, 0, 64)[0:8, :]), reads=PS(0), writes=AS.r())
                    P.op('dve', lambda e: e.tensor_tensor(R1.v[0:8, 0, :], AS.v[0:8, 0, :], AS.v[0:8, 0, :], ALU.mult),
                         reads=AS.r(), writes=R1.r())
                    P.op('dve', lambda e: e.tensor_reduce(RD.v[0:8, 0, 1:2], R1.v[0:8, 0, :], AX.X, ALU.add),
                         reads=R1.r(), writes=RD.r())
                    P.op('act', lambda e: e.activation(RD.v[0:8, 0, 1:2], RD.v[0:8, 0, 1:2], AF.Ln, bias=EPSC.v[0:8, 0, 0:1], scale=1.0 / 64),
                         reads=RD.r() + EPSC.r(), writes=RD.r())
                    P.op('act', lambda e: e.activation(RD.v[0:8, 0, 1:2], RD.v[0:8, 0, 1:2], AF.Exp, scale=-0.5),
                         reads=RD.r(), writes=RD.r())
                    P.op('dve', lambda e: e.tensor_scalar(AS.v[0:8, 0, :], AS.v[0:8, 0, :], RD.v[0:8, 0, 1:2], None,
                                                          ALU.mult), reads=AS.r() + RD.r(), writes=AS.r())
                    P.op('dve', lambda e: e.tensor_tensor(AS.v[0:8, 0, :], AS.v[0:8, 0, :], SGR.v[0:8, e_, :], ALU.mult),
                         reads=AS.r() + SGR.r(e_), writes=AS.r())
                    P.op('dve', lambda e: e.tensor_tensor(
                        AS2.v[0:8, 0, :].rearrange("p (r x) -> p r x", x=64), AS.v[0:8, 0:1, :].to_broadcast([8, 2, 64]),
                        c2v('maskp', 0, 8).rearrange("p (r x) -> p r x", x=64), ALU.mult),
                        reads=AS.r() + C2.r(), writes=AS2.r())
                    P.op('pe', lambda e: e.matmul(bk(1, 0, 4), AS2.v[0:8, 0, :], c2v('selc', 0, 8), start=True, stop=True),
                         reads=AS2.r() + C2.r(), writes=PS(1))
                    P.op('dve', lambda e, s=s: e.tensor_copy(ATT.v[:, :, NPROMPT + s], bk(1, 0, 4)),
                         reads=PS(1), writes=ATT.r(None, NPROMPT + s, NPROMPT + s + 1))

            Cc = Alloc(big, LIMIT)
            Cc.cur = S0
            UB = Cc.ten(1, 2 + NT, F32)
            GC = [Cc.ten(1, 512, F32) for _ in range(2)]
            CV = [Cc.ten(1, 512, F32) for _ in range(2)]
            assert Cc.cur <= LIMIT
            scbv = SCB.v[:, e_, :].rearrange("p (c k s) -> p c k s", c=4, k=2)
            cc_ = [0]
            for i in range(4):
                hgc = ws.get()
                hx = ws.get()
                hgb = ws.get()
                w0, w1, w2 = (vcol(('cbw', e_), k * 4 + i) for k in range(3))
                P.op('dve', lambda e: e.memset(UB.v[:, 0, 0:2], 0.0), writes=UB.r(0, 0, 2))
                for ti, (lo, hi) in enumerate(MT):
                    n = hi - lo
                    b1, b2, b3 = 0 + 3 * (cc_[0] % 2), 1 + 3 * (cc_[0] % 2), 2 + 3 * (cc_[0] % 2)
                    gc, cvt = GC[cc_[0] % 2], CV[cc_[0] % 2]
                    cc_[0] += 1
                    proj(hgc, XNS, b1, lo, hi)
                    proj(hx, XNS, b2, lo, hi)
                    proj(hgb, XNS, b3, lo, hi)
                    P.op('act', lambda e, n=n, b1=b1, gc=gc: e.activation(gc.v[:, 0, 0:n], bk(b1, 0, n), AF.Copy),
                         reads=PS(b1), writes=gc.r(0, 0, n))
                    P.op('dve', lambda e, n=n, b2=b2, gc=gc, lo=lo, hi=hi: e.tensor_tensor(
                        UB.v[:, 0, 2 + lo:2 + hi], gc.v[:, 0, 0:n], bk(b2, 0, n), ALU.mult),
                        reads=gc.r(0, 0, n) + PS(b2), writes=UB.r(0, 2 + lo, 2 + hi))
                    if ti < 4:
                        P.op('dve', lambda e, n=n, lo=lo, cvt=cvt, w0=w0: e.tensor_scalar(
                            cvt.v[:, 0, 0:n], UB.v[:, 0, lo:lo + n], w0, None, ALU.mult),
                            reads=UB.r(0, lo, lo + n) + VEC.r(), writes=cvt.r(0, 0, n))
                        P.op('dve', lambda e, n=n, lo=lo, cvt=cvt, w1=w1: e.scalar_tensor_tensor(
                            cvt.v[:, 0, 0:n], UB.v[:, 0, 1 + lo:1 + lo + n], w1, cvt.v[:, 0, 0:n], ALU.mult, ALU.add),
                            reads=UB.r(0, 1 + lo, 1 + lo + n) + VEC.r() + cvt.r(0, 0, n), writes=cvt.r(0, 0, n))
                        P.op('dve', lambda e, n=n, lo=lo, cvt=cvt, w2=w2: e.scalar_tensor_tensor(
                            cvt.v[:, 0, 0:n], UB.v[:, 0, 2 + lo:2 + lo + n], w2, cvt.v[:, 0, 0:n], ALU.mult, ALU.add),
                            reads=UB.r(0, 2 + lo, 2 + lo + n) + VEC.r() + cvt.r(0, 0, n), writes=cvt.r(0, 0, n))
                    else:
                        P.op('dve', lambda e, cvt=cvt, w0=w0, i=i: e.tensor_scalar(
                            cvt.v[:, 0, 0:4], scbv[:, i, 0, :], w0, None, ALU.mult),
                            reads=SCB.r(e_) + VEC.r(), writes=cvt.r(0, 0, 4))
                        P.op('dve', lambda e, cvt=cvt, w1=w1, i=i: e.scalar_tensor_tensor(
                            cvt.v[:, 0, 0:4], scbv[:, i, 1, :], w1, cvt.v[:, 0, 0:4], ALU.mult, ALU.add),
                            reads=SCB.r(e_) + VEC.r() + cvt.r(0, 0, 4), writes=cvt.r(0, 0, 4))
                        P.op('dve', lambda e, cvt=cvt, w2=w2: e.scalar_tensor_tensor(
                            cvt.v[:, 0, 0:4], UB.v[:, 0, 2 + NPROMPT:2 + NT], w2, cvt.v[:, 0, 0:4], ALU.mult, ALU.add),
                            reads=UB.r(0, 2 + NPROMPT, 2 + NT) + VEC.r() + cvt.r(0, 0, 4), writes=cvt.r(0, 0, 4))
                    P.op('dve', lambda e, n=n, b3=b3, cvt=cvt, lo=lo, hi=hi, i=i: e.tensor_tensor(
                        CATB.v[:, i, lo:hi], cvt.v[:, 0, 0:n], bk(b3, 0, n), ALU.mult),
                        reads=cvt.r(0, 0, n) + PS(b3), writes=CATB.r(i, lo, hi))
                P.op('sp', lambda e, i=i: e.dma_start(out=cbpT[e_, :, i, :], in_=UB.v[:, 0, NPROMPT:NPROMPT + 2]),
                     reads=UB.r(0, NPROMPT, NPROMPT + 2), chan='ocb')
                P.op('sp', lambda e, i=i: e.dma_start(out=cbs0T[e_, :, i, :], in_=scbv[:, i, 1, :]),
                     reads=SCB.r(e_), chan='ocb')
                P.op('sp', lambda e, i=i: e.dma_start(out=cbs1T[e_, :, i, :], in_=UB.v[:, 0, 2 + NPROMPT:2 + NT]),
                     reads=UB.r(0, 2 + NPROMPT, 2 + NT), chan='ocb')

            CATS = [(ATT, k) for k in range(4)] + [(CATB, k) for k in range(4)]
            oc2 = [0]
            for dch in range(8):
                ho = ws.get()
                for (lo, hi) in FT:
                    n = hi - lo
                    b = oc2[0] % 4
                    oc2[0] += 1
                    proj(ho, CATS, b, lo, hi)
                    P.op('dve', lambda e, dch=dch, lo=lo, hi=hi, n=n, b=b: e.tensor_tensor(
                        X.v[:, dch, lo:hi], bk(b, 0, n), X.v[:, dch, lo:hi], ALU.add),
                        reads=PS(b) + X.r(dch, lo, hi), writes=X.r(dch, lo, hi))

        def odd_mixer(l):
            o_ = l // 2
            B = Alloc(big, LIMIT)
            B.cur = R0
            CC = B.ten(8, NT, BF16)
            UO = B.ten(1, 30 + NT, BF16)
            UF = B.ten(1, 34, F32)
            DGM = B.ten(31, 128, BF16)
            SCS = B.ten(8, 120, F32)
            PRS = B.ten(4, 30, F32)
            CS4 = B.ten(1, 8, F32)
            MU = [B.ten(1, 342, F32) for _ in range(2)]
            T1 = [B.ten(1, 342, F32) for _ in range(2)]
            SQ = B.ten(8, 342, BF16)
            items = []
            for i in range(8):
                items += [(w_pw1[o_], i * 128), (w_pw1[o_], (8 + i) * 128)]
            for dch in range(8):
                items += [(w_pw2[o_], dch * 128)]
            ws = WStream(items, 4)
            ws.fill()
            rmsnorm(('nm', l), FT)
            P.op('sp', lambda e: e.dma_start(out=SCS.v, in_=scc[:, o_].rearrange("p c s k -> p c (s k)")),
                 writes=SCS.r(), chan='ld')
            P.op('pool', lambda e: e.memset(UO.v[:, 0, 0:30], 0.0), writes=UO.r(0, 0, 30))
            ccw0 = lay[('ccw', o_)][0]
            pc = [0]
            for i in range(8):
                ha = ws.get()
                hb = ws.get()
                ba_, bb_ = vcol(('bpw1', o_), i), vcol(('bpw1', o_), 8 + i)
                wv = VEC.v[:, 0, ccw0 + i: ccw0 + 248: 8]
                P.op('dve', lambda e, wv=wv: e.tensor_tensor(
                    DGM.v, CB.v[:, 0:1, :].to_broadcast([128, 31, 128]), wv.unsqueeze(2).to_broadcast([128, 31, 128]), ALU.mult),
                    reads=CB.r(0) + VEC.r(), writes=DGM.r())
                for ti, (lo, hi) in enumerate(FT):
                    n = hi - lo
                    b1 = 2 * (pc[0] % 2)
                    b2 = b1 + 1
                    sg = SG[pc[0] % 3]
                    pc[0] += 1
                    proj(ha, XNS, b1, lo, hi)
                    proj(hb, XNS, b2, lo, hi)
                    P.op('act', lambda e, n=n, b2=b2, sg=sg, bb_=bb_: e.activation(sg.v[:, 0, 0:n], bk(b2, 0, n), AF.Sigmoid, bias=bb_),
                         reads=PS(b2) + VEC.r(), writes=sg.r())
                    P.op('dve', lambda e, n=n, b1=b1, sg=sg, lo=lo, hi=hi, ba_=ba_: e.scalar_tensor_tensor(
                        UO.v[:, 0, 30 + lo:30 + hi], bk(b1, 0, n), ba_, sg.v[:, 0, 0:n], ALU.add, ALU.mult),
                        reads=PS(b1) + sg.r() + VEC.r(), writes=UO.r(0, 30 + lo, 30 + hi))
                    if ti == 5:
                        P.op('dve', lambda e, n=n, b1=b1, sg=sg, ba_=ba_: e.scalar_tensor_tensor(
                            UF.v[:, 0, :], bk(b1, n - 34, n), ba_, sg.v[:, 0, n - 34:n], ALU.add, ALU.mult),
                            reads=PS(b1) + sg.r() + VEC.r(), writes=UF.r())
                cb_ = vcol(('ccb', o_), i)
                for blk in range(4):
                    t0 = blk * 512
                    b = 4 + (pc[0] % 2)
                    pc[0] += 1
                    for j in range(31):
                        P.op('pe', lambda e, j=j, t0=t0, b=b: e.matmul(
                            bk(b), DGM.v[:, j, :], UO.v[:, 0, t0 + j:t0 + j + 512], start=(j == 0), stop=(j == 30)),
                            reads=DGM.r(j) + UO.r(0, t0 + j, t0 + j + 512), writes=PS(b))
                    P.op('act', lambda e, t0=t0, b=b, i=i, cb_=cb_: e.activation(
                        CC.v[:, i, t0:t0 + 512], bk(b), AF.Identity, bias=cb_),
                        reads=PS(b) + VEC.r(), writes=CC.r(i, t0, t0 + 512))
                P.op('dve', lambda e, i=i, wv=wv: e.tensor_tensor(
                    PRS.v, SCS.v[:, i, :].rearrange("p (s k) -> p s k", k=30),
                    wv[:, 0:30].unsqueeze(1).to_broadcast([128, 4, 30]), ALU.mult),
                    reads=SCS.r(i) + VEC.r(), writes=PRS.r())
                P.op('dve', lambda e: e.tensor_reduce(CS4.v[:, 0, 0:4], PRS.v, AX.X, ALU.add), reads=PRS.r(), writes=CS4.r())
                P.op('dve', lambda e, wv=wv: e.scalar_tensor_tensor(
                    CS4.v[:, 0, 0:4], UF.v[:, 0, 30:34], wv[:, 30:31], CS4.v[:, 0, 0:4], ALU.mult, ALU.add),
                    reads=UF.r() + VEC.r() + CS4.r(), writes=CS4.r())
                P.op('dve', lambda e, i=i, cb_=cb_: e.tensor_scalar(
                    CC.v[:, i, NPROMPT:NT], CS4.v[:, 0, 0:4], cb_, None, ALU.add),
                    reads=CS4.r() + VEC.r(), writes=CC.r(i, NPROMPT, NT))
                P.op('sp', lambda e, i=i: e.dma_start(out=ccpT[o_, :, i, :], in_=UF.v[:, 0, 0:30]), reads=UF.r(), chan='occ')
                P.op('sp', lambda e, i=i: e.dma_start(
                    out=ccsoT[o_, :, i, :, :], in_=SCS.v[:, i, :].rearrange("p (s k) -> p s k", k=30)[:, :, 1:30]),
                    reads=SCS.r(i), chan='occ')
                P.op('sp', lambda e, i=i: e.dma_start(out=ccsnT[o_, :, i, :], in_=UF.v[:, 0, 30:34]),
                     reads=UF.r(), chan='occ')
            lc = [0]
            for (lo, hi) in FT:
                n = hi - lo
                bm, bv = 0 + 2 * (lc[0] % 2), 1 + 2 * (lc[0] % 2)
                mu, t1 = MU[lc[0] % 2], T1[lc[0] % 2]
                lc[0] += 1
                P.op('act', lambda e, lo=lo, hi=hi, n=n: e.activation(SQ.v[:, :, 0:n], CC.v[:, :, lo:hi], AF.Square),
                     reads=CC.r(None, lo, hi), writes=SQ.r(None, 0, n))
                for kc in range(8):
                    P.op('pe', lambda e, kc=kc, n=n, bm=bm, lo=lo, hi=hi: e.matmul(
                        bk(bm, 0, n), mean_b, CC.v[:, kc, lo:hi], start=(kc == 0), stop=(kc == 7)),
                        reads=CC.r(kc, lo, hi) + CB.r(1), writes=PS(bm))
                for kc in range(8):
                    P.op('pe', lambda e, kc=kc, n=n, bv=bv: e.matmul(
                        bk(bv, 0, n), mean_b, SQ.v[:, kc, 0:n], start=(kc == 0), stop=(kc == 7)),
                        reads=SQ.r(kc, 0, n) + CB.r(1), writes=PS(bv))
                P.op('act', lambda e, n=n, bm=bm, mu=mu: e.activation(mu.v[:, 0, 0:n], bk(bm, 0, n), AF.Copy),
                     reads=PS(bm), writes=mu.r())
                P.op('dve', lambda e, n=n, mu=mu, t1=t1: e.tensor_tensor(t1.v[:, 0, 0:n], mu.v[:, 0, 0:n], mu.v[:, 0, 0:n], ALU.mult),
                     reads=mu.r(), writes=t1.r())
                P.op('dve', lambda e, n=n, bv=bv, t1=t1: e.tensor_tensor(t1.v[:, 0, 0:n], bk(bv, 0, n), t1.v[:, 0, 0:n], ALU.subtract),
                     reads=PS(bv) + t1.r(), writes=t1.r())
                P.op('act', lambda e, n=n, t1=t1: e.activation(t1.v[:, 0, 0:n], t1.v[:, 0, 0:n], AF.Sqrt, bias=epsc),
                     reads=t1.r() + EPSC.r(), writes=t1.r())
                P.op('dve', lambda e, n=n, t1=t1: e.reciprocal(t1.v[:, 0, 0:n], t1.v[:, 0, 0:n]), reads=t1.r(), writes=t1.r())
                for c in range(8):
                    sg = SG[c % 3]
                    P.op('dve', lambda e, c=c, lo=lo, hi=hi, n=n, mu=mu, sg=sg: e.tensor_tensor(
                        sg.v[:, 0, 0:n], CC.v[:, c, lo:hi], mu.v[:, 0, 0:n], ALU.subtract),
                        reads=CC.r(c, lo, hi) + mu.r(), writes=sg.r())
                    P.op('dve', lambda e, n=n, t1=t1, sg=sg: e.tensor_tensor(
                        sg.v[:, 0, 0:n], sg.v[:, 0, 0:n], t1.v[:, 0, 0:n], ALU.mult),
                        reads=sg.r() + t1.r(), writes=sg.r())
                    P.op('act', lambda e, c=c, lo=lo, hi=hi, n=n, sg=sg: e.activation(
                        XN.v[:, c, lo:hi], sg.v[:, 0, 0:n], AF.Silu, bias=vcol(('lnb', o_), c), scale=vcol(('lng', o_), c)),
                        reads=sg.r() + VEC.r(), writes=XN.r(c, lo, hi))
            oc2 = [0]
            for dch in range(8):
                ho = ws.get()
                bo = vcol(('bpw2', o_), dch)
                for (lo, hi) in FT:
                    n = hi - lo
                    b = 4 + oc2[0] % 4
                    oc2[0] += 1
                    proj(ho, XNS, b, lo, hi)
                    P.op('dve', lambda e, dch=dch, lo=lo, hi=hi, n=n, b=b, bo=bo: e.scalar_tensor_tensor(
                        X.v[:, dch, lo:hi], bk(b, 0, n), bo, X.v[:, dch, lo:hi], ALU.add, ALU.add),
                        reads=PS(b) + X.r(dch, lo, hi) + VEC.r(), writes=X.r(dch, lo, hi))

        if n_even > 0:
            even_setup()
        for l in range(depth):
            rmsnorm(('n1', l), FT)
            ffn(0, l)
            if l % 2 == 0:
                even_mixer(l)
            else:
                odd_mixer(l)
            rmsnorm(('n2', l), FT)
            ffn(1, l)
        P.op('sp', lambda e: e.dma_start(out=yT, in_=X.v), reads=X.r(), chan='out')
        P.emit()
    return nc


NCORES = 8


def make_in_maps(inp, cores):
    consts, c2, _, dstat = static_tables()
    f32 = lambda a: np.ascontiguousarray(np.asarray(a, np.float32))
    vecsT = build_vecs(inp)
    relb = np.concatenate([f32(inp['rel_bias']), np.ones((1, 8), np.float32)], 0)
    lamv = np.stack([f32(inp['lambda_q1']), f32(inp['lambda_k1']), f32(inp['lambda_q2']), f32(inp['lambda_k2'])], 1)
    shared = dict(vecsT=vecsT, consts=consts, cst2=c2, dstat=dstat, relb=relb, lamv=f32(lamv), sgrow=f32(inp['subln_gain']),
                  )
    ckf = f32(inp['cache_k']).reshape(2, 2560 * 128, 512)
    cvf = f32(inp['cache_v']).reshape(2, 2560 * 128, 512)
    for i in range(2):
        shared['cache_k%d' % i] = ckf[i]
        shared['cache_v%d' % i] = cvf[i]
    for k in ["ffn1_w_gate", "ffn1_w_up", "ffn1_w_down", "ffn2_w_gate", "ffn2_w_up", "ffn2_w_down",
              "w_in_even", "w_out_even", "w_pw1", "w_pw2"]:
        shared[k] = f32(inp[k])
    maps = []
    for core in cores:
        xp = f32(inp['x_prompt'][core])
        xs = f32(inp['x_sample'][4 * core:4 * core + 4, 0])
        xall = np.concatenate([xp, xs], 0)
        m = dict(shared)
        m['xT'] = np.ascontiguousarray(xall.reshape(NT, 8, 128).transpose(2, 1, 0))
        sb = f32(inp['state_conv_b'][:, 4 * core:4 * core + 4])
        m['scb'] = np.ascontiguousarray(sb.reshape(2, 4, 2, 4, 128).transpose(4, 0, 3, 2, 1))
        sc = f32(inp['state_conv_c'][:, 4 * core:4 * core + 4])
        m['scc'] = np.ascontiguousarray(sc.reshape(2, 4, 30, 8, 128).transpose(4, 0, 3, 1, 2))
        m['ptab'] = np.ascontiguousarray(np.asarray(inp['page_table'], np.int32)[4 * core:4 * core + 4].reshape(1, 256))
        maps.append(m)
    return maps


def assemble(results, ncores):
    f = np.float32
    y_p = np.zeros((ncores, 2048, 1024), f)
    y_s = np.zeros((4 * ncores, 1, 1024), f)
    nk_p = np.zeros((2, ncores, 2048, 16, 32), f)
    nv_p = np.zeros((2, ncores, 2048, 8, 64), f)
    nk_s = np.zeros((2, 4 * ncores, 1, 16, 32), f)
    nv_s = np.zeros((2, 4 * ncores, 1, 8, 64), f)
    cb_p = np.zeros((2, ncores, 2, 512), f)
    cb_s = np.zeros((2, 4 * ncores, 2, 512), f)
    cc_p = np.zeros((2, ncores, 30, 1024), f)
    cc_s = np.zeros((2, 4 * ncores, 30, 1024), f)
    for ci, r in enumerate(results):
        y = r['yT'].transpose(2, 1, 0).reshape(NT, 1024)
        y_p[ci] = y[:2048]
        y_s[4 * ci:4 * ci + 4, 0] = y[2048:]
        for e in range(2):
            k = r['nkT'][e].transpose(2, 1, 0).reshape(NT, 512)
            v = r['nvT'][e].transpose(2, 1, 0).reshape(NT, 512)
            nk_p[e, ci] = k[:2048].reshape(2048, 16, 32)
            nv_p[e, ci] = v[:2048].reshape(2048, 8, 64)
            nk_s[e, 4 * ci:4 * ci + 4, 0] = k[2048:].reshape(4, 16, 32)
            nv_s[e, 4 * ci:4 * ci + 4, 0] = v[2048:].reshape(4, 8, 64)
            cb_p[e, ci] = r['cbpT'][e].transpose(2, 1, 0).reshape(2, 512)
            cb_s[e, 4 * ci:4 * ci + 4, 0] = r['cbs0T'][e].transpose(2, 1, 0).reshape(4, 512)
            cb_s[e, 4 * ci:4 * ci + 4, 1] = r['cbs1T'][e].transpose(2, 1, 0).reshape(4, 512)
            cc_p[e, ci] = r['ccpT'][e].transpose(2, 1, 0).reshape(30, 1024)
            cc_s[e, 4 * ci:4 * ci + 4, 0:29] = r['ccsoT'][e].transpose(2, 3, 1, 0).reshape(4, 29, 1024)
            cc_s[e, 4 * ci:4 * ci + 4, 29] = r['ccsnT'][e].transpose(2, 1, 0).reshape(4, 1024)
    return (y_p, y_s, nk_p, nv_p, nk_s, nv_s, cb_p, cb_s, cc_p, cc_s)


def kernel(**inputs):
    nc = build_nc(DEPTH)
    maps = make_in_maps(inputs, list(range(NCORES)))
    res = run_bass_kernel_spmd(nc, maps, core_ids=list(range(NCORES)))
    return assemble(res.results, NCORES)
```

```python
import bisect
import contextlib
import math
import numpy as np
import concourse.bass as bass
import concourse.mybir as mybir
from concourse.bass_utils import run_bass_kernel_spmd

F32 = mybir.dt.float32
BF16 = mybir.dt.bfloat16
I32 = mybir.dt.int32
U8 = mybir.dt.uint8
AF = mybir.ActivationFunctionType
ALU = mybir.AluOpType
AX = mybir.AxisListType

D = 1024
NPROMPT = 2048
NSAMP = 4
NT = NPROMPT + NSAMP
DEPTH = 4
DFF = 2816
NFF = DFF // 128
GFF = 11
EPS = 1e-6
FT = [(342 * i, 342 * (i + 1)) for i in range(6)]
MT = [(0, 512), (512, 1024), (1024, 1536), (1536, 2048), (2048, 2052)]


class IMap:
    def __init__(self):
        self.starts = [0]
        self.data = {0: [1 << 40, None, []]}

    def _split(self, x):
        i = bisect.bisect_right(self.starts, x) - 1
        s = self.starts[i]
        e, w, r = self.data[s]
        if s == x or x >= e:
            return
        self.data[s] = [x, w, list(r)]
        self.data[x] = [e, w, list(r)]
        bisect.insort(self.starts, x)

    def access(self, a, b, oid, is_write, deps):
        self._split(a)
        self._split(b)
        i = bisect.bisect_left(self.starts, a)
        n = len(self.starts)
        while i < n and self.starts[i] < b:
            seg = self.data[self.starts[i]]
            if is_write:
                if seg[1] is not None:
                    deps.add((seg[1], 'waw'))
                for r in seg[2]:
                    deps.add((r, 'war'))
                seg[1] = oid
                seg[2] = []
            else:
                if seg[1] is not None:
                    deps.add((seg[1], 'raw'))
                seg[2].append(oid)
            i += 1


class Prog:
    COMPUTE = ('pe', 'act', 'dve', 'pool')
    ENGS = ('pe', 'act', 'dve', 'pool', 'sp')

    def __init__(self, nc):
        self.nc = nc
        self.ops = []
        self.buf = {}
        self.imaps = {}
        self.chan_last = {}
        self.chan_cnt = {}

    def _acc(self, k, oid, is_write, deps):
        if isinstance(k, tuple) and k and k[0] == 'iv':
            _, space, a, b = k
            self.imaps.setdefault(space, IMap()).access(a, b, oid, is_write, deps)
            return
        st = self.buf.setdefault(k, [None, []])
        if is_write:
            if st[0] is not None:
                deps.add((st[0], 'waw'))
            for r in st[1]:
                deps.add((r, 'war'))
            self.buf[k] = [oid, []]
        else:
            if st[0] is not None:
                deps.add((st[0], 'raw'))
            st[1].append(oid)

    def op(self, eng, fn, reads=(), writes=(), chan=None):
        oid = len(self.ops)
        deps = set()
        is_dma = chan is not None
        for k in reads:
            self._acc(k, oid, False, deps)
        for k in writes:
            self._acc(k, oid, True, deps)
        if is_dma and chan in self.chan_last:
            deps.add((self.chan_last[chan], 'chan'))
        dma_idx = None
        if is_dma:
            self.chan_last[chan] = oid
            dma_idx = self.chan_cnt.get(chan, 0) + 1
            self.chan_cnt[chan] = dma_idx
        fdeps = set()
        for (p, kind) in deps:
            if p == oid:
                continue
            po = self.ops[p]
            if po['chan'] is None and not is_dma and po['eng'] == eng:
                if eng == 'pe' or kind != 'raw':
                    continue
            fdeps.add(p)
        self.ops.append(dict(eng=eng, fn=fn, deps=fdeps, chan=chan, dma_idx=dma_idx, sig=False, sig_idx=None))
        return oid

    def emit(self):
        nc = self.nc
        ops = self.ops
        for o in ops:
            for p in o['deps']:
                ops[p]['sig'] = True
        cnt = {e: 0 for e in self.COMPUTE}
        for o in ops:
            if o['chan'] is None and o['sig']:
                cnt[o['eng']] += 1
                o['sig_idx'] = cnt[o['eng']]
        chans = sorted(self.chan_cnt.keys(), key=str)
        with contextlib.ExitStack() as es:
            sem = {}
            for e in self.COMPUTE:
                sem[('e', e)] = es.enter_context(nc.semaphore('s_' + e))
            for ci, c in enumerate(chans):
                sem[('c', c)] = es.enter_context(nc.semaphore('c%d' % ci))
            block = es.enter_context(nc.Block())
            by_eng = {e: [] for e in self.ENGS}
            for i, o in enumerate(ops):
                by_eng[o['eng']].append(i)

            def run(engname, engobj):
                waited = {}
                for i in by_eng[engname]:
                    o = ops[i]
                    need = {}
                    for p in o['deps']:
                        po = ops[p]
                        if po['chan'] is None:
                            k = ('e', po['eng'])
                            v = po['sig_idx']
                        else:
                            k = ('c', po['chan'])
                            v = 16 * po['dma_idx']
                        if need.get(k, 0) < v:
                            need[k] = v
                    for k, v in need.items():
                        if waited.get(k, 0) >= v:
                            continue
                        engobj.wait_ge(sem[k], v)
                        waited[k] = v
                    ins = o['fn'](engobj)
                    if o['chan'] is not None:
                        ins.then_inc(sem[('c', o['chan'])], 16)
                    elif o['sig']:
                        ins.then_inc(sem[('e', engname)], 1)
                if engname == 'sp':
                    for c in chans:
                        engobj.wait_ge(sem[('c', c)], 16 * self.chan_cnt[c])

            block.tensor(lambda e: run('pe', e))
            block.scalar(lambda e: run('act', e))
            block.vector(lambda e: run('dve', e))
            block.gpsimd(lambda e: run('pool', e))
            block.sync(lambda e: run('sp', e))


class Ten:
    def __init__(self, big, off, n0, n1, dt, esz):
        assert off % 32 == 0
        self.off, self.n0, self.n1, self.esz = off, n0, n1, esz
        self.nbytes = n0 * n1 * esz
        self.v = big[:, off:off + self.nbytes].bitcast(dt).rearrange("p (a b) -> p a b", b=n1)

    def r(self, i=None, lo=0, hi=None, i1=None):
        if hi is None:
            hi = self.n1
        if i is None:
            i, i1 = 0, self.n0
        elif i1 is None:
            i1 = i + 1
        if lo == 0 and hi == self.n1:
            return [('iv', 'sb', self.off + i * self.n1 * self.esz, self.off + i1 * self.n1 * self.esz)]
        return [('iv', 'sb', self.off + (j * self.n1 + lo) * self.esz, self.off + (j * self.n1 + hi) * self.esz)
                for j in range(i, i1)]


class Alloc:
    def __init__(self, big, limit):
        self.big, self.limit, self.cur = big, limit, 0

    def ten(self, n0, n1, dt):
        esz = {F32: 4, BF16: 2, I32: 4}[dt]
        t = Ten(self.big, self.cur, n0, n1, dt, esz)
        self.cur += (t.nbytes + 31) // 32 * 32
        assert self.cur <= self.limit, (self.cur, self.limit)
        return t

    def at(self, off, n0, n1, dt):
        esz = {F32: 4, BF16: 2, I32: 4}[dt]
        t = Ten(self.big, off, n0, n1, dt, esz)
        assert off + t.nbytes <= self.limit
        return t


NEG = -30000.0
SCALE = 32 ** -0.5
NTAB = 639
TS = 4
NPAGES = 64


def t5_bucket_np(n):
    n = np.asarray(n, np.int64)
    nn = np.maximum(n, 0)
    nf = np.maximum(nn, 1).astype(np.float32)
    large = 16 + (np.log(nf / np.float32(16)) / np.float32(math.log(128 / 16)) * np.float32(16)).astype(np.int32)
    large = np.minimum(large, 31)
    return np.where(nn < 16, nn, large).astype(np.int64)


def static_tables():
    consts = np.zeros((128, 7, 128), np.float32)
    consts[:, 0] = np.eye(128)
    consts[:, 1] = 1.0 / 1024
    consts[:, 2] = np.kron(np.eye(4), np.ones((32, 32))) / 32
    consts[:, 3] = np.eye(128)[::-1]
    consts[:, 4] = 1.0
    consts[:, 5, 0:64] = 1.0
    consts[:, 6, 64:128] = 1.0
    lay = {}
    cur = 0

    def add(name, n):
        nonlocal cur
        lay[name] = (cur, n)
        cur += n
    add('iota', 1); add('sgn0', 1); add('sgn1', 1); add('maskd', 512); add('pair', 8); add('maskp', 128); add('selc', 4); add('iota32', 32); add('negm', 1)
    c2 = np.zeros((128, cur), np.float32)
    c2[:, lay['iota'][0]] = np.arange(128)
    c2[:, lay['iota32'][0]:lay['iota32'][0] + 32] = np.arange(32)[None, :]
    c2[63, lay['negm'][0]] = NEG
    c2[127, lay['negm'][0]] = NEG
    for base in (0, 32):
        for hs in range(16):
            c2[base + hs, lay['sgn0'][0]] = 1.0 if hs % 2 == 0 else 0.0
            c2[base + hs, lay['sgn1'][0]] = -1.0 if hs % 2 == 1 else 0.0
            h = hs // 2
            c2[base + hs, lay['maskd'][0] + h * 64: lay['maskd'][0] + (h + 1) * 64] = 1.0
            c2[base + hs, lay['pair'][0] + h] = 1.0
    for h in range(8):
        r = h % 2
        c2[h, lay['maskp'][0] + r * 64: lay['maskp'][0] + (r + 1) * 64] = 1.0
        c2[h, lay['selc'][0] + h // 2] = 1.0
    ds_ = np.zeros((33, NTAB + 256), np.float32)
    for i in range(NTAB):
        n = i - 255
        if n < 0:
            ds_[32, i] = NEG
        else:
            ds_[int(t5_bucket_np(n)), i] += 1.0
            ds_[31, i] -= 1.0
    for p in range(128):
        ds_[int(t5_bucket_np(128 - p)), NTAB + p] += 1.0
        ds_[31, NTAB + p] -= 1.0
    ds_[0, NTAB + 128] += 1.0
    ds_[31, NTAB + 128] -= 1.0
    ds_[32, NTAB + 129:NTAB + 256] = NEG
    return consts, c2, lay, ds_


def vec_layout():
    lay = {}
    cur = 0

    def add(name, n):
        nonlocal cur
        lay[name] = (cur, n)
        cur += n
    for l in range(DEPTH):
        add(('n1', l), 8)
        add(('nm', l), 8)
        add(('n2', l), 8)
    for o in range(2):
        add(('bpw1', o), 16)
        add(('ccb', o), 8)
        add(('lng', o), 8)
        add(('lnb', o), 8)
        add(('bpw2', o), 8)
        add(('ccw', o), 31 * 8)
    for e in range(2):
        add(('cbw', e), 3 * 4)
        add(('qg', e), 1)
        add(('kg', e), 1)
    return lay, cur


def build_vecs(inp):
    lay, ncol = vec_layout()
    V = np.zeros((128, ncol), np.float32)

    def put(name, vec):
        c0, n = lay[name]
        V[:, c0:c0 + n] = np.asarray(vec, np.float32).reshape(n, 128).T
    for l in range(DEPTH):
        put(('n1', l), inp['norm_ffn1'][l])
        put(('nm', l), inp['norm_mix'][l])
        put(('n2', l), inp['norm_ffn2'][l])
    for o in range(2):
        put(('bpw1', o), inp['b_pw1'][o])
        put(('ccb', o), inp['conv_c_b'][o])
        put(('lng', o), inp['ln_c_gain'][o])
        put(('lnb', o), inp['ln_c_bias'][o])
        put(('bpw2', o), inp['b_pw2'][o])
        put(('ccw', o), np.asarray(inp['conv_c_w'][o]).reshape(-1))
    for e in range(2):
        put(('cbw', e), np.asarray(inp['conv_b_w'][e]).reshape(-1))
        put(('qg', e), np.tile(np.asarray(inp['q_norm_gain'][e]), 4))
        put(('kg', e), np.tile(np.asarray(inp['k_norm_gain'][e]), 4))
    return V


def build_nc(depth=DEPTH):
    nc = bass.Bass("TRN2", target_bir_lowering=False)
    lay, ncol = vec_layout()
    _, _, lay2, _ = static_tables()
    nc2 = sum(v[1] for v in lay2.values())
    n_even = (depth + 1) // 2
    n_odd = depth // 2

    def din(name, shape, dtype=F32):
        return nc.dram_tensor(name, list(shape), dtype, kind="ExternalInput").ap()

    def dout(name, shape, dtype=F32):
        return nc.dram_tensor(name, list(shape), dtype, kind="ExternalOutput").ap()

    xT = din("xT", [128, 8, NT])
    vecsT = din("vecsT", [128, ncol])
    consts = din("consts", [128, 7, 128])
    cst2 = din("cst2", [128, nc2])
    dstat = din("dstat", [33, NTAB + 256])
    relb = din("relb", [33, 8])
    lamv = din("lamv", [2, 4, 32])
    sgrow = din("sgrow", [2, 64])
    scb = din("scb", [128, 2, 4, 2, 4])
    scc = din("scc", [128, 2, 8, 4, 30])
    ptab = din("ptab", [1, NSAMP * NPAGES], I32)
    ccwrep = din("ccwrep", [2, 8, 128, 4, 32])
    cache_k = [din("cache_k%d" % i, [2560 * 128, 512]) for i in range(2)]
    cache_v = [din("cache_v%d" % i, [2560 * 128, 512]) for i in range(2)]
    NSTEP = 128 // TS
    wg = [din("ffn1_w_gate", [DEPTH, D, DFF]), din("ffn2_w_gate", [DEPTH, D, DFF])]
    wu = [din("ffn1_w_up", [DEPTH, D, DFF]), din("ffn2_w_up", [DEPTH, D, DFF])]
    wd = [din("ffn1_w_down", [DEPTH, DFF, D]), din("ffn2_w_down", [DEPTH, DFF, D])]
    w_in = din("w_in_even", [2, D, 3072])
    w_out = din("w_out_even", [2, D, D])
    w_pw1 = din("w_pw1", [2, D, 2048])
    w_pw2 = din("w_pw2", [2, D, D])
    yT = dout("yT", [128, 8, NT])
    nkT = dout("nkT", [2, 128, 4, NT])
    nvT = dout("nvT", [2, 128, 4, NT])
    cbpT = dout("cbpT", [2, 128, 4, 2])
    cbs0T = dout("cbs0T", [2, 128, 4, 4])
    cbs1T = dout("cbs1T", [2, 128, 4, 4])
    ccpT = dout("ccpT", [2, 128, 8, 30])
    ccsoT = dout("ccsoT", [2, 128, 8, 4, 29])
    ccsnT = dout("ccsnT", [2, 128, 8, 4])
    tabd = nc.dram_tensor("tabd", [8, NTAB], F32)

    P = Prog(nc)
    with contextlib.ExitStack() as es:
        LIMIT = 212800
        big = es.enter_context(nc.sbuf_tensor("big", [128, LIMIT], U8))
        PSA = es.enter_context(nc.psum_tensor("psa", [128, 4096], F32))
        A = Alloc(big, LIMIT)
        X = A.ten(8, NT, F32)
        XN = A.ten(8, NT, BF16)
        VEC = A.ten(1, ncol, F32)
        CF = A.ten(7, 128, F32)
        CB = A.ten(5, 128, BF16)
        C2 = A.ten(1, nc2, F32)
        RSTD = A.ten(1, NT, F32)
        EPSC = A.ten(1, 8, F32)
        LAMIN = A.ten(2, 128, F32)
        LAM = A.ten(2, 8, F32)
        SGR = A.ten(2, 64, F32)
        BDEC = A.ten(2, 8, F32)
        LAMC = A.ten(2, 1, F32)
        SCB = A.ten(2, 32, F32)
        IDX63 = A.ten(1, 8, I32)
        QS = A.ten(4, 4, F32)
        KS = A.ten(4, 4, F32)
        VS = A.ten(4, 4, F32)
        IDX2 = A.ten(2, 32, I32)
        NHS = 8
        HS = [A.ten(8, 128, BF16) for _ in range(NHS)]
        SG = [A.ten(1, 342, F32) for _ in range(3)]
        R0 = A.cur
        WD = A.ten(GFF, 1024, BF16)
        H = A.ten(GFF, NT, BF16)
        assert A.cur <= LIMIT
        RSIZE = LIMIT - R0
        SQH = A.at(H.off, 8, 512, BF16)

        def bk(b, n0=0, n1=512):
            return PSA[:, b * 512 + n0: b * 512 + n1]

        def PS(*bs):
            return [('ps', b) for b in bs]

        def vcol(name, j=0):
            return VEC.v[:, 0, lay[name][0] + j: lay[name][0] + j + 1]

        def c2v(name, p0, p1):
            c0, n = lay2[name]
            return C2.v[p0:p1, 0, c0:c0 + n]

        ident = CF.v[:, 0, :]
        ones_f = CF.v[:, 4, :]
        Jf = CF.v[:, 3, :]
        mean_b = CB.v[:, 1, :]
        blk_b = CB.v[:, 2, :]
        ident_b = CB.v[:, 0, :]
        ones_b = CB.v[:, 4, :]

        P.op('sp', lambda e: e.dma_start(out=X.v, in_=xT), writes=X.r(), chan='ld0')
        P.op('sp', lambda e: e.dma_start(out=VEC.v[:, 0, :], in_=vecsT), writes=VEC.r(), chan='ld')
        P.op('sp', lambda e: e.dma_start(out=CF.v, in_=consts), writes=CF.r(), chan='ld')
        P.op('sp', lambda e: e.dma_start(out=C2.v[:, 0, :], in_=cst2), writes=C2.r(), chan='ld')
        P.op('dve', lambda e: e.tensor_copy(CB.v, CF.v[:, 0:5, :]), reads=CF.r(), writes=CB.r())
        P.op('dve', lambda e: e.memset(EPSC.v[:, 0, :], EPS), writes=EPSC.r())
        epsc = EPSC.v[:, 0, 0:1]

        hs_ctr = [0]

        def load_col(w2d, col0):
            i = hs_ctr[0] % NHS
            hs_ctr[0] += 1
            src = w2d.rearrange("(kc p) f -> p kc f", p=128)[:, :, col0:col0 + 128]
            t = HS[i]
            P.op('pool', lambda e: e.dma_start(out=t.v, in_=src), writes=t.r(), chan=('hs', i))
            return t

        class WStream:
            def __init__(self, items, depth):
                self.items, self.i, self.q, self.depth = list(items), 0, [], depth

            def fill(self):
                while self.i < len(self.items) and len(self.q) < self.depth:
                    self.q.append(load_col(*self.items[self.i]))
                    self.i += 1

            def get(self):
                self.fill()
                h = self.q.pop(0)
                self.fill()
                return h

        def proj(hs, srcs, b, lo, hi):
            n = hi - lo
            for kc in range(8):
                t, row = srcs[kc]
                P.op('pe', lambda e, kc=kc, t=t, row=row: e.matmul(
                    bk(b, 0, n), hs.v[:, kc, :], t.v[:, row, lo:hi], start=(kc == 0), stop=(kc == 7)),
                    reads=hs.r(kc) + t.r(row, lo, hi), writes=PS(b))

        XNS = [(XN, kc) for kc in range(8)]

        nrm_ctr = [0]

        def rmsnorm(gname, tiles):
            SQ = SQH
            for (lo, hi) in tiles:
                n = hi - lo
                b = 6 + nrm_ctr[0] % 2
                nrm_ctr[0] += 1
                P.op('act', lambda e, lo=lo, hi=hi, n=n: e.activation(SQ.v[:, :, 0:n], X.v[:, :, lo:hi], AF.Square),
                     reads=X.r(None, lo, hi), writes=SQ.r(None, 0, n))
                for kc in range(8):
                    P.op('pe', lambda e, kc=kc, n=n, b=b: e.matmul(bk(b, 0, n), mean_b, SQ.v[:, kc, 0:n],
                                                                  start=(kc == 0), stop=(kc == 7)),
                         reads=SQ.r(kc, 0, n) + CB.r(1), writes=PS(b))
                P.op('act', lambda e, lo=lo, hi=hi, n=n, b=b: e.activation(
                    RSTD.v[:, 0, lo:hi], bk(b, 0, n), AF.Sqrt, bias=epsc),
                    reads=PS(b) + EPSC.r(), writes=RSTD.r(0, lo, hi))
                P.op('dve', lambda e, lo=lo, hi=hi: e.reciprocal(RSTD.v[:, 0, lo:hi], RSTD.v[:, 0, lo:hi]),
                     reads=RSTD.r(0, lo, hi), writes=RSTD.r(0, lo, hi))
                for c in range(8):
                    P.op('dve', lambda e, c=c, lo=lo, hi=hi: e.scalar_tensor_tensor(
                        XN.v[:, c, lo:hi], X.v[:, c, lo:hi], vcol(gname, c), RSTD.v[:, 0, lo:hi],
                        ALU.mult, ALU.mult),
                        reads=X.r(c, lo, hi) + RSTD.r(0, lo, hi) + VEC.r(), writes=XN.r(c, lo, hi))

        def ffn(which, l):
            gu_ctr = 0
            sgc = 0
            dn_ctr = 0
            pend = []
            nxt = [0]

            def prefetch(upto):
                while nxt[0] < min(upto, NFF):
                    f = nxt[0]
                    pend.append((load_col(wg[which][l], f * 128), load_col(wu[which][l], f * 128)))
                    nxt[0] += 1
            for g in range(2):
                prefetch(g * GFF + 2)
                for j in range(GFF):
                    f = g * GFF + j
                    src = wd[which][l][f * 128:(f + 1) * 128, :]
                    P.op('pool', lambda e, j=j, src=src: e.dma_start(out=WD.v[:, j, :], in_=src),
                         writes=WD.r(j), chan=('wd', j % 4))
                for j in range(GFF):
                    f = g * GFF + j
                    prefetch(f + 3)
                    hg, hu = pend.pop(0)
                    for (lo, hi) in FT:
                        n = hi - lo
                        bg = (gu_ctr % 3) * 2
                        bu = bg + 1
                        gu_ctr += 1
                        proj(hg, XNS, bg, lo, hi)
                        proj(hu, XNS, bu, lo, hi)
                        sg = SG[sgc % 3]
                        sgc += 1
                        P.op('act', lambda e, n=n, bg=bg, sg=sg: e.activation(sg.v[:, 0, 0:n], bk(bg, 0, n), AF.Silu),
                             reads=PS(bg), writes=sg.r())
                        P.op('dve', lambda e, n=n, bu=bu, sg=sg, j=j, lo=lo, hi=hi: e.tensor_tensor(
                            H.v[:, j, lo:hi], sg.v[:, 0, 0:n], bk(bu, 0, n), ALU.mult),
                            reads=sg.r() + PS(bu), writes=H.r(j, lo, hi))
                for (lo, hi) in FT:
                    n = hi - lo
                    for dch in range(8):
                        b = 6 + (dn_ctr % 2)
                        dn_ctr += 1
                        for j in range(GFF):
                            P.op('pe', lambda e, j=j, dch=dch, lo=lo, hi=hi, n=n, b=b: e.matmul(
                                bk(b, 0, n), WD.v[:, j, dch * 128:(dch + 1) * 128], H.v[:, j, lo:hi],
                                start=(j == 0), stop=(j == GFF - 1)),
                                reads=WD.r(j) + H.r(j, lo, hi), writes=PS(b))
                        P.op('dve', lambda e, dch=dch, lo=lo, hi=hi, n=n, b=b: e.scalar_tensor_tensor(
                            X.v[:, dch, lo:hi], bk(b, 0, n), 0.5, X.v[:, dch, lo:hi], ALU.mult, ALU.add),
                            reads=PS(b) + X.r(dch, lo, hi), writes=X.r(dch, lo, hi))

        def even_setup():
            B0 = Alloc(big, LIMIT)
            B0.cur = R0
            RB = B0.ten(1, 8, F32)
            DS = B0.ten(1, NTAB + 256, F32)
            TB = B0.ten(1, NTAB, F32)
            PTI = B0.ten(1, NSAMP * NPAGES, I32)
            PTF = B0.ten(1, NSAMP * NPAGES, F32)
            PRD = B0.ten(2, 32, F32)
            P63 = B0.ten(1, 8, F32)
            PT2I = B0.ten(1, 8, I32)
            PT2F = B0.ten(1, 8, F32)
            P2F = B0.ten(1, 32, F32)
            P.op('sp', lambda e: e.dma_start(out=RB.v[0:33, 0, :], in_=relb), writes=RB.r(), chan='ld')
            P.op('sp', lambda e: e.dma_start(out=DS.v[0:33, 0, :], in_=dstat), writes=DS.r(), chan='ld')
            for (c0, c1) in ((0, 512), (512, NTAB)):
                P.op('pe', lambda e, c0=c0, c1=c1: e.matmul(bk(0, 0, c1 - c0)[0:8, :], RB.v[0:33, 0, :], DS.v[0:33, 0, c0:c1],
                                                           start=True, stop=True),
                     reads=RB.r() + DS.r(), writes=PS(0))
                P.op('dve', lambda e, c0=c0, c1=c1: e.tensor_copy(TB.v[0:8, 0, c0:c1], bk(0, 0, c1 - c0)[0:8, :]),
                     reads=PS(0), writes=TB.r(0, c0, c1))
            P.op('sp', lambda e: e.dma_start(out=tabd.ap(), in_=TB.v[0:8, 0, :]), reads=TB.r(), writes=['tabd'], chan='ld')
            for k in range(2):
                P.op('pe', lambda e, k=k: e.matmul(bk(1, 0, 8), DS.v[0:33, 0, NTAB + 128 * k: NTAB + 128 * (k + 1)],
                                                  RB.v[0:33, 0, :], start=True, stop=True),
                     reads=RB.r() + DS.r(), writes=PS(1))
                P.op('dve', lambda e, k=k: e.tensor_copy(BDEC.v[:, k, :], bk(1, 0, 8)), reads=PS(1), writes=BDEC.r(k))
            P.op('sp', lambda e: e.dma_start(
                out=LAMIN.v, in_=bass.AP(lamv.tensor, 0, [[0, 128], [128, 2], [1, 128]])), writes=LAMIN.r(), chan='ld')
            P.op('sp', lambda e: e.dma_start(
                out=SGR.v, in_=bass.AP(sgrow.tensor, 0, [[0, 128], [64, 2], [1, 64]])), writes=SGR.r(), chan='ld')
            P.op('sp', lambda e: e.dma_start(out=SCB.v, in_=scb.rearrange("p e c k s -> p e (c k s)")),
                 writes=SCB.r(), chan='ld')
            P.op('sp', lambda e: e.dma_start(
                out=PTI.v[:, 0, :], in_=bass.AP(ptab.tensor, 0, [[0, 128], [1, NSAMP * NPAGES]])),
                writes=PTI.r(), chan='ld')
            P.op('dve', lambda e: e.tensor_copy(PTF.v, PTI.v), reads=PTI.r(), writes=PTF.r())
            P.op('dve', lambda e: e.tensor_scalar(
                P63.v[:, 0, 0:4], PTF.v[:, 0, NPAGES - 1:NSAMP * NPAGES:NPAGES], 128.0, c2v('iota', 0, 128), ALU.mult, ALU.add),
                reads=PTF.r() + C2.r(), writes=P63.r())
            P.op('dve', lambda e: e.tensor_copy(IDX63.v[:, 0, 0:4], P63.v[:, 0, 0:4]), reads=P63.r(), writes=IDX63.r())
            for a in range(2):
                P.op('sp', lambda e, a=a: e.dma_start(out=PT2I.v[:, 0, a:a + 1], in_=bass.AP(ptab.tensor, a * 128, [[1, 128], [1, 1]])),
                     writes=PT2I.r(), chan='ld')
            P.op('dve', lambda e: e.tensor_copy(PT2F.v[:, 0, 0:2], PT2I.v[:, 0, 0:2]), reads=PT2I.r(), writes=PT2F.r())
            for a in range(2):
                P.op('dve', lambda e, a=a: e.scalar_tensor_tensor(
                    P2F.v[:, 0, :], c2v('iota32', 0, 128), 1.0 / NSTEP, PT2F.v[:, 0, a:a + 1].to_broadcast([128, 32]), ALU.mult, ALU.add),
                    reads=PT2F.r() + C2.r(), writes=P2F.r())
                P.op('dve', lambda e: e.tensor_scalar(P2F.v[:, 0, :], P2F.v[:, 0, :], float(NSTEP), None, ALU.mult),
                     reads=P2F.r(), writes=P2F.r())
                P.op('dve', lambda e, a=a: e.tensor_copy(IDX2.v[:, a, :], P2F.v[:, 0, :]), reads=P2F.r(), writes=IDX2.r(a))
            for e_ in range(n_even):
                lam_init = 0.8 - 0.6 * math.exp(-0.3 * (2 * e_))
                for k in range(2):
                    P.op('dve', lambda e, e_=e_, k=k: e.tensor_tensor(
                        PRD.v[:, k, :], LAMIN.v[:, e_, 64 * k:64 * k + 32], LAMIN.v[:, e_, 64 * k + 32:64 * k + 64],
                        ALU.mult), reads=LAMIN.r(), writes=PRD.r(k))
                P.op('dve', lambda e, e_=e_: e.tensor_reduce(LAM.v[:, e_, 2:4], PRD.v, AX.X, ALU.add),
                     reads=PRD.r(), writes=LAM.r(e_))
                P.op('act', lambda e, e_=e_: e.activation(LAM.v[:, e_, 2:4], LAM.v[:, e_, 2:4], AF.Exp),
                     reads=LAM.r(e_), writes=LAM.r(e_))
                P.op('dve', lambda e, e_=e_: e.tensor_tensor(LAM.v[:, e_, 0:1], LAM.v[:, e_, 2:3], LAM.v[:, e_, 3:4],
                                                            ALU.subtract), reads=LAM.r(e_), writes=LAM.r(e_))
                P.op('dve', lambda e, e_=e_, li=lam_init: e.tensor_scalar(
                    LAM.v[:, e_, 0:1], LAM.v[:, e_, 0:1], li, None, ALU.add), reads=LAM.r(e_), writes=LAM.r(e_))
                P.op('dve', lambda e, e_=e_: e.tensor_scalar(LAM.v[:, e_, 1:2], LAM.v[:, e_, 0:1], -1.0, None, ALU.mult),
                     reads=LAM.r(e_), writes=LAM.r(e_))
                P.op('dve', lambda e, e_=e_, li=lam_init: e.tensor_scalar(SGR.v[:, e_, :], SGR.v[:, e_, :], 1.0 - li, None, ALU.mult),
                     reads=SGR.r(e_), writes=SGR.r(e_))
                P.op('dve', lambda e, e_=e_: e.scalar_tensor_tensor(
                    LAMC.v[0:48, e_, :], c2v('sgn1', 0, 48), LAM.v[0:48, e_, 0:1], c2v('sgn0', 0, 48), ALU.mult, ALU.add),
                    reads=LAM.r(e_) + C2.r(), writes=LAMC.r(e_))

        def even_mixer(l):
            e_ = l // 2
            lam_init = 0.8 - 0.6 * math.exp(-0.3 * l)
            B = Alloc(big, LIMIT)
            B.cur = R0
            ATT = B.ten(4, NT, BF16)
            CATB = B.ten(4, NT, BF16)
            S0 = B.cur
            QN = B.ten(1, NT, BF16)
            KN = B.ten(1, NT, BF16)
            VA = B.ten(16, 130, BF16)
            ECB = B.ten(2, 512, BF16)
            ZS = [B.ten(1, 512, F32) for _ in range(2)]
            SQB = [B.ten(1, 512, BF16) for _ in range(2)]
            RQ = [B.ten(1, 512, F32) for _ in range(2)]
            KF = [B.ten(1, 512, F32) for _ in range(2)]
            PT = [B.ten(2, 256, BF16) for _ in range(3)]
            E2 = B.at(KF[0].off, 2, 512, F32)
            assert KF[1].off == KF[0].off + 2048
            B2 = Alloc(big, SG[2].off + 1376)
            B2.cur = SG[0].off
            ATOK = [B2.ten(1, 256, F32) for _ in range(2)]
            OS = B2.ten(4, 130, F32)
            A1 = B.ten(2, 128, F32)
            RR = B.ten(1, 16, F32)
            S1 = B.cur
            w2d = w_in[e_]
            qg, kg = vcol(('qg', e_)), vcol(('kg', e_))
            nlam = LAM.v[:, e_, 1:2]

            items = []
            for c in range(4):
                items += [(w2d, c * 128), (w2d, (4 + c) * 128), (w2d, (8 + c) * 128)]
            for i in range(4):
                items += [(w2d, (16 + i) * 128), (w2d, (20 + i) * 128), (w2d, (12 + i) * 128)]
            for dch in range(8):
                items += [(w_out[e_], dch * 128)]
            ws = WStream(items, 4)
            ws.fill()
            rmsnorm(('nm', l), MT)
            P.op('pool', lambda e: e.memset(VA.v, 1.0), writes=VA.r())

            zc = [0]
            tc_ = [0]
            for c in range(4):
                hq = ws.get()
                hk = ws.get()
                hv = ws.get()
                srcE = bass.AP(tabd, 2 * c * NTAB, [[1, 128], [NTAB, 2], [1, 512]])
                P.op('sp', lambda e, srcE=srcE: e.dma_start(out=E2.v, in_=srcE), reads=['tabd'], writes=E2.r(), chan='e2')
                for hh in range(2):
                    P.op('pe', lambda e, hh=hh: e.matmul(bk(7), Jf, E2.v[:, hh, :], start=True, stop=True),
                         reads=E2.r(hh) + CF.r(3), writes=PS(7))
                    P.op('act', lambda e, hh=hh: e.activation(ECB.v[:, hh, :], bk(7), AF.Exp), reads=PS(7), writes=ECB.r(hh))
                tl = [(kind, hs, ti, lo, hi) for kind, hs in (('q', hq), ('k', hk), ('v', hv)) for ti, (lo, hi) in enumerate(MT)]
                tinfo = {}

                def st1(t):
                    kind, hs, ti, lo, hi = tl[t]
                    n = hi - lo
                    b = zc[0] % 4
                    zi = zc[0] % 2
                    zc[0] += 1
                    tinfo[t] = (b, zi)
                    proj(hs, XNS, b, lo, hi)
                    if kind in 'qk':
                        zs, sqb = ZS[zi], SQB[zi]
                        P.op('act', lambda e, n=n, b=b, zs=zs: e.activation(zs.v[:, 0, 0:n], bk(b, 0, n), AF.Copy),
                             reads=PS(b), writes=zs.r(0, 0, n))
                        P.op('act', lambda e, n=n, b=b, sqb=sqb: e.activation(sqb.v[:, 0, 0:n], bk(b, 0, n), AF.Square),
                             reads=PS(b), writes=sqb.r(0, 0, n))
                    else:
                        vf = KF[zi]
                        P.op('act', lambda e, n=n, b=b, vf=vf: e.activation(vf.v[:, 0, 0:n], bk(b, 0, n), AF.Copy),
                             reads=PS(b), writes=vf.r(0, 0, n))
                        P.op('sp', lambda e, n=n, lo=lo, hi=hi, vf=vf, c=c: e.dma_start(
                            out=nvT[e_, :, c, lo:hi], in_=vf.v[:, 0, 0:n]), reads=vf.r(0, 0, n), chan=('ok', zi))

                def st2(t):
                    kind, hs, ti, lo, hi = tl[t]
                    n = hi - lo
                    b, zi = tinfo[t]
                    if kind in 'qk':
                        zs, sqb, rq = ZS[zi], SQB[zi], RQ[zi]
                        b2 = 4 + zi
                        P.op('pe', lambda e, n=n, b2=b2, sqb=sqb: e.matmul(bk(b2, 0, n), blk_b, sqb.v[:, 0, 0:n],
                                                                          start=True, stop=True),
                             reads=sqb.r(0, 0, n) + CB.r(2), writes=PS(b2))
                        P.op('act', lambda e, n=n, b2=b2, rq=rq: e.activation(rq.v[:, 0, 0:n], bk(b2, 0, n), AF.Ln, bias=epsc),
                             reads=PS(b2) + EPSC.r(), writes=rq.r(0, 0, n))
                        P.op('act', lambda e, n=n, rq=rq: e.activation(rq.v[:, 0, 0:n], rq.v[:, 0, 0:n], AF.Exp, scale=-0.5),
                             reads=rq.r(0, 0, n), writes=rq.r(0, 0, n))
                        if kind == 'q':
                            P.op('dve', lambda e, n=n, lo=lo, hi=hi, zs=zs, rq=rq: e.scalar_tensor_tensor(
                                QN.v[:, 0, lo:hi], zs.v[:, 0, 0:n], qg, rq.v[:, 0, 0:n], ALU.mult, ALU.mult),
                                reads=zs.r(0, 0, n) + rq.r(0, 0, n) + VEC.r(), writes=QN.r(0, lo, hi))
                            if ti == 4:
                                P.op('dve', lambda e, n=n, zs=zs, rq=rq, c=c: e.scalar_tensor_tensor(
                                    QS.v[:, c, :], zs.v[:, 0, 0:n], qg, rq.v[:, 0, 0:n], ALU.mult, ALU.mult),
                                    reads=zs.r(0, 0, n) + rq.r(0, 0, n) + VEC.r(), writes=QS.r(c))
                        else:
                            kf = KF[zi]
                            P.op('dve', lambda e, n=n, zs=zs, rq=rq, kf=kf: e.scalar_tensor_tensor(
                                kf.v[:, 0, 0:n], zs.v[:, 0, 0:n], kg, rq.v[:, 0, 0:n], ALU.mult, ALU.mult),
                                reads=zs.r(0, 0, n) + rq.r(0, 0, n) + VEC.r(), writes=kf.r(0, 0, n))
                            P.op('pool', lambda e, n=n, lo=lo, hi=hi, kf=kf: e.tensor_copy(KN.v[:, 0, lo:hi], kf.v[:, 0, 0:n]),
                                 reads=kf.r(0, 0, n), writes=KN.r(0, lo, hi))
                            P.op('sp', lambda e, n=n, lo=lo, hi=hi, kf=kf, c=c: e.dma_start(
                                out=nkT[e_, :, c, lo:hi], in_=kf.v[:, 0, 0:n]), reads=kf.r(0, 0, n), chan=('ok', zi))
                            if ti == 4:
                                P.op('pool', lambda e, n=n, kf=kf, c=c: e.tensor_copy(KS.v[:, c, :], kf.v[:, 0, 0:n]),
                                     reads=kf.r(0, 0, n), writes=KS.r(c))
                    else:
                        vf = KF[zi]
                        if ti == 4:
                            P.op('pool', lambda e, n=n, vf=vf, c=c: e.tensor_copy(VS.v[:, c, :], vf.v[:, 0, 0:n]),
                                 reads=vf.r(0, 0, n), writes=VS.r(c))
                        else:
                            for bi in range(4):
                                kb = ti * 4 + bi
                                tb = 6 + tc_[0] % 2
                                tc_[0] += 1
                                P.op('pe', lambda e, bi=bi, tb=tb, vf=vf: e.transpose(
                                    bk(tb, 0, 128), vf.v[:, 0, bi * 128:(bi + 1) * 128], ident),
                                    reads=vf.r(0, bi * 128, (bi + 1) * 128) + CF.r(0), writes=PS(tb))
                                P.op('dve', lambda e, kb=kb, tb=tb: e.tensor_copy(
                                    VA.v[:, kb, :].rearrange("p (h x) -> p h x", x=65)[:, :, 0:64],
                                    bk(tb, 0, 128).rearrange("p (h x) -> p h x", x=64)),
                                    reads=PS(tb), writes=VA.r(kb))

                st1(0)
                for t in range(len(tl)):
                    if t + 1 < len(tl):
                        st1(t + 1)
                    st2(t)
                steps = [(qb, hh, kb) for qb in range(8) for hh in range(2) for kb in range(2 * qb + 2)]
                nst = len(steps)
                first_flag = {}

                def emit_S(i):
                    qb, hh, kb = steps[i]
                    q0, k0 = qb * 256, kb * 128
                    d = q0 - k0
                    col_lo = max(0, -d)
                    ncol = 256 - col_lo
                    sb = 2 * (i % 3)
                    pt = PT[i % 3]
                    for sub in range(2):
                        j = 2 * hh + sub
                        P.op('pe', lambda e, j=j, sb=sb, sub=sub, k0=k0, q0=q0, col_lo=col_lo, ncol=ncol: e.matmul(
                            bk(sb + sub, 0, ncol), KN.v[32 * j:32 * j + 32, 0, k0:k0 + 128],
                            QN.v[32 * j:32 * j + 32, 0, q0 + col_lo:q0 + 256], start=True, stop=True,
                            tile_position=(32 * j, 0)),
                            reads=KN.r(0, k0, k0 + 128) + QN.r(0, q0 + col_lo, q0 + 256), writes=PS(sb + sub))
                    pspair = PSA[:, sb * 512:(sb + 2) * 512].rearrange("p (s n) -> p s n", n=512)[:, :, 0:ncol]
                    P.op('act', lambda e, pspair=pspair, pt=pt, ncol=ncol: e.activation(
                        pt.v[:, :, 0:ncol], pspair, AF.Exp, scale=SCALE),
                        reads=PS(sb, sb + 1), writes=pt.r(None, 0, ncol))
                    if d < 256:
                        w0 = col_lo + d + 128
                        P.op('dve', lambda e, pt=pt, hh=hh, w0=w0, ncol=ncol: e.tensor_tensor(
                            pt.v[:, :, 0:ncol], pt.v[:, :, 0:ncol],
                            ECB.v[:, hh:hh + 1, w0:w0 + ncol].to_broadcast([128, 2, ncol]), ALU.mult),
                            reads=pt.r(None, 0, ncol) + ECB.r(hh), writes=pt.r(None, 0, ncol))

                def emit_PV(i):
                    qb, hh, kb = steps[i]
                    q0, k0 = qb * 256, kb * 128
                    col_lo = max(0, k0 - q0)
                    pt = PT[i % 3]
                    ob = 6 + hh
                    for sub in range(2):
                        for qs in range(2):
                            if 128 * qs < col_lo:
                                continue
                            c0 = 128 * qs - col_lo
                            last_kb = 2 * qb + qs
                            st = not first_flag.get((qb, hh), False)
                            first_flag[(qb, hh)] = True
                            gi_ = sub * 2 + qs
                            P.op('pe', lambda e, sub=sub, gi_=gi_, c0=c0, pt=pt, kb=kb, hh=hh, ob=ob, st=st, last_kb=last_kb: e.matmul(
                                bk(ob, gi_ * 65, (gi_ + 1) * 65), pt.v[:, sub, c0:c0 + 128],
                                VA.v[:, kb, hh * 65:(hh + 1) * 65], start=st, stop=(kb == last_kb),
                                skip_group_check=True),
                                reads=pt.r(sub, c0, c0 + 128) + VA.r(kb), writes=PS(ob))

                def norm_A(qb):
                    osv = OS.v[:, :, :].rearrange("p a (q x) -> p a q x", x=65)
                    P.op('dve', lambda e: e.tensor_copy(
                        OS.v.rearrange("p (h s) x -> p h (s x)", s=2),
                        PSA[:, 6 * 512:8 * 512].rearrange("p (a n) -> p a n", n=512)[:, :, 0:260]),
                        reads=PS(6, 7), writes=OS.r())
                    P.op('dve', lambda e: e.reciprocal(RR.v[:, 0, 0:8].rearrange("p (a q) -> p a q", q=2), osv[:, :, :, 64]),
                         reads=OS.r(), writes=RR.r())
                    P.op('dve', lambda e: e.tensor_tensor(
                        osv[:, :, :, 0:64], osv[:, :, :, 0:64],
                        RR.v[:, 0, 0:8].rearrange("p (a q) -> p a q", q=2).unsqueeze(3).to_broadcast([128, 4, 2, 64]), ALU.mult),
                        reads=OS.r() + RR.r(), writes=OS.r())
                    os5 = OS.v[:, :, :].rearrange("p (h s) (q x) -> p h s q x", s=2, x=65)
                    a1v = A1.v[:, :, :].rearrange("p h (q x) -> p h q x", x=64)
                    for h2 in range(2):
                        P.op('dve', lambda e, h2=h2: e.scalar_tensor_tensor(
                            a1v[:, h2], os5[:, h2, 1, :, 0:64], nlam, os5[:, h2, 0, :, 0:64], ALU.mult, ALU.add),
                            reads=OS.r() + LAM.r(e_), writes=A1.r(h2))
                    sqv = OS.v[:, 0:2, 0:128].rearrange("p h (q x) -> p h q x", x=64)
                    P.op('dve', lambda e: e.tensor_tensor(sqv, a1v, a1v, ALU.mult), reads=A1.r(), writes=OS.r())
                    P.op('dve', lambda e: e.tensor_reduce(RR.v[:, 0, 8:12].rearrange("p (h q) -> p h q", q=2), sqv, AX.X, ALU.add),
                         reads=OS.r(), writes=RR.r())

                def norm_B(qb):
                    atok = ATOK[qb % 2]
                    a1v = A1.v[:, :, :].rearrange("p h (q x) -> p h q x", x=64)
                    P.op('act', lambda e: e.activation(RR.v[:, 0, 8:12], RR.v[:, 0, 8:12], AF.Ln, bias=epsc, scale=1.0 / 64),
                         reads=RR.r() + EPSC.r(), writes=RR.r())
                    P.op('act', lambda e: e.activation(RR.v[:, 0, 8:12], RR.v[:, 0, 8:12], AF.Exp, scale=-0.5),
                         reads=RR.r(), writes=RR.r())
                    P.op('dve', lambda e: e.tensor_tensor(
                        a1v, a1v, RR.v[:, 0, 8:12].rearrange("p (h q) -> p h q", q=2).unsqueeze(3).to_broadcast([128, 2, 2, 64]),
                        ALU.mult), reads=A1.r() + RR.r(), writes=A1.r())
                    av = atok.v[:, 0, :].rearrange("p (q h x) -> p h q x", q=2, h=2)
                    P.op('dve', lambda e, av=av: e.tensor_tensor(
                        av, a1v, SGR.v[:, e_:e_ + 1, :].unsqueeze(1).to_broadcast([128, 2, 2, 64]), ALU.mult),
                        reads=A1.r() + SGR.r(e_), writes=atok.r())

                def trans(qb):
                    atok = ATOK[qb % 2]
                    q0 = qb * 256
                    for qs in range(2):
                        tb = 2 * (qb % 3) + qs
                        P.op('pe', lambda e, qs=qs, tb=tb, atok=atok: e.transpose(
                            bk(tb, 0, 128), atok.v[:, 0, qs * 128:(qs + 1) * 128], ident),
                            reads=atok.r() + CF.r(0), writes=PS(tb))
                        P.op('act', lambda e, qs=qs, tb=tb, c=c, q0=q0: e.activation(
                            ATT.v[:, c, q0 + 128 * qs:q0 + 128 * qs + 128], bk(tb, 0, 128), AF.Copy),
                            reads=PS(tb), writes=ATT.r(c, q0 + 128 * qs, q0 + 128 * qs + 128))

                deferred = []
                emit_S(0)
                emit_S(1)
                for i in range(nst):
                    if i + 2 < nst:
                        emit_S(i + 2)
                    emit_PV(i)
                    qb, hh, kb = steps[i]
                    if hh == 1 and kb == 2 * qb + 1:
                        norm_A(qb)
                        deferred.append((i + 2, norm_B, qb))
                        deferred.append((i + 5, trans, qb))
                    for item in list(deferred):
                        if item[0] <= i:
                            item[1](item[2])
                            deferred.remove(item)
                for item in sorted(deferred, key=lambda t: t[0]):
                    item[1](item[2])

            Dd = Alloc(big, LIMIT)
            Dd.cur = S0
            KP2 = [Dd.ten(TS, 512, BF16) for _ in range(3)]
            VP2 = [Dd.ten(TS, 512, BF16) for _ in range(3)]
            PRD = Dd.ten(TS, 512, BF16)
            SS2 = Dd.ten(TS, 16, F32)
            PD2 = [Dd.ten(TS, 48, BF16) for _ in range(3)]
            KP1 = Dd.at(KP2[0].off, 1, 512, BF16)
            VP1 = Dd.at(VP2[0].off, 1, 512, BF16)
            SS1 = Dd.ten(2, 16, F32)
            PD1 = Dd.ten(2, 16, BF16)
            DG = Dd.ten(4, 128, F32)
            DGB = Dd.ten(4, 128, F32)
            QB2 = Dd.ten(1, 512, BF16)
            QB = Dd.ten(1, 512, BF16)
            KSF = Dd.ten(1, 512, BF16)
            VSF = Dd.ten(1, 512, BF16)
            OM = Dd.at(DGB.off, 1, 512, F32)
            R1 = Dd.ten(1, 64, F32)
            AS = Dd.ten(1, 64, F32)
            AS2 = Dd.ten(1, 128, F32)
            RD = Dd.ten(1, 8, F32)
            assert Dd.cur <= LIMIT, (Dd.cur, LIMIT)
            ck1, cv1 = cache_k[e_], cache_v[e_]
            ck2 = cache_k[e_].rearrange("(r t) f -> r (t f)", t=TS)
            cv2 = cache_v[e_].rearrange("(r t) f -> r (t f)", t=TS)
            negm = c2v('negm', 0, 128)
            LA, LB = CF.v[:, 5, :], CF.v[:, 6, :]
            gctr = [0]

            def diag(dst, src, s):
                P.op('dve', lambda e: e.tensor_tensor(
                    dst.v, CF.v[:, 0:1, :].to_broadcast([128, 4, 128]),
                    src.v[:, :, s:s + 1].to_broadcast([128, 4, 128]), ALU.mult),
                    reads=CF.r(0) + src.r(), writes=dst.r())

            for pd_ in PD2:
                P.op('pool', lambda e, pd_=pd_: e.memset(pd_.v, 0.0), writes=pd_.r())
            for pair in range(2):
                sA, sB = 2 * pair, 2 * pair + 1
                diag(DG, QS, sA)
                diag(DGB, QS, sB)
                P.op('pe', lambda e: e.matmul(bk(0), LA, DG.v.rearrange("p a b -> p (a b)"), start=True, stop=False),
                     reads=DG.r() + CF.r(5), writes=PS(0))
                P.op('pe', lambda e: e.matmul(bk(0), LB, DGB.v.rearrange("p a b -> p (a b)"), start=False, stop=True),
                     reads=DGB.r() + CF.r(6), writes=PS(0))
                P.op('act', lambda e: e.activation(QB2.v[:, 0, :], bk(0), AF.Copy), reads=PS(0), writes=QB2.r())
                for g in range(NSTEP):
                    st_ = gctr[0] % 3
                    gctr[0] += 1
                    kp, vp, pd = KP2[st_], VP2[st_], PD2[st_]
                    P.op('pool', lambda e, g=g, kp=kp, pair=pair: e.indirect_dma_start(
                        out=kp.v.rearrange("p a b -> p (a b)"), out_offset=None, in_=ck2,
                        in_offset=bass.IndirectOffsetOnAxis(ap=IDX2.v[:, pair, g:g + 1], axis=0)),
                        reads=IDX2.r(pair), writes=kp.r(), chan=('kp', st_))
                    P.op('pool', lambda e, g=g, vp=vp, pair=pair: e.indirect_dma_start(
                        out=vp.v.rearrange("p a b -> p (a b)"), out_offset=None, in_=cv2,
                        in_offset=bass.IndirectOffsetOnAxis(ap=IDX2.v[:, pair, g:g + 1], axis=0)),
                        reads=IDX2.r(pair), writes=vp.r(), chan=('vp', st_))
                    P.op('dve', lambda e, kp=kp: e.tensor_tensor(
                        PRD.v, kp.v, QB2.v.to_broadcast([128, TS, 512]), ALU.mult),
                        reads=kp.r() + QB2.r(), writes=PRD.r())
                    P.op('dve', lambda e: e.tensor_reduce(
                        SS2.v.rearrange("p a b -> p (a b)"), PRD.v.rearrange("p a (h x) -> p (a h) x", x=32), AX.X, ALU.add),
                        reads=PRD.r(), writes=SS2.r())
                    for half in range(2):
                        p0, p1 = 64 * half, 64 * half + 64
                        P.op('act', lambda e, pd=pd, p0=p0, p1=p1, half=half: e.activation(
                            pd.v[p0:p1, :, 32 * half:32 * half + 16], SS2.v[p0:p1, :, :], AF.Exp,
                            bias=c2v('negm', p0, p1), scale=SCALE),
                            reads=SS2.r() + C2.r(), writes=pd.r())
                    for t in range(TS):
                        first = (g == 0 and t == 0)
                        P.op('pe', lambda e, t=t, pd=pd, vp=vp, first=first: e.matmul(
                            bk(3)[0:48, :], pd.v[:, t, :], vp.v[:, t, :], start=first, stop=False),
                            reads=pd.r(t) + vp.r(t), writes=PS(3))
                        P.op('pe', lambda e, t=t, pd=pd, first=first: e.matmul(
                            bk(5, 0, 1)[0:48, :], pd.v[:, t, :], ones_b[:, 0:1], start=first, stop=False),
                            reads=pd.r(t) + CB.r(4), writes=PS(5))
                for half, s in enumerate((sA, sB)):
                    ob_, db_ = 3, 5
                    r0, r1 = 32 * half, 32 * half + 16
                    for (src, dst, b) in ((QS, QB, 0), (KS, KSF, 1), (VS, VSF, 2)):
                        diag(DG, src, s)
                        P.op('pe', lambda e, b=b: e.matmul(bk(b), ones_f, DG.v.rearrange("p a b -> p (a b)"), start=True, stop=True),
                             reads=DG.r() + CF.r(4), writes=PS(b))
                        P.op('act', lambda e, dst=dst, b=b: e.activation(dst.v[:, 0, :], bk(b), AF.Copy), reads=PS(b), writes=dst.r())
                    P.op('pool', lambda e, s=s: e.indirect_dma_start(
                        out=KP1.v[:, 0, :], out_offset=None, in_=ck1,
                        in_offset=bass.IndirectOffsetOnAxis(ap=IDX63.v[:, 0, s:s + 1], axis=0)),
                        reads=IDX63.r(), writes=KP1.r(), chan='kp1')
                    P.op('pool', lambda e, s=s: e.indirect_dma_start(
                        out=VP1.v[:, 0, :], out_offset=None, in_=cv1,
                        in_offset=bass.IndirectOffsetOnAxis(ap=IDX63.v[:, 0, s:s + 1], axis=0)),
                        reads=IDX63.r(), writes=VP1.r(), chan='vp1')
                    for k, ksrc in enumerate((KP1, KSF)):
                        P.op('dve', lambda e, ksrc=ksrc: e.tensor_tensor(PRD.v[:, 0, :], ksrc.v[:, 0, :], QB.v[:, 0, :], ALU.mult),
                             reads=ksrc.r() + QB.r(), writes=PRD.r(0))
                        P.op('dve', lambda e, k=k: e.tensor_reduce(
                            SS1.v[:, k, :], PRD.v[:, 0, :].rearrange("p (h x) -> p h x", x=32), AX.X, ALU.add),
                            reads=PRD.r(0), writes=SS1.r(k))
                    P.op('dve', lambda e: e.scalar_tensor_tensor(
                        SS1.v.rearrange("p k (h x) -> p k h x", x=2)[:, 0], SS1.v.rearrange("p k (h x) -> p k h x", x=2)[:, 0], SCALE,
                        BDEC.v[:, 0, :].unsqueeze(2).to_broadcast([128, 8, 2]), ALU.mult, ALU.add),
                        reads=SS1.r(0) + BDEC.r(0), writes=SS1.r(0))
                    P.op('dve', lambda e: e.scalar_tensor_tensor(
                        SS1.v.rearrange("p k (h x) -> p k h x", x=2)[:, 1], SS1.v.rearrange("p k (h x) -> p k h x", x=2)[:, 1], SCALE,
                        BDEC.v[:, 1, :].unsqueeze(2).to_broadcast([128, 8, 2]), ALU.mult, ALU.add),
                        reads=SS1.r(1) + BDEC.r(1), writes=SS1.r(1))
                    P.op('act', lambda e: e.activation(PD1.v, SS1.v, AF.Exp), reads=SS1.r(), writes=PD1.r())
                    for k, vsrc in enumerate((VP1, VSF)):
                        P.op('pe', lambda e, k=k, vsrc=vsrc, ob_=ob_, r0=r0, r1=r1: e.matmul(
                            bk(ob_)[r0:r1, :], PD1.v[:, k, :], vsrc.v[:, 0, :], start=False, stop=(k == 1),
                            tile_position=(0, r0), skip_group_check=True),
                            reads=PD1.r(k) + vsrc.r(), writes=PS(ob_))
                        P.op('pe', lambda e, k=k, db_=db_, r0=r0, r1=r1: e.matmul(
                            bk(db_, 0, 1)[r0:r1, :], PD1.v[:, k, :], ones_b[:, 0:1], start=False, stop=(k == 1),
                            tile_position=(0, r0), skip_group_check=True),
                            reads=PD1.r(k) + CB.r(4), writes=PS(db_))
                    P.op('dve', lambda e, db_=db_, r0=r0, r1=r1: e.reciprocal(RD.v[r0:r1, 0, 0:1], bk(db_, 0, 1)[r0:r1, :]), reads=PS(db_), writes=RD.r())
                    P.op('dve', lambda e, r0=r0, r1=r1: e.tensor_tensor(RD.v[r0:r1, 0, 0:1], RD.v[r0:r1, 0, 0:1], LAMC.v[r0:r1, e_, :], ALU.mult),
                         reads=RD.r() + LAMC.r(e_), writes=RD.r())
                    P.op('dve', lambda e, ob_=ob_, r0=r0, r1=r1: e.tensor_tensor(OM.v[r0:r1, 0, :], bk(ob_)[r0:r1, :], c2v('maskd', r0, r1), ALU.mult),
                         reads=PS(ob_) + C2.r(), writes=OM.r())
                    P.op('dve', lambda e, r0=r0, r1=r1: e.tensor_reduce(
                        R1.v[r0:r1, 0, :], OM.v[r0:r1, 0, :].rearrange("p (h x) -> p x h", x=64), AX.X, ALU.add),
                        reads=OM.r(), writes=R1.r())
                    P.op('dve', lambda e, r0=r0, r1=r1: e.tensor_scalar(R1.v[r0:r1, 0, :], R1.v[r0:r1, 0, :], RD.v[r0:r1, 0, 0:1], None, ALU.mult),
                         reads=R1.r() + RD.r(), writes=R1.r())
                    P.op('pe', lambda e, r0=r0, r1=r1: e.matmul(bk(0, 0, 64)[0:8, :], c2v('pair', r0, r1), R1.v[r0:r1, 0, :], start=True, stop=True),
                         reads=R1.r() + C2.r(), writes=PS(0))
                    P.op('dve', lambda e: e.tensor_copy(AS.v[0:8, 0, :], bk(0, 0, 64)[0:8, :]), reads=PS(0), writes=AS.r())
                    P.op('dve', lambda e: e.tensor_tensor(R1.v[0:8, 0, :], AS.v[0:8, 0, :], AS.v[0:8, 0, :], ALU.mult),
                         reads=AS.r(), writes=R1.r())
                    P.op('dve', lambda e: e.tensor_reduce(RD.v[0:8, 0, 1:2], R1.v[0:8, 0, :], AX.X, ALU.add),
                         reads=R1.r(), writes=RD.r())
                    P.op('act', lambda e: e.activation(RD.v[0:8, 0, 1:2], RD.v[0:8, 0, 1:2], AF.Ln, bias=EPSC.v[0:8, 0, 0:1], scale=1.0 / 64),
                         reads=RD.r() + EPSC.r(), writes=RD.r())
                    P.op('act', lambda e: e.activation(RD.v[0:8, 0, 1:2], RD.v[0:8, 0, 1:2], AF.Exp, scale=-0.5),
                         reads=RD.r(), writes=RD.r())
                    P.op('dve', lambda e: e.tensor_scalar(AS.v[0:8, 0, :], AS.v[0:8, 0, :], RD.v[0:8, 0, 1:2], None,
                                                          ALU.mult), reads=AS.r() + RD.r(), writes=AS.r())
                    P.op('dve', lambda e: e.tensor_tensor(AS.v[0:8, 0, :], AS.v[0:8, 0, :], SGR.v[0:8, e_, :], ALU.mult),
                         reads=AS.r() + SGR.r(e_), writes=AS.r())
                    P.op('dve', lambda e: e.tensor_tensor(
                        AS2.v[0:8, 0, :].rearrange("p (r x) -> p r x", x=64), AS.v[0:8, 0:1, :].to_broadcast([8, 2, 64]),
                        c2v('maskp', 0, 8).rearrange("p (r x) -> p r x", x=64), ALU.mult),
                        reads=AS.r() + C2.r(), writes=AS2.r())
                    P.op('pe', lambda e: e.matmul(bk(1, 0, 4), AS2.v[0:8, 0, :], c2v('selc', 0, 8), start=True, stop=True),
                         reads=AS2.r() + C2.r(), writes=PS(1))
                    P.op('dve', lambda e, s=s: e.tensor_copy(ATT.v[:, :, NPROMPT + s], bk(1, 0, 4)),
                         reads=PS(1), writes=ATT.r(None, NPROMPT + s, NPROMPT + s + 1))

            Cc = Alloc(big, LIMIT)
            Cc.cur = S0
            UB = Cc.ten(1, 2 + NT, F32)
            GC = [Cc.ten(1, 512, F32) for _ in range(2)]
            CV = [Cc.ten(1, 512, F32) for _ in range(2)]
            assert Cc.cur <= LIMIT
            scbv = SCB.v[:, e_, :].rearrange("p (c k s) -> p c k s", c=4, k=2)
            cc_ = [0]
            for i in range(4):
                hgc = ws.get()
                hx = ws.get()
                hgb = ws.get()
                w0, w1, w2 = (vcol(('cbw', e_), k * 4 + i) for k in range(3))
                P.op('dve', lambda e: e.memset(UB.v[:, 0, 0:2], 0.0), writes=UB.r(0, 0, 2))
                for ti, (lo, hi) in enumerate(MT):
                    n = hi - lo
                    b1, b2, b3 = 0 + 3 * (cc_[0] % 2), 1 + 3 * (cc_[0] % 2), 2 + 3 * (cc_[0] % 2)
                    gc, cvt = GC[cc_[0] % 2], CV[cc_[0] % 2]
                    cc_[0] += 1
                    proj(hgc, XNS, b1, lo, hi)
                    proj(hx, XNS, b2, lo, hi)
                    proj(hgb, XNS, b3, lo, hi)
                    P.op('act', lambda e, n=n, b1=b1, gc=gc: e.activation(gc.v[:, 0, 0:n], bk(b1, 0, n), AF.Copy),
                         reads=PS(b1), writes=gc.r(0, 0, n))
                    P.op('dve', lambda e, n=n, b2=b2, gc=gc, lo=lo, hi=hi: e.tensor_tensor(
                        UB.v[:, 0, 2 + lo:2 + hi], gc.v[:, 0, 0:n], bk(b2, 0, n), ALU.mult),
                        reads=gc.r(0, 0, n) + PS(b2), writes=UB.r(0, 2 + lo, 2 + hi))
                    if ti < 4:
                        P.op('dve', lambda e, n=n, lo=lo, cvt=cvt, w0=w0: e.tensor_scalar(
                            cvt.v[:, 0, 0:n], UB.v[:, 0, lo:lo + n], w0, None, ALU.mult),
                            reads=UB.r(0, lo, lo + n) + VEC.r(), writes=cvt.r(0, 0, n))
                        P.op('dve', lambda e, n=n, lo=lo, cvt=cvt, w1=w1: e.scalar_tensor_tensor(
                            cvt.v[:, 0, 0:n], UB.v[:, 0, 1 + lo:1 + lo + n], w1, cvt.v[:, 0, 0:n], ALU.mult, ALU.add),
                            reads=UB.r(0, 1 + lo, 1 + lo + n) + VEC.r() + cvt.r(0, 0, n), writes=cvt.r(0, 0, n))
                        P.op('dve', lambda e, n=n, lo=lo, cvt=cvt, w2=w2: e.scalar_tensor_tensor(
                            cvt.v[:, 0, 0:n], UB.v[:, 0, 2 + lo:2 + lo + n], w2, cvt.v[:, 0, 0:n], ALU.mult, ALU.add),
                            reads=UB.r(0, 2 + lo, 2 + lo + n) + VEC.r() + cvt.r(0, 0, n), writes=cvt.r(0, 0, n))
                    else:
                        P.op('dve', lambda e, cvt=cvt, w0=w0, i=i: e.tensor_scalar(
                            cvt.v[:, 0, 0:4], scbv[:, i, 0, :], w0, None, ALU.mult),
                            reads=SCB.r(e_) + VEC.r(), writes=cvt.r(0, 0, 4))
                        P.op('dve', lambda e, cvt=cvt, w1=w1, i=i: e.scalar_tensor_tensor(
                            cvt.v[:, 0, 0:4], scbv[:, i, 1, :], w1, cvt.v[:, 0, 0:4], ALU.mult, ALU.add),
                            reads=SCB.r(e_) + VEC.r() + cvt.r(0, 0, 4), writes=cvt.r(0, 0, 4))
                        P.op('dve', lambda e, cvt=cvt, w2=w2: e.scalar_tensor_tensor(
                            cvt.v[:, 0, 0:4], UB.v[:, 0, 2 + NPROMPT:2 + NT], w2, cvt.v[:, 0, 0:4], ALU.mult, ALU.add),
                            reads=UB.r(0, 2 + NPROMPT, 2 + NT) + VEC.r() + cvt.r(0, 0, 4), writes=cvt.r(0, 0, 4))
                    P.op('dve', lambda e, n=n, b3=b3, cvt=cvt, lo=lo, hi=hi, i=i: e.tensor_tensor(
                        CATB.v[:, i, lo:hi], cvt.v[:, 0, 0:n], bk(b3, 0, n), ALU.mult),
                        reads=cvt.r(0, 0, n) + PS(b3), writes=CATB.r(i, lo, hi))
                P.op('sp', lambda e, i=i: e.dma_start(out=cbpT[e_, :, i, :], in_=UB.v[:, 0, NPROMPT:NPROMPT + 2]),
                     reads=UB.r(0, NPROMPT, NPROMPT + 2), chan='ocb')
                P.op('sp', lambda e, i=i: e.dma_start(out=cbs0T[e_, :, i, :], in_=scbv[:, i, 1, :]),
                     reads=SCB.r(e_), chan='ocb')
                P.op('sp', lambda e, i=i: e.dma_start(out=cbs1T[e_, :, i, :], in_=UB.v[:, 0, 2 + NPROMPT:2 + NT]),
                     reads=UB.r(0, 2 + NPROMPT, 2 + NT), chan='ocb')

            CATS = [(ATT, k) for k in range(4)] + [(CATB, k) for k in range(4)]
            oc2 = [0]
            for dch in range(8):
                ho = ws.get()
                for (lo, hi) in FT:
                    n = hi - lo
                    b = oc2[0] % 4
                    oc2[0] += 1
                    proj(ho, CATS, b, lo, hi)
                    P.op('dve', lambda e, dch=dch, lo=lo, hi=hi, n=n, b=b: e.tensor_tensor(
                        X.v[:, dch, lo:hi], bk(b, 0, n), X.v[:, dch, lo:hi], ALU.add),
                        reads=PS(b) + X.r(dch, lo, hi), writes=X.r(dch, lo, hi))

        def odd_mixer(l):
            o_ = l // 2
            B = Alloc(big, LIMIT)
            B.cur = R0
            CC = B.ten(8, NT, BF16)
            UO2 = [B.ten(1, 30 + NT, BF16) for _ in range(2)]
            UF2 = [B.ten(1, 34, F32) for _ in range(2)]
            UREP = B.ten(4, 30 + NT + 2, BF16)
            LW = B.ten(32, 32, BF16)
            WR = B.ten(4, 32, F32)
            SCS = B.ten(8, 120, F32)
            PRS = B.ten(4, 30, F32)
            CS4 = B.ten(1, 8, F32)
            B3 = Alloc(big, UREP.off + UREP.nbytes)
            B3.cur = UREP.off
            SQ = B3.ten(8, 342, BF16)
            MU = [B3.ten(1, 342, F32) for _ in range(2)]
            T1 = [B3.ten(1, 342, F32) for _ in range(2)]
            items = []
            for i in range(8):
                items += [(w_pw1[o_], i * 128), (w_pw1[o_], (8 + i) * 128)]
            for dch in range(8):
                items += [(w_pw2[o_], dch * 128)]
            ws = WStream(items, 4)
            ws.fill()
            rmsnorm(('nm', l), FT)
            P.op('sp', lambda e: e.dma_start(out=SCS.v, in_=scc[:, o_].rearrange("p c s k -> p c (s k)")),
                 writes=SCS.r(), chan='ld')
            for UO in UO2:
                P.op('pool', lambda e, UO=UO: e.memset(UO.v[:, 0, 0:30], 0.0), writes=UO.r(0, 0, 30))
            ccw0 = lay[('ccw', o_)][0]
            pc = [0]

            def stageA(i):
                UO, UF = UO2[i % 2], UF2[i % 2]
                ha = ws.get()
                hb = ws.get()
                ba_, bb_ = vcol(('bpw1', o_), i), vcol(('bpw1', o_), 8 + i)
                for ti, (lo, hi) in enumerate(FT):
                    n = hi - lo
                    b1 = 2 * (pc[0] % 2)
                    b2 = b1 + 1
                    sg = SG[pc[0] % 3]
                    pc[0] += 1
                    proj(ha, XNS, b1, lo, hi)
                    proj(hb, XNS, b2, lo, hi)
                    P.op('act', lambda e, n=n, b2=b2, sg=sg, bb_=bb_: e.activation(sg.v[:, 0, 0:n], bk(b2, 0, n), AF.Sigmoid, bias=bb_),
                         reads=PS(b2) + VEC.r(), writes=sg.r())
                    P.op('dve', lambda e, n=n, b1=b1, sg=sg, lo=lo, hi=hi, ba_=ba_, UO=UO: e.scalar_tensor_tensor(
                        UO.v[:, 0, 30 + lo:30 + hi], bk(b1, 0, n), ba_, sg.v[:, 0, 0:n], ALU.add, ALU.mult),
                        reads=PS(b1) + sg.r() + VEC.r(), writes=UO.r(0, 30 + lo, 30 + hi))
                    if ti == 5:
                        P.op('dve', lambda e, n=n, b1=b1, sg=sg, ba_=ba_, UF=UF: e.scalar_tensor_tensor(
                            UF.v[:, 0, :], bk(b1, n - 34, n), ba_, sg.v[:, 0, n - 34:n], ALU.add, ALU.mult),
                            reads=PS(b1) + sg.r() + VEC.r(), writes=UF.r())

            def stageR(i):
                UO = UO2[i % 2]
                for r in range(4):
                    for cg in range(4):
                        P.op('sp', lambda e, r=r, cg=cg, UO=UO: e.dma_start(
                            out=UREP.v[32 * r:32 * r + 32, cg, 0:2076], in_=UO.v[32 * cg:32 * cg + 32, 0, r:r + 2076]),
                            reads=UO.r(0, r, r + 2076), writes=[('urep', cg, r)], chan=('ur', (r * 4 + cg) % 8))

            def stageB(i):
                UO, UF = UO2[i % 2], UF2[i % 2]
                wv = VEC.v[:, 0, ccw0 + i: ccw0 + 248: 8]
                P.op('sp', lambda e, i=i: e.dma_start(out=WR.v, in_=ccwrep[o_, i]), writes=WR.r(), chan='wr')
                for r in range(4):
                    P.op('dve', lambda e, r=r: e.tensor_tensor(
                        LW.v[32 * r:32 * r + 32, :, :],
                        CB.v[32 * r:32 * r + 32, 0:1, 32 * r:32 * r + 32].to_broadcast([32, 32, 32]),
                        WR.v[32 * r:32 * r + 32, :, r::4].rearrange("p c j -> p (c j)").unsqueeze(2).to_broadcast([32, 32, 32]),
                        ALU.mult), reads=CB.r(0) + WR.r(), writes=LW.r())
                cb_ = vcol(('ccb', o_), i)
                for blk in range(4):
                    t0 = blk * 512
                    b = 4 + (pc[0] % 2)
                    pc[0] += 1
                    for jg in range(8):
                        for cg in range(4):
                            P.op('pe', lambda e, jg=jg, cg=cg, t0=t0, b=b: e.matmul(
                                bk(b)[32 * cg:32 * cg + 32, :], LW.v[:, cg * 8 + jg, :],
                                UREP.v[:, cg, t0 + 4 * jg:t0 + 4 * jg + 512], start=(jg == 0), stop=(jg == 7),
                                tile_position=(0, 32 * cg), skip_group_check=True),
                                reads=LW.r(cg * 8 + jg) + [('urep', cg, r_) for r_ in range(4)], writes=PS(b))
                    P.op('act', lambda e, t0=t0, b=b, i=i, cb_=cb_: e.activation(
                        CC.v[:, i, t0:t0 + 512], bk(b), AF.Identity, bias=cb_),
                        reads=PS(b) + VEC.r(), writes=CC.r(i, t0, t0 + 512))
                P.op('dve', lambda e, i=i, wv=wv: e.tensor_tensor(
                    PRS.v, SCS.v[:, i, :].rearrange("p (s k) -> p s k", k=30),
                    wv[:, 0:30].unsqueeze(1).to_broadcast([128, 4, 30]), ALU.mult),
                    reads=SCS.r(i) + VEC.r(), writes=PRS.r())
                P.op('dve', lambda e: e.tensor_reduce(CS4.v[:, 0, 0:4], PRS.v, AX.X, ALU.add), reads=PRS.r(), writes=CS4.r())
                P.op('dve', lambda e, wv=wv, UF=UF: e.scalar_tensor_tensor(
                    CS4.v[:, 0, 0:4], UF.v[:, 0, 30:34], wv[:, 30:31], CS4.v[:, 0, 0:4], ALU.mult, ALU.add),
                    reads=UF.r() + VEC.r() + CS4.r(), writes=CS4.r())
                P.op('dve', lambda e, i=i, cb_=cb_: e.tensor_scalar(
                    CC.v[:, i, NPROMPT:NT], CS4.v[:, 0, 0:4], cb_, None, ALU.add),
                    reads=CS4.r() + VEC.r(), writes=CC.r(i, NPROMPT, NT))
                P.op('sp', lambda e, i=i, UF=UF: e.dma_start(out=ccpT[o_, :, i, :], in_=UF.v[:, 0, 0:30]), reads=UF.r(), chan='occ')
                P.op('sp', lambda e, i=i: e.dma_start(
                    out=ccsoT[o_, :, i, :, :], in_=SCS.v[:, i, :].rearrange("p (s k) -> p s k", k=30)[:, :, 1:30]),
                    reads=SCS.r(i), chan='occ')
                P.op('sp', lambda e, i=i, UF=UF: e.dma_start(out=ccsnT[o_, :, i, :], in_=UF.v[:, 0, 30:34]),
                     reads=UF.r(), chan='occ')

            stageA(0)
            stageR(0)
            for i in range(8):
                if i + 1 < 8:
                    stageA(i + 1)
                stageB(i)
                if i + 1 < 8:
                    stageR(i + 1)
            lc = [0]
            for (lo, hi) in FT:
                n = hi - lo
                bm, bv = 0 + 2 * (lc[0] % 2), 1 + 2 * (lc[0] % 2)
                mu, t1 = MU[lc[0] % 2], T1[lc[0] % 2]
                lc[0] += 1
                P.op('act', lambda e, lo=lo, hi=hi, n=n: e.activation(SQ.v[:, :, 0:n], CC.v[:, :, lo:hi], AF.Square),
                     reads=CC.r(None, lo, hi), writes=SQ.r(None, 0, n))
                for kc in range(8):
                    P.op('pe', lambda e, kc=kc, n=n, bm=bm, lo=lo, hi=hi: e.matmul(
                        bk(bm, 0, n), mean_b, CC.v[:, kc, lo:hi], start=(kc == 0), stop=(kc == 7)),
                        reads=CC.r(kc, lo, hi) + CB.r(1), writes=PS(bm))
                for kc in range(8):
                    P.op('pe', lambda e, kc=kc, n=n, bv=bv: e.matmul(
                        bk(bv, 0, n), mean_b, SQ.v[:, kc, 0:n], start=(kc == 0), stop=(kc == 7)),
                        reads=SQ.r(kc, 0, n) + CB.r(1), writes=PS(bv))
                P.op('act', lambda e, n=n, bm=bm, mu=mu: e.activation(mu.v[:, 0, 0:n], bk(bm, 0, n), AF.Copy),
                     reads=PS(bm), writes=mu.r())
                P.op('dve', lambda e, n=n, mu=mu, t1=t1: e.tensor_tensor(t1.v[:, 0, 0:n], mu.v[:, 0, 0:n], mu.v[:, 0, 0:n], ALU.mult),
                     reads=mu.r(), writes=t1.r())
                P.op('dve', lambda e, n=n, bv=bv, t1=t1: e.tensor_tensor(t1.v[:, 0, 0:n], bk(bv, 0, n), t1.v[:, 0, 0:n], ALU.subtract),
                     reads=PS(bv) + t1.r(), writes=t1.r())
                P.op('act', lambda e, n=n, t1=t1: e.activation(t1.v[:, 0, 0:n], t1.v[:, 0, 0:n], AF.Sqrt, bias=epsc),
                     reads=t1.r() + EPSC.r(), writes=t1.r())
                P.op('dve', lambda e, n=n, t1=t1: e.reciprocal(t1.v[:, 0, 0:n], t1.v[:, 0, 0:n]), reads=t1.r(), writes=t1.r())
                for c in range(8):
                    sg = SG[c % 3]
                    P.op('dve', lambda e, c=c, lo=lo, hi=hi, n=n, mu=mu, sg=sg: e.tensor_tensor(
                        sg.v[:, 0, 0:n], CC.v[:, c, lo:hi], mu.v[:, 0, 0:n], ALU.subtract),
                        reads=CC.r(c, lo, hi) + mu.r(), writes=sg.r())
                    P.op('dve', lambda e, n=n, t1=t1, sg=sg: e.tensor_tensor(
                        sg.v[:, 0, 0:n], sg.v[:, 0, 0:n], t1.v[:, 0, 0:n], ALU.mult),
                        reads=sg.r() + t1.r(), writes=sg.r())
                    P.op('act', lambda e, c=c, lo=lo, hi=hi, n=n, sg=sg: e.activation(
                        XN.v[:, c, lo:hi], sg.v[:, 0, 0:n], AF.Silu, bias=vcol(('lnb', o_), c), scale=vcol(('lng', o_), c)),
                        reads=sg.r() + VEC.r(), writes=XN.r(c, lo, hi))
            oc2 = [0]
            for dch in range(8):
                ho = ws.get()
                bo = vcol(('bpw2', o_), dch)
                for (lo, hi) in FT:
                    n = hi - lo
                    b = 4 + oc2[0] % 4
                    oc2[0] += 1
                    proj(ho, XNS, b, lo, hi)
                    P.op('dve', lambda e, dch=dch, lo=lo, hi=hi, n=n, b=b, bo=bo: e.scalar_tensor_tensor(
                        X.v[:, dch, lo:hi], bk(b, 0, n), bo, X.v[:, dch, lo:hi], ALU.add, ALU.add),
                        reads=PS(b) + X.r(dch, lo, hi) + VEC.r(), writes=X.r(dch, lo, hi))

        if n_even > 0:
            even_setup()
        for l in range(depth):
            rmsnorm(('n1', l), FT)
            ffn(0, l)
            if l % 2 == 0:
                even_mixer(l)
            else:
                odd_mixer(l)
            rmsnorm(('n2', l), FT)
            ffn(1, l)
        P.op('sp', lambda e: e.dma_start(out=yT, in_=X.v), reads=X.r(), chan='out')
        P.emit()
    return nc


NCORES = 8


def make_in_maps(inp, cores):
    consts, c2, _, dstat = static_tables()
    f32 = lambda a: np.ascontiguousarray(np.asarray(a, np.float32))
    vecsT = build_vecs(inp)
    relb = np.concatenate([f32(inp['rel_bias']), np.ones((1, 8), np.float32)], 0)
    lamv = np.stack([f32(inp['lambda_q1']), f32(inp['lambda_k1']), f32(inp['lambda_q2']), f32(inp['lambda_k2'])], 1)
    shared = dict(vecsT=vecsT, consts=consts, cst2=c2, dstat=dstat, relb=relb, lamv=f32(lamv), sgrow=f32(inp['subln_gain']),
                  )
    ckf = f32(inp['cache_k']).reshape(2, 2560 * 128, 512)
    cvf = f32(inp['cache_v']).reshape(2, 2560 * 128, 512)
    for i in range(2):
        shared['cache_k%d' % i] = ckf[i]
        shared['cache_v%d' % i] = cvf[i]
    for k in ["ffn1_w_gate", "ffn1_w_up", "ffn1_w_down", "ffn2_w_gate", "ffn2_w_up", "ffn2_w_down",
              "w_in_even", "w_out_even", "w_pw1", "w_pw2"]:
        shared[k] = f32(inp[k])
    cw = f32(inp['conv_c_w'])
    cwp = np.zeros((2, 32, 1024), np.float32)
    cwp[:, :31] = cw
    t_ = cwp.reshape(2, 32, 8, 4, 32).transpose(0, 2, 4, 3, 1)
    shared['ccwrep'] = np.ascontiguousarray(np.broadcast_to(t_[:, :, None], (2, 8, 4, 32, 4, 32)).reshape(2, 8, 128, 4, 32))
    maps = []
    for core in cores:
        xp = f32(inp['x_prompt'][core])
        xs = f32(inp['x_sample'][4 * core:4 * core + 4, 0])
        xall = np.concatenate([xp, xs], 0)
        m = dict(shared)
        m['xT'] = np.ascontiguousarray(xall.reshape(NT, 8, 128).transpose(2, 1, 0))
        sb = f32(inp['state_conv_b'][:, 4 * core:4 * core + 4])
        m['scb'] = np.ascontiguousarray(sb.reshape(2, 4, 2, 4, 128).transpose(4, 0, 3, 2, 1))
        sc = f32(inp['state_conv_c'][:, 4 * core:4 * core + 4])
        m['scc'] = np.ascontiguousarray(sc.reshape(2, 4, 30, 8, 128).transpose(4, 0, 3, 1, 2))
        m['ptab'] = np.ascontiguousarray(np.asarray(inp['page_table'], np.int32)[4 * core:4 * core + 4].reshape(1, 256))
        maps.append(m)
    return maps


def assemble(results, ncores):
    f = np.float32
    y_p = np.zeros((ncores, 2048, 1024), f)
    y_s = np.zeros((4 * ncores, 1, 1024), f)
    nk_p = np.zeros((2, ncores, 2048, 16, 32), f)
    nv_p = np.zeros((2, ncores, 2048, 8, 64), f)
    nk_s = np.zeros((2, 4 * ncores, 1, 16, 32), f)
    nv_s = np.zeros((2, 4 * ncores, 1, 8, 64), f)
    cb_p = np.zeros((2, ncores, 2, 512), f)
    cb_s = np.zeros((2, 4 * ncores, 2, 512), f)
    cc_p = np.zeros((2, ncores, 30, 1024), f)
    cc_s = np.zeros((2, 4 * ncores, 30, 1024), f)
    for ci, r in enumerate(results):
        y = r['yT'].transpose(2, 1, 0).reshape(NT, 1024)
        y_p[ci] = y[:2048]
        y_s[4 * ci:4 * ci + 4, 0] = y[2048:]
        for e in range(2):
            k = r['nkT'][e].transpose(2, 1, 0).reshape(NT, 512)
            v = r['nvT'][e].transpose(2, 1, 0).reshape(NT, 512)
            nk_p[e, ci] = k[:2048].reshape(2048, 16, 32)
            nv_p[e, ci] = v[:2048].reshape(2048, 8, 64)
            nk_s[e, 4 * ci:4 * ci + 4, 0] = k[2048:].reshape(4, 16, 32)
            nv_s[e, 4 * ci:4 * ci + 4, 0] = v[2048:].reshape(4, 8, 64)
            cb_p[e, ci] = r['cbpT'][e].transpose(2, 1, 0).reshape(2, 512)
            cb_s[e, 4 * ci:4 * ci + 4, 0] = r['cbs0T'][e].transpose(2, 1, 0).reshape(4, 512)
            cb_s[e, 4 * ci:4 * ci + 4, 1] = r['cbs1T'][e].transpose(2, 1, 0).reshape(4, 512)
            cc_p[e, ci] = r['ccpT'][e].transpose(2, 1, 0).reshape(30, 1024)
            cc_s[e, 4 * ci:4 * ci + 4, 0:29] = r['ccsoT'][e].transpose(2, 3, 1, 0).reshape(4, 29, 1024)
            cc_s[e, 4 * ci:4 * ci + 4, 29] = r['ccsnT'][e].transpose(2, 1, 0).reshape(4, 1024)
    return (y_p, y_s, nk_p, nv_p, nk_s, nv_s, cb_p, cb_s, cc_p, cc_s)


def kernel(**inputs):
    nc = build_nc(DEPTH)
    maps = make_in_maps(inputs, list(range(NCORES)))
    res = run_bass_kernel_spmd(nc, maps, core_ids=list(range(NCORES)))
    return assemble(res.results, NCORES)
```

```python
import bisect
import contextlib
import math
import numpy as np
import concourse.bass as bass
import concourse.mybir as mybir
from concourse.bass_utils import run_bass_kernel_spmd

F32 = mybir.dt.float32
BF16 = mybir.dt.bfloat16
I32 = mybir.dt.int32
U8 = mybir.dt.uint8
AF = mybir.ActivationFunctionType
ALU = mybir.AluOpType
AX = mybir.AxisListType

D = 1024
NPROMPT = 2048
NSAMP = 4
NT = NPROMPT + NSAMP
DEPTH = 4
DFF = 2816
NFF = DFF // 128
GFF = 11
EPS = 1e-6
FT = [(342 * i, 342 * (i + 1)) for i in range(6)]
MT = [(0, 512), (512, 1024), (1024, 1536), (1536, 2048), (2048, 2052)]


class IMap:
    def __init__(self):
        self.starts = [0]
        self.data = {0: [1 << 40, None, []]}

    def _split(self, x):
        i = bisect.bisect_right(self.starts, x) - 1
        s = self.starts[i]
        e, w, r = self.data[s]
        if s == x or x >= e:
            return
        self.data[s] = [x, w, list(r)]
        self.data[x] = [e, w, list(r)]
        bisect.insort(self.starts, x)

    def access(self, a, b, oid, is_write, deps):
        self._split(a)
        self._split(b)
        i = bisect.bisect_left(self.starts, a)
        n = len(self.starts)
        while i < n and self.starts[i] < b:
            seg = self.data[self.starts[i]]
            if is_write:
                if seg[1] is not None:
                    deps.add((seg[1], 'waw'))
                for r in seg[2]:
                    deps.add((r, 'war'))
                seg[1] = oid
                seg[2] = []
            else:
                if seg[1] is not None:
                    deps.add((seg[1], 'raw'))
                seg[2].append(oid)
            i += 1


class Prog:
    COMPUTE = ('pe', 'act', 'dve', 'pool')
    ENGS = ('pe', 'act', 'dve', 'pool', 'sp')

    def __init__(self, nc):
        self.nc = nc
        self.ops = []
        self.buf = {}
        self.imaps = {}
        self.chan_last = {}
        self.chan_cnt = {}

    def _acc(self, k, oid, is_write, deps):
        if isinstance(k, tuple) and k and k[0] == 'iv':
            _, space, a, b = k
            self.imaps.setdefault(space, IMap()).access(a, b, oid, is_write, deps)
            return
        st = self.buf.setdefault(k, [None, []])
        if is_write:
            if st[0] is not None:
                deps.add((st[0], 'waw'))
            for r in st[1]:
                deps.add((r, 'war'))
            self.buf[k] = [oid, []]
        else:
            if st[0] is not None:
                deps.add((st[0], 'raw'))
            st[1].append(oid)

    def op(self, eng, fn, reads=(), writes=(), chan=None):
        oid = len(self.ops)
        deps = set()
        is_dma = chan is not None
        for k in reads:
            self._acc(k, oid, False, deps)
        for k in writes:
            self._acc(k, oid, True, deps)
        if is_dma and chan in self.chan_last:
            deps.add((self.chan_last[chan], 'chan'))
        dma_idx = None
        if is_dma:
            self.chan_last[chan] = oid
            dma_idx = self.chan_cnt.get(chan, 0) + 1
            self.chan_cnt[chan] = dma_idx
        fdeps = set()
        for (p, kind) in deps:
            if p == oid:
                continue
            po = self.ops[p]
            if po['chan'] is None and not is_dma and po['eng'] == eng:
                if eng == 'pe' or kind != 'raw':
                    continue
            fdeps.add(p)
        self.ops.append(dict(eng=eng, fn=fn, deps=fdeps, chan=chan, dma_idx=dma_idx, sig=False, sig_idx=None))
        return oid

    def emit(self):
        nc = self.nc
        ops = self.ops
        for o in ops:
            for p in o['deps']:
                ops[p]['sig'] = True
        cnt = {e: 0 for e in self.COMPUTE}
        for o in ops:
            if o['chan'] is None and o['sig']:
                cnt[o['eng']] += 1
                o['sig_idx'] = cnt[o['eng']]
        chans = sorted(self.chan_cnt.keys(), key=str)
        with contextlib.ExitStack() as es:
            sem = {}
            for e in self.COMPUTE:
                sem[('e', e)] = es.enter_context(nc.semaphore('s_' + e))
            for ci, c in enumerate(chans):
                sem[('c', c)] = es.enter_context(nc.semaphore('c%d' % ci))
            block = es.enter_context(nc.Block())
            by_eng = {e: [] for e in self.ENGS}
            for i, o in enumerate(ops):
                by_eng[o['eng']].append(i)

            def run(engname, engobj):
                waited = {}
                for i in by_eng[engname]:
                    o = ops[i]
                    need = {}
                    for p in o['deps']:
                        po = ops[p]
                        if po['chan'] is None:
                            k = ('e', po['eng'])
                            v = po['sig_idx']
                        else:
                            k = ('c', po['chan'])
                            v = 16 * po['dma_idx']
                        if need.get(k, 0) < v:
                            need[k] = v
                    for k, v in need.items():
                        if waited.get(k, 0) >= v:
                            continue
                        engobj.wait_ge(sem[k], v)
                        waited[k] = v
                    ins = o['fn'](engobj)
                    if o['chan'] is not None:
                        ins.then_inc(sem[('c', o['chan'])], 16)
                    elif o['sig']:
                        ins.then_inc(sem[('e', engname)], 1)
                if engname == 'sp':
                    for c in chans:
                        engobj.wait_ge(sem[('c', c)], 16 * self.chan_cnt[c])

            block.tensor(lambda e: run('pe', e))
            block.scalar(lambda e: run('act', e))
            block.vector(lambda e: run('dve', e))
            block.gpsimd(lambda e: run('pool', e))
            block.sync(lambda e: run('sp', e))


class Ten:
    def __init__(self, big, off, n0, n1, dt, esz):
        assert off % 32 == 0
        self.off, self.n0, self.n1, self.esz = off, n0, n1, esz
        self.nbytes = n0 * n1 * esz
        self.v = big[:, off:off + self.nbytes].bitcast(dt).rearrange("p (a b) -> p a b", b=n1)

    def r(self, i=None, lo=0, hi=None, i1=None):
        if hi is None:
            hi = self.n1
        if i is None:
            i, i1 = 0, self.n0
        elif i1 is None:
            i1 = i + 1
        if lo == 0 and hi == self.n1:
            return [('iv', 'sb', self.off + i * self.n1 * self.esz, self.off + i1 * self.n1 * self.esz)]
        return [('iv', 'sb', self.off + (j * self.n1 + lo) * self.esz, self.off + (j * self.n1 + hi) * self.esz)
                for j in range(i, i1)]


class Alloc:
    def __init__(self, big, limit):
        self.big, self.limit, self.cur = big, limit, 0

    def ten(self, n0, n1, dt):
        esz = {F32: 4, BF16: 2, I32: 4}[dt]
        t = Ten(self.big, self.cur, n0, n1, dt, esz)
        self.cur += (t.nbytes + 31) // 32 * 32
        assert self.cur <= self.limit, (self.cur, self.limit)
        return t

    def at(self, off, n0, n1, dt):
        esz = {F32: 4, BF16: 2, I32: 4}[dt]
        t = Ten(self.big, off, n0, n1, dt, esz)
        assert off + t.nbytes <= self.limit
        return t


NEG = -30000.0
SCALE = 32 ** -0.5
NTAB = 639
TS = 4
NPAGES = 64


def t5_bucket_np(n):
    n = np.asarray(n, np.int64)
    nn = np.maximum(n, 0)
    nf = np.maximum(nn, 1).astype(np.float32)
    large = 16 + (np.log(nf / np.float32(16)) / np.float32(math.log(128 / 16)) * np.float32(16)).astype(np.int32)
    large = np.minimum(large, 31)
    return np.where(nn < 16, nn, large).astype(np.int64)


def static_tables():
    consts = np.zeros((128, 7, 128), np.float32)
    consts[:, 0] = np.eye(128)
    consts[:, 1] = 1.0 / 1024
    consts[:, 2] = np.kron(np.eye(4), np.ones((32, 32))) / 32
    consts[:, 3] = np.eye(128)[::-1]
    consts[:, 4] = 1.0
    consts[:, 5, 0:64] = 1.0
    consts[:, 6, 64:128] = 1.0
    lay = {}
    cur = 0

    def add(name, n):
        nonlocal cur
        lay[name] = (cur, n)
        cur += n
    add('iota', 1); add('sgn0', 1); add('sgn1', 1); add('maskd', 512); add('pair', 8); add('maskp', 128); add('selc', 4); add('iota32', 32); add('negm', 1)
    c2 = np.zeros((128, cur), np.float32)
    c2[:, lay['iota'][0]] = np.arange(128)
    c2[:, lay['iota32'][0]:lay['iota32'][0] + 32] = np.arange(32)[None, :]
    c2[63, lay['negm'][0]] = NEG
    c2[127, lay['negm'][0]] = NEG
    for base in (0, 32):
        for hs in range(16):
            c2[base + hs, lay['sgn0'][0]] = 1.0 if hs % 2 == 0 else 0.0
            c2[base + hs, lay['sgn1'][0]] = -1.0 if hs % 2 == 1 else 0.0
            h = hs // 2
            c2[base + hs, lay['maskd'][0] + h * 64: lay['maskd'][0] + (h + 1) * 64] = 1.0
            c2[base + hs, lay['pair'][0] + h] = 1.0
    for h in range(8):
        r = h % 2
        c2[h, lay['maskp'][0] + r * 64: lay['maskp'][0] + (r + 1) * 64] = 1.0
        c2[h, lay['selc'][0] + h // 2] = 1.0
    ds_ = np.zeros((33, NTAB + 256), np.float32)
    for i in range(NTAB):
        n = i - 255
        if n < 0:
            ds_[32, i] = NEG
        else:
            ds_[int(t5_bucket_np(n)), i] += 1.0
            ds_[31, i] -= 1.0
    for p in range(128):
        ds_[int(t5_bucket_np(128 - p)), NTAB + p] += 1.0
        ds_[31, NTAB + p] -= 1.0
    ds_[0, NTAB + 128] += 1.0
    ds_[31, NTAB + 128] -= 1.0
    ds_[32, NTAB + 129:NTAB + 256] = NEG
    return consts, c2, lay, ds_


def vec_layout():
    lay = {}
    cur = 0

    def add(name, n):
        nonlocal cur
        lay[name] = (cur, n)
        cur += n
    for l in range(DEPTH):
        add(('n1', l), 8)
        add(('nm', l), 8)
        add(('n2', l), 8)
    for o in range(2):
        add(('bpw1', o), 16)
        add(('ccb', o), 8)
        add(('lng', o), 8)
        add(('lnb', o), 8)
        add(('bpw2', o), 8)
        add(('ccw', o), 31 * 8)
    for e in range(2):
        add(('cbw', e), 3 * 4)
        add(('qg', e), 1)
        add(('kg', e), 1)
    return lay, cur


def build_vecs(inp):
    lay, ncol = vec_layout()
    V = np.zeros((128, ncol), np.float32)

    def put(name, vec):
        c0, n = lay[name]
        V[:, c0:c0 + n] = np.asarray(vec, np.float32).reshape(n, 128).T
    for l in range(DEPTH):
        put(('n1', l), inp['norm_ffn1'][l])
        put(('nm', l), inp['norm_mix'][l])
        put(('n2', l), inp['norm_ffn2'][l])
    for o in range(2):
        put(('bpw1', o), inp['b_pw1'][o])
        put(('ccb', o), inp['conv_c_b'][o])
        put(('lng', o), inp['ln_c_gain'][o])
        put(('lnb', o), inp['ln_c_bias'][o])
        put(('bpw2', o), inp['b_pw2'][o])
        put(('ccw', o), np.asarray(inp['conv_c_w'][o]).reshape(-1))
    for e in range(2):
        put(('cbw', e), np.asarray(inp['conv_b_w'][e]).reshape(-1))
        put(('qg', e), np.tile(np.asarray(inp['q_norm_gain'][e]), 4))
        put(('kg', e), np.tile(np.asarray(inp['k_norm_gain'][e]), 4))
    return V


def build_nc(depth=DEPTH):
    nc = bass.Bass("TRN2", target_bir_lowering=False)
    lay, ncol = vec_layout()
    _, _, lay2, _ = static_tables()
    nc2 = sum(v[1] for v in lay2.values())
    n_even = (depth + 1) // 2
    n_odd = depth // 2

    def din(name, shape, dtype=F32):
        return nc.dram_tensor(name, list(shape), dtype, kind="ExternalInput").ap()

    def dout(name, shape, dtype=F32):
        return nc.dram_tensor(name, list(shape), dtype, kind="ExternalOutput").ap()

    xT = din("xT", [128, 8, NT])
    vecsT = din("vecsT", [128, ncol])
    consts = din("consts", [128, 7, 128])
    cst2 = din("cst2", [128, nc2])
    dstat = din("dstat", [33, NTAB + 256])
    relb = din("relb", [33, 8])
    lamv = din("lamv", [2, 4, 32])
    sgrow = din("sgrow", [2, 64])
    scb = din("scb", [128, 2, 4, 2, 4])
    scc = din("scc", [128, 2, 8, 4, 30])
    ptab = din("ptab", [1, NSAMP * NPAGES], I32)
    ccwrep = din("ccwrep", [2, 8, 128, 4, 32])
    cache_k = [din("cache_k%d" % i, [2560 * 128, 512]) for i in range(2)]
    cache_v = [din("cache_v%d" % i, [2560 * 128, 512]) for i in range(2)]
    NSTEP = 128 // TS
    wg = [din("ffn1_w_gate", [DEPTH, D, DFF]), din("ffn2_w_gate", [DEPTH, D, DFF])]
    wu = [din("ffn1_w_up", [DEPTH, D, DFF]), din("ffn2_w_up", [DEPTH, D, DFF])]
    wd = [din("ffn1_w_down", [DEPTH, DFF, D]), din("ffn2_w_down", [DEPTH, DFF, D])]
    w_in = din("w_in_even", [2, D, 3072])
    w_out = din("w_out_even", [2, D, D])
    w_pw1 = din("w_pw1", [2, D, 2048])
    w_pw2 = din("w_pw2", [2, D, D])
    yT = dout("yT", [128, 8, NT])
    nkT = dout("nkT", [2, 128, 4, NT])
    nvT = dout("nvT", [2, 128, 4, NT])
    cbpT = dout("cbpT", [2, 128, 4, 2])
    cbs0T = dout("cbs0T", [2, 128, 4, 4])
    cbs1T = dout("cbs1T", [2, 128, 4, 4])
    ccpT = dout("ccpT", [2, 128, 8, 30])
    ccsoT = dout("ccsoT", [2, 128, 8, 4, 29])
    ccsnT = dout("ccsnT", [2, 128, 8, 4])
    tabd = nc.dram_tensor("tabd", [8, NTAB], F32)

    P = Prog(nc)
    with contextlib.ExitStack() as es:
        LIMIT = 212800
        big = es.enter_context(nc.sbuf_tensor("big", [128, LIMIT], U8))
        PSA = es.enter_context(nc.psum_tensor("psa", [128, 4096], F32))
        A = Alloc(big, LIMIT)
        X = A.ten(8, NT, F32)
        XN = A.ten(8, NT, BF16)
        VEC = A.ten(1, ncol, F32)
        CF = A.ten(7, 128, F32)
        CB = A.ten(5, 128, BF16)
        C2 = A.ten(1, nc2, F32)
        RSTD = A.ten(1, NT, F32)
        EPSC = A.ten(1, 8, F32)
        LAMIN = A.ten(2, 128, F32)
        LAM = A.ten(2, 8, F32)
        SGR = A.ten(2, 64, F32)
        BDEC = A.ten(2, 8, F32)
        LAMC = A.ten(2, 1, F32)
        SCB = A.ten(2, 32, F32)
        IDX63 = A.ten(1, 8, I32)
        QS = A.ten(4, 4, F32)
        KS = A.ten(4, 4, F32)
        VS = A.ten(4, 4, F32)
        IDX2 = A.ten(2, 32, I32)
        NHS = 8
        HS = [A.ten(8, 128, BF16) for _ in range(NHS)]
        SG = [A.ten(1, 342, F32) for _ in range(3)]
        R0 = A.cur
        WD = A.ten(GFF, 1024, BF16)
        H = A.ten(GFF, NT, BF16)
        assert A.cur <= LIMIT
        RSIZE = LIMIT - R0
        SQH = A.at(H.off, 8, 512, BF16)

        def bk(b, n0=0, n1=512):
            return PSA[:, b * 512 + n0: b * 512 + n1]

        def PS(*bs):
            return [('ps', b) for b in bs]

        def vcol(name, j=0):
            return VEC.v[:, 0, lay[name][0] + j: lay[name][0] + j + 1]

        def c2v(name, p0, p1):
            c0, n = lay2[name]
            return C2.v[p0:p1, 0, c0:c0 + n]

        ident = CF.v[:, 0, :]
        ones_f = CF.v[:, 4, :]
        Jf = CF.v[:, 3, :]
        mean_b = CB.v[:, 1, :]
        blk_b = CB.v[:, 2, :]
        ident_b = CB.v[:, 0, :]
        ones_b = CB.v[:, 4, :]

        P.op('sp', lambda e: e.dma_start(out=X.v, in_=xT), writes=X.r(), chan='ld0')
        P.op('sp', lambda e: e.dma_start(out=VEC.v[:, 0, :], in_=vecsT), writes=VEC.r(), chan='ld')
        P.op('sp', lambda e: e.dma_start(out=CF.v, in_=consts), writes=CF.r(), chan='ld')
        P.op('sp', lambda e: e.dma_start(out=C2.v[:, 0, :], in_=cst2), writes=C2.r(), chan='ld')
        P.op('dve', lambda e: e.tensor_copy(CB.v, CF.v[:, 0:5, :]), reads=CF.r(), writes=CB.r())
        P.op('dve', lambda e: e.memset(EPSC.v[:, 0, :], EPS), writes=EPSC.r())
        epsc = EPSC.v[:, 0, 0:1]

        hs_ctr = [0]

        def load_col(w2d, col0):
            i = hs_ctr[0] % NHS
            hs_ctr[0] += 1
            src = w2d.rearrange("(kc p) f -> p kc f", p=128)[:, :, col0:col0 + 128]
            t = HS[i]
            P.op('pool', lambda e: e.dma_start(out=t.v, in_=src), writes=t.r(), chan=('hs', i))
            return t

        class WStream:
            def __init__(self, items, depth):
                self.items, self.i, self.q, self.depth = list(items), 0, [], depth

            def fill(self):
                while self.i < len(self.items) and len(self.q) < self.depth:
                    self.q.append(load_col(*self.items[self.i]))
                    self.i += 1

            def get(self):
                self.fill()
                h = self.q.pop(0)
                self.fill()
                return h

        def proj(hs, srcs, b, lo, hi):
            n = hi - lo
            for kc in range(8):
                t, row = srcs[kc]
                P.op('pe', lambda e, kc=kc, t=t, row=row: e.matmul(
                    bk(b, 0, n), hs.v[:, kc, :], t.v[:, row, lo:hi], start=(kc == 0), stop=(kc == 7)),
                    reads=hs.r(kc) + t.r(row, lo, hi), writes=PS(b))

        XNS = [(XN, kc) for kc in range(8)]

        nrm_ctr = [0]

        def rmsnorm(gname, tiles):
            SQ = SQH
            for (lo, hi) in tiles:
                n = hi - lo
                b = 6 + nrm_ctr[0] % 2
                nrm_ctr[0] += 1
                P.op('act', lambda e, lo=lo, hi=hi, n=n: e.activation(SQ.v[:, :, 0:n], X.v[:, :, lo:hi], AF.Square),
                     reads=X.r(None, lo, hi), writes=SQ.r(None, 0, n))
                for kc in range(8):
                    P.op('pe', lambda e, kc=kc, n=n, b=b: e.matmul(bk(b, 0, n), mean_b, SQ.v[:, kc, 0:n],
                                                                  start=(kc == 0), stop=(kc == 7)),
                         reads=SQ.r(kc, 0, n) + CB.r(1), writes=PS(b))
                P.op('act', lambda e, lo=lo, hi=hi, n=n, b=b: e.activation(
                    RSTD.v[:, 0, lo:hi], bk(b, 0, n), AF.Sqrt, bias=epsc),
                    reads=PS(b) + EPSC.r(), writes=RSTD.r(0, lo, hi))
                P.op('dve', lambda e, lo=lo, hi=hi: e.reciprocal(RSTD.v[:, 0, lo:hi], RSTD.v[:, 0, lo:hi]),
                     reads=RSTD.r(0, lo, hi), writes=RSTD.r(0, lo, hi))
                for c in range(8):
                    P.op('dve', lambda e, c=c, lo=lo, hi=hi: e.scalar_tensor_tensor(
                        XN.v[:, c, lo:hi], X.v[:, c, lo:hi], vcol(gname, c), RSTD.v[:, 0, lo:hi],
                        ALU.mult, ALU.mult),
                        reads=X.r(c, lo, hi) + RSTD.r(0, lo, hi) + VEC.r(), writes=XN.r(c, lo, hi))

        def ffn(which, l):
            gu_ctr = 0
            sgc = 0
            dn_ctr = 0
            pend = []
            nxt = [0]

            def prefetch(upto):
                while nxt[0] < min(upto, NFF):
                    f = nxt[0]
                    pend.append((load_col(wg[which][l], f * 128), load_col(wu[which][l], f * 128)))
                    nxt[0] += 1
            for g in range(2):
                prefetch(g * GFF + 2)
                for j in range(GFF):
                    f = g * GFF + j
                    src = wd[which][l][f * 128:(f + 1) * 128, :]
                    P.op('pool', lambda e, j=j, src=src: e.dma_start(out=WD.v[:, j, :], in_=src),
                         writes=WD.r(j), chan=('wd', j % 4))
                for j in range(GFF):
                    f = g * GFF + j
                    prefetch(f + 3)
                    hg, hu = pend.pop(0)
                    for (lo, hi) in FT:
                        n = hi - lo
                        bg = (gu_ctr % 3) * 2
                        bu = bg + 1
                        gu_ctr += 1
                        proj(hg, XNS, bg, lo, hi)
                        proj(hu, XNS, bu, lo, hi)
                        sg = SG[sgc % 3]
                        sgc += 1
                        P.op('act', lambda e, n=n, bg=bg, sg=sg: e.activation(sg.v[:, 0, 0:n], bk(bg, 0, n), AF.Silu),
                             reads=PS(bg), writes=sg.r())
                        P.op('dve', lambda e, n=n, bu=bu, sg=sg, j=j, lo=lo, hi=hi: e.tensor_tensor(
                            H.v[:, j, lo:hi], sg.v[:, 0, 0:n], bk(bu, 0, n), ALU.mult),
                            reads=sg.r() + PS(bu), writes=H.r(j, lo, hi))
                for (lo, hi) in FT:
                    n = hi - lo
                    for dch in range(8):
                        b = 6 + (dn_ctr % 2)
                        dn_ctr += 1
                        for j in range(GFF):
                            P.op('pe', lambda e, j=j, dch=dch, lo=lo, hi=hi, n=n, b=b: e.matmul(
                                bk(b, 0, n), WD.v[:, j, dch * 128:(dch + 1) * 128], H.v[:, j, lo:hi],
                                start=(j == 0), stop=(j == GFF - 1)),
                                reads=WD.r(j) + H.r(j, lo, hi), writes=PS(b))
                        P.op('dve', lambda e, dch=dch, lo=lo, hi=hi, n=n, b=b: e.scalar_tensor_tensor(
                            X.v[:, dch, lo:hi], bk(b, 0, n), 0.5, X.v[:, dch, lo:hi], ALU.mult, ALU.add),
                            reads=PS(b) + X.r(dch, lo, hi), writes=X.r(dch, lo, hi))

        def even_setup():
            B0 = Alloc(big, LIMIT)
            B0.cur = R0
            RB = B0.ten(1, 8, F32)
            DS = B0.ten(1, NTAB + 256, F32)
            TB = B0.ten(1, NTAB, F32)
            PTI = B0.ten(1, NSAMP * NPAGES, I32)
            PTF = B0.ten(1, NSAMP * NPAGES, F32)
            PRD = B0.ten(2, 32, F32)
            P63 = B0.ten(1, 8, F32)
            PT2I = B0.ten(1, 8, I32)
            PT2F = B0.ten(1, 8, F32)
            P2F = B0.ten(1, 32, F32)
            P.op('sp', lambda e: e.dma_start(out=RB.v[0:33, 0, :], in_=relb), writes=RB.r(), chan='ld')
            P.op('sp', lambda e: e.dma_start(out=DS.v[0:33, 0, :], in_=dstat), writes=DS.r(), chan='ld')
            for (c0, c1) in ((0, 512), (512, NTAB)):
                P.op('pe', lambda e, c0=c0, c1=c1: e.matmul(bk(0, 0, c1 - c0)[0:8, :], RB.v[0:33, 0, :], DS.v[0:33, 0, c0:c1],
                                                           start=True, stop=True),
                     reads=RB.r() + DS.r(), writes=PS(0))
                P.op('dve', lambda e, c0=c0, c1=c1: e.tensor_copy(TB.v[0:8, 0, c0:c1], bk(0, 0, c1 - c0)[0:8, :]),
                     reads=PS(0), writes=TB.r(0, c0, c1))
            P.op('sp', lambda e: e.dma_start(out=tabd.ap(), in_=TB.v[0:8, 0, :]), reads=TB.r(), writes=['tabd'], chan='ld')
            for k in range(2):
                P.op('pe', lambda e, k=k: e.matmul(bk(1, 0, 8), DS.v[0:33, 0, NTAB + 128 * k: NTAB + 128 * (k + 1)],
                                                  RB.v[0:33, 0, :], start=True, stop=True),
                     reads=RB.r() + DS.r(), writes=PS(1))
                P.op('dve', lambda e, k=k: e.tensor_copy(BDEC.v[:, k, :], bk(1, 0, 8)), reads=PS(1), writes=BDEC.r(k))
            P.op('sp', lambda e: e.dma_start(
                out=LAMIN.v, in_=bass.AP(lamv.tensor, 0, [[0, 128], [128, 2], [1, 128]])), writes=LAMIN.r(), chan='ld')
            P.op('sp', lambda e: e.dma_start(
                out=SGR.v, in_=bass.AP(sgrow.tensor, 0, [[0, 128], [64, 2], [1, 64]])), writes=SGR.r(), chan='ld')
            P.op('sp', lambda e: e.dma_start(out=SCB.v, in_=scb.rearrange("p e c k s -> p e (c k s)")),
                 writes=SCB.r(), chan='ld')
            P.op('sp', lambda e: e.dma_start(
                out=PTI.v[:, 0, :], in_=bass.AP(ptab.tensor, 0, [[0, 128], [1, NSAMP * NPAGES]])),
                writes=PTI.r(), chan='ld')
            P.op('dve', lambda e: e.tensor_copy(PTF.v, PTI.v), reads=PTI.r(), writes=PTF.r())
            P.op('dve', lambda e: e.tensor_scalar(
                P63.v[:, 0, 0:4], PTF.v[:, 0, NPAGES - 1:NSAMP * NPAGES:NPAGES], 128.0, c2v('iota', 0, 128), ALU.mult, ALU.add),
                reads=PTF.r() + C2.r(), writes=P63.r())
            P.op('dve', lambda e: e.tensor_copy(IDX63.v[:, 0, 0:4], P63.v[:, 0, 0:4]), reads=P63.r(), writes=IDX63.r())
            for a in range(2):
                P.op('sp', lambda e, a=a: e.dma_start(out=PT2I.v[:, 0, a:a + 1], in_=bass.AP(ptab.tensor, a * 128, [[1, 128], [1, 1]])),
                     writes=PT2I.r(), chan='ld')
            P.op('dve', lambda e: e.tensor_copy(PT2F.v[:, 0, 0:2], PT2I.v[:, 0, 0:2]), reads=PT2I.r(), writes=PT2F.r())
            for a in range(2):
                P.op('dve', lambda e, a=a: e.scalar_tensor_tensor(
                    P2F.v[:, 0, :], c2v('iota32', 0, 128), 1.0 / NSTEP, PT2F.v[:, 0, a:a + 1].to_broadcast([128, 32]), ALU.mult, ALU.add),
                    reads=PT2F.r() + C2.r(), writes=P2F.r())
                P.op('dve', lambda e: e.tensor_scalar(P2F.v[:, 0, :], P2F.v[:, 0, :], float(NSTEP), None, ALU.mult),
                     reads=P2F.r(), writes=P2F.r())
                P.op('dve', lambda e, a=a: e.tensor_copy(IDX2.v[:, a, :], P2F.v[:, 0, :]), reads=P2F.r(), writes=IDX2.r(a))
            for e_ in range(n_even):
                lam_init = 0.8 - 0.6 * math.exp(-0.3 * (2 * e_))
                for k in range(2):
                    P.op('dve', lambda e, e_=e_, k=k: e.tensor_tensor(
                        PRD.v[:, k, :], LAMIN.v[:, e_, 64 * k:64 * k + 32], LAMIN.v[:, e_, 64 * k + 32:64 * k + 64],
                        ALU.mult), reads=LAMIN.r(), writes=PRD.r(k))
                P.op('dve', lambda e, e_=e_: e.tensor_reduce(LAM.v[:, e_, 2:4], PRD.v, AX.X, ALU.add),
                     reads=PRD.r(), writes=LAM.r(e_))
                P.op('act', lambda e, e_=e_: e.activation(LAM.v[:, e_, 2:4], LAM.v[:, e_, 2:4], AF.Exp),
                     reads=LAM.r(e_), writes=LAM.r(e_))
                P.op('dve', lambda e, e_=e_: e.tensor_tensor(LAM.v[:, e_, 0:1], LAM.v[:, e_, 2:3], LAM.v[:, e_, 3:4],
                                                            ALU.subtract), reads=LAM.r(e_), writes=LAM.r(e_))
                P.op('dve', lambda e, e_=e_, li=lam_init: e.tensor_scalar(
                    LAM.v[:, e_, 0:1], LAM.v[:, e_, 0:1], li, None, ALU.add), reads=LAM.r(e_), writes=LAM.r(e_))
                P.op('dve', lambda e, e_=e_: e.tensor_scalar(LAM.v[:, e_, 1:2], LAM.v[:, e_, 0:1], -1.0, None, ALU.mult),
                     reads=LAM.r(e_), writes=LAM.r(e_))
                P.op('dve', lambda e, e_=e_, li=lam_init: e.tensor_scalar(SGR.v[:, e_, :], SGR.v[:, e_, :], 1.0 - li, None, ALU.mult),
                     reads=SGR.r(e_), writes=SGR.r(e_))
                P.op('dve', lambda e, e_=e_: e.scalar_tensor_tensor(
                    LAMC.v[0:48, e_, :], c2v('sgn1', 0, 48), LAM.v[0:48, e_, 0:1], c2v('sgn0', 0, 48), ALU.mult, ALU.add),
                    reads=LAM.r(e_) + C2.r(), writes=LAMC.r(e_))

        def even_mixer(l):
            e_ = l // 2
            lam_init = 0.8 - 0.6 * math.exp(-0.3 * l)
            B = Alloc(big, LIMIT)
            B.cur = R0
            ATT = B.ten(4, NT, BF16)
            CATB = B.ten(4, NT, BF16)
            S0 = B.cur
            QN = B.ten(1, NT, BF16)
            KN = B.ten(1, NT, BF16)
            VA = B.ten(16, 130, BF16)
            ECB = B.ten(2, 512, BF16)
            ZS = [B.ten(1, 512, F32) for _ in range(2)]
            SQB = [B.ten(1, 512, BF16) for _ in range(2)]
            RQ = [B.ten(1, 512, F32) for _ in range(2)]
            KF = [B.ten(1, 512, F32) for _ in range(2)]
            PT = [B.ten(2, 256, BF16) for _ in range(3)]
            E2 = B.at(KF[0].off, 2, 512, F32)
            assert KF[1].off == KF[0].off + 2048
            B2 = Alloc(big, SG[2].off + 1376)
            B2.cur = SG[0].off
            ATOK = [B2.ten(1, 256, F32) for _ in range(2)]
            OS = B2.ten(4, 130, F32)
            A1 = B.ten(2, 128, F32)
            RR = B.ten(1, 16, F32)
            S1 = B.cur
            w2d = w_in[e_]
            qg, kg = vcol(('qg', e_)), vcol(('kg', e_))
            nlam = LAM.v[:, e_, 1:2]

            items = []
            for c in range(4):
                items += [(w2d, c * 128), (w2d, (4 + c) * 128), (w2d, (8 + c) * 128)]
            for i in range(4):
                items += [(w2d, (16 + i) * 128), (w2d, (20 + i) * 128), (w2d, (12 + i) * 128)]
            for dch in range(8):
                items += [(w_out[e_], dch * 128)]
            ws = WStream(items, 4)
            ws.fill()
            rmsnorm(('nm', l), MT)
            P.op('pool', lambda e: e.memset(VA.v, 1.0), writes=VA.r())

            zc = [0]
            tc_ = [0]
            for c in range(4):
                hq = ws.get()
                hk = ws.get()
                hv = ws.get()
                srcE = bass.AP(tabd, 2 * c * NTAB, [[1, 128], [NTAB, 2], [1, 512]])
                P.op('sp', lambda e, srcE=srcE: e.dma_start(out=E2.v, in_=srcE), reads=['tabd'], writes=E2.r(), chan='e2')
                for hh in range(2):
                    P.op('pe', lambda e, hh=hh: e.matmul(bk(7), Jf, E2.v[:, hh, :], start=True, stop=True),
                         reads=E2.r(hh) + CF.r(3), writes=PS(7))
                    P.op('act', lambda e, hh=hh: e.activation(ECB.v[:, hh, :], bk(7), AF.Exp), reads=PS(7), writes=ECB.r(hh))
                tl = [(kind, hs, ti, lo, hi) for kind, hs in (('q', hq), ('k', hk), ('v', hv)) for ti, (lo, hi) in enumerate(MT)]
                tinfo = {}

                def st1(t):
                    kind, hs, ti, lo, hi = tl[t]
                    n = hi - lo
                    b = zc[0] % 4
                    zi = zc[0] % 2
                    zc[0] += 1
                    tinfo[t] = (b, zi)
                    proj(hs, XNS, b, lo, hi)
                    if kind in 'qk':
                        zs, sqb = ZS[zi], SQB[zi]
                        P.op('act', lambda e, n=n, b=b, zs=zs: e.activation(zs.v[:, 0, 0:n], bk(b, 0, n), AF.Copy),
                             reads=PS(b), writes=zs.r(0, 0, n))
                        P.op('act', lambda e, n=n, b=b, sqb=sqb: e.activation(sqb.v[:, 0, 0:n], bk(b, 0, n), AF.Square),
                             reads=PS(b), writes=sqb.r(0, 0, n))
                    else:
                        vf = KF[zi]
                        P.op('act', lambda e, n=n, b=b, vf=vf: e.activation(vf.v[:, 0, 0:n], bk(b, 0, n), AF.Copy),
                             reads=PS(b), writes=vf.r(0, 0, n))
                        P.op('sp', lambda e, n=n, lo=lo, hi=hi, vf=vf, c=c: e.dma_start(
                            out=nvT[e_, :, c, lo:hi], in_=vf.v[:, 0, 0:n]), reads=vf.r(0, 0, n), chan=('ok', zi))

                def st2(t):
                    kind, hs, ti, lo, hi = tl[t]
                    n = hi - lo
                    b, zi = tinfo[t]
                    if kind in 'qk':
                        zs, sqb, rq = ZS[zi], SQB[zi], RQ[zi]
                        b2 = 4 + zi
                        P.op('pe', lambda e, n=n, b2=b2, sqb=sqb: e.matmul(bk(b2, 0, n), blk_b, sqb.v[:, 0, 0:n],
                                                                          start=True, stop=True),
                             reads=sqb.r(0, 0, n) + CB.r(2), writes=PS(b2))
                        P.op('act', lambda e, n=n, b2=b2, rq=rq: e.activation(rq.v[:, 0, 0:n], bk(b2, 0, n), AF.Ln, bias=epsc),
                             reads=PS(b2) + EPSC.r(), writes=rq.r(0, 0, n))
                        P.op('act', lambda e, n=n, rq=rq: e.activation(rq.v[:, 0, 0:n], rq.v[:, 0, 0:n], AF.Exp, scale=-0.5),
                             reads=rq.r(0, 0, n), writes=rq.r(0, 0, n))
                        if kind == 'q':
                            P.op('dve', lambda e, n=n, lo=lo, hi=hi, zs=zs, rq=rq: e.scalar_tensor_tensor(
                                QN.v[:, 0, lo:hi], zs.v[:, 0, 0:n], qg, rq.v[:, 0, 0:n], ALU.mult, ALU.mult),
                                reads=zs.r(0, 0, n) + rq.r(0, 0, n) + VEC.r(), writes=QN.r(0, lo, hi))
                            if ti == 4:
                                P.op('dve', lambda e, n=n, zs=zs, rq=rq, c=c: e.scalar_tensor_tensor(
                                    QS.v[:, c, :], zs.v[:, 0, 0:n], qg, rq.v[:, 0, 0:n], ALU.mult, ALU.mult),
                                    reads=zs.r(0, 0, n) + rq.r(0, 0, n) + VEC.r(), writes=QS.r(c))
                        else:
                            kf = KF[zi]
                            P.op('dve', lambda e, n=n, zs=zs, rq=rq, kf=kf: e.scalar_tensor_tensor(
                                kf.v[:, 0, 0:n], zs.v[:, 0, 0:n], kg, rq.v[:, 0, 0:n], ALU.mult, ALU.mult),
                                reads=zs.r(0, 0, n) + rq.r(0, 0, n) + VEC.r(), writes=kf.r(0, 0, n))
                            P.op('pool', lambda e, n=n, lo=lo, hi=hi, kf=kf: e.tensor_copy(KN.v[:, 0, lo:hi], kf.v[:, 0, 0:n]),
                                 reads=kf.r(0, 0, n), writes=KN.r(0, lo, hi))
                            P.op('sp', lambda e, n=n, lo=lo, hi=hi, kf=kf, c=c: e.dma_start(
                                out=nkT[e_, :, c, lo:hi], in_=kf.v[:, 0, 0:n]), reads=kf.r(0, 0, n), chan=('ok', zi))
                            if ti == 4:
                                P.op('pool', lambda e, n=n, kf=kf, c=c: e.tensor_copy(KS.v[:, c, :], kf.v[:, 0, 0:n]),
                                     reads=kf.r(0, 0, n), writes=KS.r(c))
                    else:
                        vf = KF[zi]
                        if ti == 4:
                            P.op('pool', lambda e, n=n, vf=vf, c=c: e.tensor_copy(VS.v[:, c, :], vf.v[:, 0, 0:n]),
                                 reads=vf.r(0, 0, n), writes=VS.r(c))
                        else:
                            for bi in range(4):
                                kb = ti * 4 + bi
                                tb = 6 + tc_[0] % 2
                                tc_[0] += 1
                                P.op('pe', lambda e, bi=bi, tb=tb, vf=vf: e.transpose(
                                    bk(tb, 0, 128), vf.v[:, 0, bi * 128:(bi + 1) * 128], ident),
                                    reads=vf.r(0, bi * 128, (bi + 1) * 128) + CF.r(0), writes=PS(tb))
                                P.op('dve', lambda e, kb=kb, tb=tb: e.tensor_copy(
                                    VA.v[:, kb, :].rearrange("p (h x) -> p h x", x=65)[:, :, 0:64],
                                    bk(tb, 0, 128).rearrange("p (h x) -> p h x", x=64)),
                                    reads=PS(tb), writes=VA.r(kb))

                st1(0)
                for t in range(len(tl)):
                    if t + 1 < len(tl):
                        st1(t + 1)
                    st2(t)
                steps = [(qb, hh, kb) for qb in range(8) for hh in range(2) for kb in range(2 * qb + 2)]
                nst = len(steps)
                first_flag = {}

                def emit_S(i):
                    qb, hh, kb = steps[i]
                    q0, k0 = qb * 256, kb * 128
                    d = q0 - k0
                    col_lo = max(0, -d)
                    ncol = 256 - col_lo
                    sb = 2 * (i % 3)
                    pt = PT[i % 3]
                    for sub in range(2):
                        j = 2 * hh + sub
                        P.op('pe', lambda e, j=j, sb=sb, sub=sub, k0=k0, q0=q0, col_lo=col_lo, ncol=ncol: e.matmul(
                            bk(sb + sub, 0, ncol), KN.v[32 * j:32 * j + 32, 0, k0:k0 + 128],
                            QN.v[32 * j:32 * j + 32, 0, q0 + col_lo:q0 + 256], start=True, stop=True,
                            tile_position=(32 * j, 0)),
                            reads=KN.r(0, k0, k0 + 128) + QN.r(0, q0 + col_lo, q0 + 256), writes=PS(sb + sub))
                    pspair = PSA[:, sb * 512:(sb + 2) * 512].rearrange("p (s n) -> p s n", n=512)[:, :, 0:ncol]
                    P.op('act', lambda e, pspair=pspair, pt=pt, ncol=ncol: e.activation(
                        pt.v[:, :, 0:ncol], pspair, AF.Exp, scale=SCALE),
                        reads=PS(sb, sb + 1), writes=pt.r(None, 0, ncol))
                    if d < 256:
                        w0 = col_lo + d + 128
                        P.op('dve', lambda e, pt=pt, hh=hh, w0=w0, ncol=ncol: e.tensor_tensor(
                            pt.v[:, :, 0:ncol], pt.v[:, :, 0:ncol],
                            ECB.v[:, hh:hh + 1, w0:w0 + ncol].to_broadcast([128, 2, ncol]), ALU.mult),
                            reads=pt.r(None, 0, ncol) + ECB.r(hh), writes=pt.r(None, 0, ncol))

                def emit_PV(i):
                    qb, hh, kb = steps[i]
                    q0, k0 = qb * 256, kb * 128
                    col_lo = max(0, k0 - q0)
                    pt = PT[i % 3]
                    ob = 6 + hh
                    for sub in range(2):
                        for qs in range(2):
                            if 128 * qs < col_lo:
                                continue
                            c0 = 128 * qs - col_lo
                            last_kb = 2 * qb + qs
                            st = not first_flag.get((qb, hh), False)
                            first_flag[(qb, hh)] = True
                            gi_ = sub * 2 + qs
                            P.op('pe', lambda e, sub=sub, gi_=gi_, c0=c0, pt=pt, kb=kb, hh=hh, ob=ob, st=st, last_kb=last_kb: e.matmul(
                                bk(ob, gi_ * 65, (gi_ + 1) * 65), pt.v[:, sub, c0:c0 + 128],
                                VA.v[:, kb, hh * 65:(hh + 1) * 65], start=st, stop=(kb == last_kb),
                                skip_group_check=True),
                                reads=pt.r(sub, c0, c0 + 128) + VA.r(kb), writes=PS(ob))

                def norm_A(qb):
                    osv = OS.v[:, :, :].rearrange("p a (q x) -> p a q x", x=65)
                    P.op('dve', lambda e: e.tensor_copy(
                        OS.v.rearrange("p (h s) x -> p h (s x)", s=2),
                        PSA[:, 6 * 512:8 * 512].rearrange("p (a n) -> p a n", n=512)[:, :, 0:260]),
                        reads=PS(6, 7), writes=OS.r())
                    P.op('dve', lambda e: e.reciprocal(RR.v[:, 0, 0:8].rearrange("p (a q) -> p a q", q=2), osv[:, :, :, 64]),
                         reads=OS.r(), writes=RR.r())
                    P.op('dve', lambda e: e.tensor_tensor(
                        osv[:, :, :, 0:64], osv[:, :, :, 0:64],
                        RR.v[:, 0, 0:8].rearrange("p (a q) -> p a q", q=2).unsqueeze(3).to_broadcast([128, 4, 2, 64]), ALU.mult),
                        reads=OS.r() + RR.r(), writes=OS.r())
                    os5 = OS.v[:, :, :].rearrange("p (h s) (q x) -> p h s q x", s=2, x=65)
                    a1v = A1.v[:, :, :].rearrange("p h (q x) -> p h q x", x=64)
                    for h2 in range(2):
                        P.op('dve', lambda e, h2=h2: e.scalar_tensor_tensor(
                            a1v[:, h2], os5[:, h2, 1, :, 0:64], nlam, os5[:, h2, 0, :, 0:64], ALU.mult, ALU.add),
                            reads=OS.r() + LAM.r(e_), writes=A1.r(h2))
                    sqv = OS.v[:, 0:2, 0:128].rearrange("p h (q x) -> p h q x", x=64)
                    P.op('dve', lambda e: e.tensor_tensor(sqv, a1v, a1v, ALU.mult), reads=A1.r(), writes=OS.r())
                    P.op('dve', lambda e: e.tensor_reduce(RR.v[:, 0, 8:12].rearrange("p (h q) -> p h q", q=2), sqv, AX.X, ALU.add),
                         reads=OS.r(), writes=RR.r())

                def norm_B(qb):
                    atok = ATOK[qb % 2]
                    a1v = A1.v[:, :, :].rearrange("p h (q x) -> p h q x", x=64)
                    P.op('act', lambda e: e.activation(RR.v[:, 0, 8:12], RR.v[:, 0, 8:12], AF.Ln, bias=epsc, scale=1.0 / 64),
                         reads=RR.r() + EPSC.r(), writes=RR.r())
                    P.op('act', lambda e: e.activation(RR.v[:, 0, 8:12], RR.v[:, 0, 8:12], AF.Exp, scale=-0.5),
                         reads=RR.r(), writes=RR.r())
                    P.op('dve', lambda e: e.tensor_tensor(
                        a1v, a1v, RR.v[:, 0, 8:12].rearrange("p (h q) -> p h q", q=2).unsqueeze(3).to_broadcast([128, 2, 2, 64]),
                        ALU.mult), reads=A1.r() + RR.r(), writes=A1.r())
                    av = atok.v[:, 0, :].rearrange("p (q h x) -> p h q x", q=2, h=2)
                    P.op('dve', lambda e, av=av: e.tensor_tensor(
                        av, a1v, SGR.v[:, e_:e_ + 1, :].unsqueeze(1).to_broadcast([128, 2, 2, 64]), ALU.mult),
                        reads=A1.r() + SGR.r(e_), writes=atok.r())

                def trans(qb):
                    atok = ATOK[qb % 2]
                    q0 = qb * 256
                    for qs in range(2):
                        tb = 2 * (qb % 3) + qs
                        P.op('pe', lambda e, qs=qs, tb=tb, atok=atok: e.transpose(
                            bk(tb, 0, 128), atok.v[:, 0, qs * 128:(qs + 1) * 128], ident),
                            reads=atok.r() + CF.r(0), writes=PS(tb))
                        P.op('act', lambda e, qs=qs, tb=tb, c=c, q0=q0: e.activation(
                            ATT.v[:, c, q0 + 128 * qs:q0 + 128 * qs + 128], bk(tb, 0, 128), AF.Copy),
                            reads=PS(tb), writes=ATT.r(c, q0 + 128 * qs, q0 + 128 * qs + 128))

                deferred = []
                emit_S(0)
                emit_S(1)
                for i in range(nst):
                    if i + 2 < nst:
                        emit_S(i + 2)
                    emit_PV(i)
                    qb, hh, kb = steps[i]
                    if hh == 1 and kb == 2 * qb + 1:
                        norm_A(qb)
                        deferred.append((i + 2, norm_B, qb))
                        deferred.append((i + 5, trans, qb))
                    for item in list(deferred):
                        if item[0] <= i:
                            item[1](item[2])
                            deferred.remove(item)
                for item in sorted(deferred, key=lambda t: t[0]):
                    item[1](item[2])

            Dd = Alloc(big, LIMIT)
            Dd.cur = S0
            KP2 = [Dd.ten(TS, 512, BF16) for _ in range(3)]
            VP2 = [Dd.ten(TS, 512, BF16) for _ in range(3)]
            PRD = Dd.ten(TS, 512, BF16)
            SS2 = Dd.ten(TS, 16, F32)
            PD2 = [Dd.ten(TS, 48, BF16) for _ in range(3)]
            KP1 = Dd.at(KP2[0].off, 1, 512, BF16)
            VP1 = Dd.at(VP2[0].off, 1, 512, BF16)
            SS1 = Dd.ten(2, 16, F32)
            PD1 = Dd.ten(2, 16, BF16)
            DG = Dd.ten(4, 128, F32)
            DGB = Dd.ten(4, 128, F32)
            QB2 = Dd.ten(1, 512, BF16)
            QB = Dd.ten(1, 512, BF16)
            KSF = Dd.ten(1, 512, BF16)
            VSF = Dd.ten(1, 512, BF16)
            OM = Dd.at(DGB.off, 1, 512, F32)
            R1 = Dd.ten(1, 64, F32)
            AS = Dd.ten(1, 64, F32)
            AS2 = Dd.ten(1, 128, F32)
            RD = Dd.ten(1, 8, F32)
            assert Dd.cur <= LIMIT, (Dd.cur, LIMIT)
            ck1, cv1 = cache_k[e_], cache_v[e_]
            ck2 = cache_k[e_].rearrange("(r t) f -> r (t f)", t=TS)
            cv2 = cache_v[e_].rearrange("(r t) f -> r (t f)", t=TS)
            negm = c2v('negm', 0, 128)
            LA, LB = CF.v[:, 5, :], CF.v[:, 6, :]
            gctr = [0]

            def diag(dst, src, s):
                P.op('dve', lambda e: e.tensor_tensor(
                    dst.v, CF.v[:, 0:1, :].to_broadcast([128, 4, 128]),
                    src.v[:, :, s:s + 1].to_broadcast([128, 4, 128]), ALU.mult),
                    reads=CF.r(0) + src.r(), writes=dst.r())

            for pd_ in PD2:
                P.op('pool', lambda e, pd_=pd_: e.memset(pd_.v, 0.0), writes=pd_.r())
            for pair in range(2):
                sA, sB = 2 * pair, 2 * pair + 1
                diag(DG, QS, sA)
                diag(DGB, QS, sB)
                P.op('pe', lambda e: e.matmul(bk(0), LA, DG.v.rearrange("p a b -> p (a b)"), start=True, stop=False),
                     reads=DG.r() + CF.r(5), writes=PS(0))
                P.op('pe', lambda e: e.matmul(bk(0), LB, DGB.v.rearrange("p a b -> p (a b)"), start=False, stop=True),
                     reads=DGB.r() + CF.r(6), writes=PS(0))
                P.op('act', lambda e: e.activation(QB2.v[:, 0, :], bk(0), AF.Copy), reads=PS(0), writes=QB2.r())
                for g in range(NSTEP):
                    st_ = gctr[0] % 3
                    gctr[0] += 1
                    kp, vp, pd = KP2[st_], VP2[st_], PD2[st_]
                    P.op('pool', lambda e, g=g, kp=kp, pair=pair: e.indirect_dma_start(
                        out=kp.v.rearrange("p a b -> p (a b)"), out_offset=None, in_=ck2,
                        in_offset=bass.IndirectOffsetOnAxis(ap=IDX2.v[:, pair, g:g + 1], axis=0)),
                        reads=IDX2.r(pair), writes=kp.r(), chan=('kp', st_))
                    P.op('pool', lambda e, g=g, vp=vp, pair=pair: e.indirect_dma_start(
                        out=vp.v.rearrange("p a b -> p (a b)"), out_offset=None, in_=cv2,
                        in_offset=bass.IndirectOffsetOnAxis(ap=IDX2.v[:, pair, g:g + 1], axis=0)),
                        reads=IDX2.r(pair), writes=vp.r(), chan=('vp', st_))
                    P.op('dve', lambda e, kp=kp: e.tensor_tensor(
                        PRD.v, kp.v, QB2.v.to_broadcast([128, TS, 512]), ALU.mult),
                        reads=kp.r() + QB2.r(), writes=PRD.r())
                    P.op('dve', lambda e: e.tensor_reduce(
                        SS2.v.rearrange("p a b -> p (a b)"), PRD.v.rearrange("p a (h x) -> p (a h) x", x=32), AX.X, ALU.add),
                        reads=PRD.r(), writes=SS2.r())
                    for half in range(2):
                        p0, p1 = 64 * half, 64 * half + 64
                        P.op('act', lambda e, pd=pd, p0=p0, p1=p1, half=half: e.activation(
                            pd.v[p0:p1, :, 32 * half:32 * half + 16], SS2.v[p0:p1, :, :], AF.Exp,
                            bias=c2v('negm', p0, p1), scale=SCALE),
                            reads=SS2.r() + C2.r(), writes=pd.r())
                    for t in range(TS):
                        first = (g == 0 and t == 0)
                        P.op('pe', lambda e, t=t, pd=pd, vp=vp, first=first: e.matmul(
                            bk(3)[0:48, :], pd.v[:, t, :], vp.v[:, t, :], start=first, stop=False),
                            reads=pd.r(t) + vp.r(t), writes=PS(3))
                        P.op('pe', lambda e, t=t, pd=pd, first=first: e.matmul(
                            bk(5, 0, 1)[0:48, :], pd.v[:, t, :], ones_b[:, 0:1], start=first, stop=False),
                            reads=pd.r(t) + CB.r(4), writes=PS(5))
                for half, s in enumerate((sA, sB)):
                    ob_, db_ = 3, 5
                    r0, r1 = 32 * half, 32 * half + 16
                    for (src, dst, b) in ((QS, QB, 0), (KS, KSF, 1), (VS, VSF, 2)):
                        diag(DG, src, s)
                        P.op('pe', lambda e, b=b: e.matmul(bk(b), ones_f, DG.v.rearrange("p a b -> p (a b)"), start=True, stop=True),
                             reads=DG.r() + CF.r(4), writes=PS(b))
                        P.op('act', lambda e, dst=dst, b=b: e.activation(dst.v[:, 0, :], bk(b), AF.Copy), reads=PS(b), writes=dst.r())
                    P.op('pool', lambda e, s=s: e.indirect_dma_start(
                        out=KP1.v[:, 0, :], out_offset=None, in_=ck1,
                        in_offset=bass.IndirectOffsetOnAxis(ap=IDX63.v[:, 0, s:s + 1], axis=0)),
                        reads=IDX63.r(), writes=KP1.r(), chan='kp1')
                    P.op('pool', lambda e, s=s: e.indirect_dma_start(
                        out=VP1.v[:, 0, :], out_offset=None, in_=cv1,
                        in_offset=bass.IndirectOffsetOnAxis(ap=IDX63.v[:, 0, s:s + 1], axis=0)),
                        reads=IDX63.r(), writes=VP1.r(), chan='vp1')
                    for k, ksrc in enumerate((KP1, KSF)):
                        P.op('dve', lambda e, ksrc=ksrc: e.tensor_tensor(PRD.v[:, 0, :], ksrc.v[:, 0, :], QB.v[:, 0, :], ALU.mult),
                             reads=ksrc.r() + QB.r(), writes=PRD.r(0))
                        P.op('dve', lambda e, k=k: e.tensor_reduce(
                            SS1.v[:, k, :], PRD.v[:, 0, :].rearrange("p (h x) -> p h x", x=32), AX.X, ALU.add),
                            reads=PRD.r(0), writes=SS1.r(k))
                    P.op('dve', lambda e: e.scalar_tensor_tensor(
                        SS1.v.rearrange("p k (h x) -> p k h x", x=2)[:, 0], SS1.v.rearrange("p k (h x) -> p k h x", x=2)[:, 0], SCALE,
                        BDEC.v[:, 0, :].unsqueeze(2).to_broadcast([128, 8, 2]), ALU.mult, ALU.add),
                        reads=SS1.r(0) + BDEC.r(0), writes=SS1.r(0))
                    P.op('dve', lambda e: e.scalar_tensor_tensor(
                        SS1.v.rearrange("p k (h x) -> p k h x", x=2)[:, 1], SS1.v.rearrange("p k (h x) -> p k h x", x=2)[:, 1], SCALE,
                        BDEC.v[:, 1, :].unsqueeze(2).to_broadcast([128, 8, 2]), ALU.mult, ALU.add),
                        reads=SS1.r(1) + BDEC.r(1), writes=SS1.r(1))
                    P.op('act', lambda e: e.activation(PD1.v, SS1.v, AF.Exp), reads=SS1.r(), writes=PD1.r())
                    for k, vsrc in enumerate((VP1, VSF)):
                        P.op('pe', lambda e, k=k, vsrc=vsrc, ob_=ob_, r0=r0, r1=r1: e.matmul(
                            bk(ob_)[r0:r1, :], PD1.v[:, k, :], vsrc.v[:, 0, :], start=False, stop=(k == 1),
                            tile_position=(0, r0), skip_group_check=True),
                            reads=PD1.r(k) + vsrc.r(), writes=PS(ob_))
                        P.op('pe', lambda e, k=k, db_=db_, r0=r0, r1=r1: e.matmul(
                            bk(db_, 0, 1)[r0:r1, :], PD1.v[:, k, :], ones_b[:, 0:1], start=False, stop=(k == 1),
                            tile_position=(0, r0), skip_group_check=True),
                            reads=PD1.r(k) + CB.r(4), writes=PS(db_))
                    P.op('dve', lambda e, db_=db_, r0=r0, r1=r1: e.reciprocal(RD.v[r0:r1, 0, 0:1], bk(db_, 0, 1)[r0:r1, :]), reads=PS(db_), writes=RD.r())
                    P.op('dve', lambda e, r0=r0, r1=r1: e.tensor_tensor(RD.v[r0:r1, 0, 0:1], RD.v[r0:r1, 0, 0:1], LAMC.v[r0:r1, e_, :], ALU.mult),
                         reads=RD.r() + LAMC.r(e_), writes=RD.r())
                    P.op('dve', lambda e, ob_=ob_, r0=r0, r1=r1: e.tensor_tensor(OM.v[r0:r1, 0, :], bk(ob_)[r0:r1, :], c2v('maskd', r0, r1), ALU.mult),
                         reads=PS(ob_) + C2.r(), writes=OM.r())
                    P.op('dve', lambda e, r0=r0, r1=r1: e.tensor_reduce(
                        R1.v[r0:r1, 0, :], OM.v[r0:r1, 0, :].rearrange("p (h x) -> p x h", x=64), AX.X, ALU.add),
                        reads=OM.r(), writes=R1.r())
                    P.op('dve', lambda e, r0=r0, r1=r1: e.tensor_scalar(R1.v[r0:r1, 0, :], R1.v[r0:r1, 0, :], RD.v[r0:r1, 0, 0:1], None, ALU.mult),
                         reads=R1.r() + RD.r(), writes=R1.r())
                    P.op('pe', lambda e, r0=r0, r1=r1: e.matmul(bk(0, 0, 64)[0:8, :], c2v('pair', r0, r1), R1.v[r0:r1, 0, :], start=True, stop=True),
                         reads=R1.r() + C2.r(), writes=PS(0))
                    P.op('dve', lambda e: e.tensor_copy(AS.v[0:8, 0, :], bk(0, 0, 64)[0:8, :]), reads=PS(0), writes=AS.r())
                    P.op('dve', lambda e: e.tensor_tensor(R1.v[0:8, 0, :], AS.v[0:8, 0, :], AS.v[0:8, 0, :], ALU.mult),
                         reads=AS.r(), writes=R1.r())
                    P.op('dve', lambda e: e.tensor_reduce(RD.v[0:8, 0, 1:2], R1.v[0:8, 0, :], AX.X, ALU.add),
                         reads=R1.r(), writes=RD.r())
                    P.op('act', lambda e: e.activation(RD.v[0:8, 0, 1:2], RD.v[0:8, 0, 1:2], AF.Ln, bias=EPSC.v[0:8, 0, 0:1], scale=1.0 / 64),
                         reads=RD.r() + EPSC.r(), writes=RD.r())
                    P.op('act', lambda e: e.activation(RD.v[0:8, 0, 1:2], RD.v[0:8, 0, 1:2], AF.Exp, scale=-0.5),
                         reads=RD.r(), writes=RD.r())
                    P.op('dve', lambda e: e.tensor_scalar(AS.v[0:8, 0, :], AS.v[0:8, 0, :], RD.v[0:8, 0, 1:2], None,
                                                          ALU.mult), reads=AS.r() + RD.r(), writes=AS.r())
                    P.op('dve', lambda e: e.tensor_tensor(AS.v[0:8, 0, :], AS.v[0:8, 0, :], SGR.v[0:8, e_, :], ALU.mult),
                         reads=AS.r() + SGR.r(e_), writes=AS.r())
                    P.op('dve', lambda e: e.tensor_tensor(
                        AS2.v[0:8, 0, :].rearrange("p (r x) -> p r x", x=64), AS.v[0:8, 0:1, :].to_broadcast([8, 2, 64]),
                        c2v('maskp', 0, 8).rearrange("p (r x) -> p r x", x=64), ALU.mult),
                        reads=AS.r() + C2.r(), writes=AS2.r())
                    P.op('pe', lambda e: e.matmul(bk(1, 0, 4), AS2.v[0:8, 0, :], c2v('selc', 0, 8), start=True, stop=True),
                         reads=AS2.r() + C2.r(), writes=PS(1))
                    P.op('dve', lambda e, s=s: e.tensor_copy(ATT.v[:, :, NPROMPT + s], bk(1, 0, 4)),
                         reads=PS(1), writes=ATT.r(None, NPROMPT + s, NPROMPT + s + 1))

            Cc = Alloc(big, LIMIT)
            Cc.cur = S0
            UB = Cc.ten(1, 2 + NT, F32)
            GC = [Cc.ten(1, 512, F32) for _ in range(2)]
            CV = [Cc.ten(1, 512, F32) for _ in range(2)]
            assert Cc.cur <= LIMIT
            scbv = SCB.v[:, e_, :].rearrange("p (c k s) -> p c k s", c=4, k=2)
            cc_ = [0]
            for i in range(4):
                hgc = ws.get()
                hx = ws.get()
                hgb = ws.get()
                w0, w1, w2 = (vcol(('cbw', e_), k * 4 + i) for k in range(3))
                P.op('dve', lambda e: e.memset(UB.v[:, 0, 0:2], 0.0), writes=UB.r(0, 0, 2))
                for ti, (lo, hi) in enumerate(MT):
                    n = hi - lo
                    b1, b2, b3 = 0 + 3 * (cc_[0] % 2), 1 + 3 * (cc_[0] % 2), 2 + 3 * (cc_[0] % 2)
                    gc, cvt = GC[cc_[0] % 2], CV[cc_[0] % 2]
                    cc_[0] += 1
                    proj(hgc, XNS, b1, lo, hi)
                    proj(hx, XNS, b2, lo, hi)
                    proj(hgb, XNS, b3, lo, hi)
                    P.op('act', lambda e, n=n, b1=b1, gc=gc: e.activation(gc.v[:, 0, 0:n], bk(b1, 0, n), AF.Copy),
                         reads=PS(b1), writes=gc.r(0, 0, n))
                    P.op('dve', lambda e, n=n, b2=b2, gc=gc, lo=lo, hi=hi: e.tensor_tensor(
                        UB.v[:, 0, 2 + lo:2 + hi], gc.v[:, 0, 0:n], bk(b2, 0, n), ALU.mult),
                        reads=gc.r(0, 0, n) + PS(b2), writes=UB.r(0, 2 + lo, 2 + hi))
                    if ti < 4:
                        P.op('dve', lambda e, n=n, lo=lo, cvt=cvt, w0=w0: e.tensor_scalar(
                            cvt.v[:, 0, 0:n], UB.v[:, 0, lo:lo + n], w0, None, ALU.mult),
                            reads=UB.r(0, lo, lo + n) + VEC.r(), writes=cvt.r(0, 0, n))
                        P.op('dve', lambda e, n=n, lo=lo, cvt=cvt, w1=w1: e.scalar_tensor_tensor(
                            cvt.v[:, 0, 0:n], UB.v[:, 0, 1 + lo:1 + lo + n], w1, cvt.v[:, 0, 0:n], ALU.mult, ALU.add),
                            reads=UB.r(0, 1 + lo, 1 + lo + n) + VEC.r() + cvt.r(0, 0, n), writes=cvt.r(0, 0, n))
                        P.op('dve', lambda e, n=n, lo=lo, cvt=cvt, w2=w2: e.scalar_tensor_tensor(
                            cvt.v[:, 0, 0:n], UB.v[:, 0, 2 + lo:2 + lo + n], w2, cvt.v[:, 0, 0:n], ALU.mult, ALU.add),
                            reads=UB.r(0, 2 + lo, 2 + lo + n) + VEC.r() + cvt.r(0, 0, n), writes=cvt.r(0, 0, n))
                    else:
                        P.op('dve', lambda e, cvt=cvt, w0=w0, i=i: e.tensor_scalar(
                            cvt.v[:, 0, 0:4], scbv[:, i, 0, :], w0, None, ALU.mult),
                            reads=SCB.r(e_) + VEC.r(), writes=cvt.r(0, 0, 4))
                        P.op('dve', lambda e, cvt=cvt, w1=w1, i=i: e.scalar_tensor_tensor(
                            cvt.v[:, 0, 0:4], scbv[:, i, 1, :], w1, cvt.v[:, 0, 0:4], ALU.mult, ALU.add),
                            reads=SCB.r(e_) + VEC.r() + cvt.r(0, 0, 4), writes=cvt.r(0, 0, 4))
                        P.op('dve', lambda e, cvt=cvt, w2=w2: e.scalar_tensor_tensor(
                            cvt.v[:, 0, 0:4], UB.v[:, 0, 2 + NPROMPT:2 + NT], w2, cvt.v[:, 0, 0:4], ALU.mult, ALU.add),
                            reads=UB.r(0, 2 + NPROMPT, 2 + NT) + VEC.r() + cvt.r(0, 0, 4), writes=cvt.r(0, 0, 4))
                    P.op('dve', lambda e, n=n, b3=b3, cvt=cvt, lo=lo, hi=hi, i=i: e.tensor_tensor(
                        CATB.v[:, i, lo:hi], cvt.v[:, 0, 0:n], bk(b3, 0, n), ALU.mult),
                        reads=cvt.r(0, 0, n) + PS(b3), writes=CATB.r(i, lo, hi))
                P.op('sp', lambda e, i=i: e.dma_start(out=cbpT[e_, :, i, :], in_=UB.v[:, 0, NPROMPT:NPROMPT + 2]),
                     reads=UB.r(0, NPROMPT, NPROMPT + 2), chan='ocb')
                P.op('sp', lambda e, i=i: e.dma_start(out=cbs0T[e_, :, i, :], in_=scbv[:, i, 1, :]),
                     reads=SCB.r(e_), chan='ocb')
                P.op('sp', lambda e, i=i: e.dma_start(out=cbs1T[e_, :, i, :], in_=UB.v[:, 0, 2 + NPROMPT:2 + NT]),
                     reads=UB.r(0, 2 + NPROMPT, 2 + NT), chan='ocb')

            CATS = [(ATT, k) for k in range(4)] + [(CATB, k) for k in range(4)]
            oc2 = [0]
            for dch in range(8):
                ho = ws.get()
                for (lo, hi) in FT:
                    n = hi - lo
                    b = oc2[0] % 4
                    oc2[0] += 1
                    proj(ho, CATS, b, lo, hi)
                    P.op('dve', lambda e, dch=dch, lo=lo, hi=hi, n=n, b=b: e.tensor_tensor(
                        X.v[:, dch, lo:hi], bk(b, 0, n), X.v[:, dch, lo:hi], ALU.add),
                        reads=PS(b) + X.r(dch, lo, hi), writes=X.r(dch, lo, hi))

        def odd_mixer(l):
            o_ = l // 2
            B = Alloc(big, LIMIT)
            B.cur = R0
            CC = B.ten(8, NT, BF16)
            UO2 = [B.ten(1, 30 + NT, BF16) for _ in range(2)]
            UF2 = [B.ten(1, 34, F32) for _ in range(2)]
            UREP = B.ten(4, 30 + NT + 2, BF16)
            LW = B.ten(32, 32, BF16)
            WR = B.ten(4, 32, F32)
            SCS = B.ten(8, 120, F32)
            PRS = B.ten(4, 30, F32)
            CS4 = B.ten(1, 8, F32)
            B3 = Alloc(big, UREP.off + UREP.nbytes)
            B3.cur = UREP.off
            SQ = B3.ten(8, 342, BF16)
            MU = [B3.ten(1, 342, F32) for _ in range(2)]
            T1 = [B3.ten(1, 342, F32) for _ in range(2)]
            items = []
            for i in range(8):
                items += [(w_pw1[o_], i * 128), (w_pw1[o_], (8 + i) * 128)]
            for dch in range(8):
                items += [(w_pw2[o_], dch * 128)]
            ws = WStream(items, 4)
            ws.fill()
            rmsnorm(('nm', l), FT)
            P.op('sp', lambda e: e.dma_start(out=SCS.v, in_=scc[:, o_].rearrange("p c s k -> p c (s k)")),
                 writes=SCS.r(), chan='ld')
            for UO in UO2:
                P.op('pool', lambda e, UO=UO: e.memset(UO.v[:, 0, 0:30], 0.0), writes=UO.r(0, 0, 30))
            ccw0 = lay[('ccw', o_)][0]
            pc = [0]

            def stageA(i):
                UO, UF = UO2[i % 2], UF2[i % 2]
                ha = ws.get()
                hb = ws.get()
                ba_, bb_ = vcol(('bpw1', o_), i), vcol(('bpw1', o_), 8 + i)
                for ti, (lo, hi) in enumerate(FT):
                    n = hi - lo
                    b1 = 2 * (pc[0] % 2)
                    b2 = b1 + 1
                    sg = SG[pc[0] % 3]
                    pc[0] += 1
                    proj(ha, XNS, b1, lo, hi)
                    proj(hb, XNS, b2, lo, hi)
                    P.op('act', lambda e, n=n, b2=b2, sg=sg, bb_=bb_: e.activation(sg.v[:, 0, 0:n], bk(b2, 0, n), AF.Sigmoid, bias=bb_),
                         reads=PS(b2) + VEC.r(), writes=sg.r())
                    P.op('dve', lambda e, n=n, b1=b1, sg=sg, lo=lo, hi=hi, ba_=ba_, UO=UO: e.scalar_tensor_tensor(
                        UO.v[:, 0, 30 + lo:30 + hi], bk(b1, 0, n), ba_, sg.v[:, 0, 0:n], ALU.add, ALU.mult),
                        reads=PS(b1) + sg.r() + VEC.r(), writes=UO.r(0, 30 + lo, 30 + hi))
                    if ti == 5:
                        P.op('dve', lambda e, n=n, b1=b1, sg=sg, ba_=ba_, UF=UF: e.scalar_tensor_tensor(
                            UF.v[:, 0, :], bk(b1, n - 34, n), ba_, sg.v[:, 0, n - 34:n], ALU.add, ALU.mult),
                            reads=PS(b1) + sg.r() + VEC.r(), writes=UF.r())

            def stageR(i):
                UO = UO2[i % 2]
                for r in range(4):
                    for cg in range(4):
                        P.op('sp', lambda e, r=r, cg=cg, UO=UO: e.dma_start(
                            out=UREP.v[32 * r:32 * r + 32, cg, 0:2076], in_=UO.v[32 * cg:32 * cg + 32, 0, r:r + 2076]),
                            reads=UO.r(0, r, r + 2076), writes=[('urep', cg, r)], chan=('ur', (r * 4 + cg) % 8))

            def stageB(i):
                UO, UF = UO2[i % 2], UF2[i % 2]
                wv = VEC.v[:, 0, ccw0 + i: ccw0 + 248: 8]
                P.op('sp', lambda e, i=i: e.dma_start(out=WR.v, in_=ccwrep[o_, i]), writes=WR.r(), chan='wr')
                for r in range(4):
                    P.op('dve', lambda e, r=r: e.tensor_tensor(
                        LW.v[32 * r:32 * r + 32, :, :],
                        CB.v[32 * r:32 * r + 32, 0:1, 32 * r:32 * r + 32].to_broadcast([32, 32, 32]),
                        WR.v[32 * r:32 * r + 32, :, r::4].rearrange("p c j -> p (c j)").unsqueeze(2).to_broadcast([32, 32, 32]),
                        ALU.mult), reads=CB.r(0) + WR.r(), writes=LW.r())
                cb_ = vcol(('ccb', o_), i)
                for blk in range(4):
                    t0 = blk * 512
                    b = 4 + (pc[0] % 2)
                    pc[0] += 1
                    for jg in range(8):
                        for cg in range(4):
                            P.op('pe', lambda e, jg=jg, cg=cg, t0=t0, b=b: e.matmul(
                                bk(b)[32 * cg:32 * cg + 32, :], LW.v[:, cg * 8 + jg, :],
                                UREP.v[:, cg, t0 + 4 * jg:t0 + 4 * jg + 512], start=(jg == 0), stop=(jg == 7),
                                tile_position=(0, 32 * cg), skip_group_check=True),
                                reads=LW.r(cg * 8 + jg) + [('urep', cg, r_) for r_ in range(4)], writes=PS(b))
                    P.op('act', lambda e, t0=t0, b=b, i=i, cb_=cb_: e.activation(
                        CC.v[:, i, t0:t0 + 512], bk(b), AF.Identity, bias=cb_),
                        reads=PS(b) + VEC.r(), writes=CC.r(i, t0, t0 + 512))
                P.op('dve', lambda e, i=i, wv=wv: e.tensor_tensor(
                    PRS.v, SCS.v[:, i, :].rearrange("p (s k) -> p s k", k=30),
                    wv[:, 0:30].unsqueeze(1).to_broadcast([128, 4, 30]), ALU.mult),
                    reads=SCS.r(i) + VEC.r(), writes=PRS.r())
                P.op('dve', lambda e: e.tensor_reduce(CS4.v[:, 0, 0:4], PRS.v, AX.X, ALU.add), reads=PRS.r(), writes=CS4.r())
                P.op('dve', lambda e, wv=wv, UF=UF: e.scalar_tensor_tensor(
                    CS4.v[:, 0, 0:4], UF.v[:, 0, 30:34], wv[:, 30:31], CS4.v[:, 0, 0:4], ALU.mult, ALU.add),
                    reads=UF.r() + VEC.r() + CS4.r(), writes=CS4.r())
                P.op('dve', lambda e, i=i, cb_=cb_: e.tensor_scalar(
                    CC.v[:, i, NPROMPT:NT], CS4.v[:, 0, 0:4], cb_, None, ALU.add),
                    reads=CS4.r() + VEC.r(), writes=CC.r(i, NPROMPT, NT))
                P.op('sp', lambda e, i=i, UF=UF: e.dma_start(out=ccpT[o_, :, i, :], in_=UF.v[:, 0, 0:30]), reads=UF.r(), chan='occ')
                P.op('sp', lambda e, i=i: e.dma_start(
                    out=ccsoT[o_, :, i, :, :], in_=SCS.v[:, i, :].rearrange("p (s k) -> p s k", k=30)[:, :, 1:30]),
                    reads=SCS.r(i), chan='occ')
                P.op('sp', lambda e, i=i, UF=UF: e.dma_start(out=ccsnT[o_, :, i, :], in_=UF.v[:, 0, 30:34]),
                     reads=UF.r(), chan='occ')

            stageA(0)
            stageR(0)
            for i in range(8):
                if i + 1 < 8:
                    stageA(i + 1)
                stageB(i)
                if i + 1 < 8:
                    stageR(i + 1)
            lc = [0]
            hos = [ws.get() for _ in range(8)]
            oc2 = [0]
            for (lo, hi) in FT:
                n = hi - lo
                bm, bv = 0 + 2 * (lc[0] % 2), 1 + 2 * (lc[0] % 2)
                mu, t1 = MU[lc[0] % 2], T1[lc[0] % 2]
                lc[0] += 1
                P.op('act', lambda e, lo=lo, hi=hi, n=n: e.activation(SQ.v[:, :, 0:n], CC.v[:, :, lo:hi], AF.Square),
                     reads=CC.r(None, lo, hi), writes=SQ.r(None, 0, n))
                for kc in range(8):
                    P.op('pe', lambda e, kc=kc, n=n, bm=bm, lo=lo, hi=hi: e.matmul(
                        bk(bm, 0, n), mean_b, CC.v[:, kc, lo:hi], start=(kc == 0), stop=(kc == 7)),
                        reads=CC.r(kc, lo, hi) + CB.r(1), writes=PS(bm))
                for kc in range(8):
                    P.op('pe', lambda e, kc=kc, n=n, bv=bv: e.matmul(
                        bk(bv, 0, n), mean_b, SQ.v[:, kc, 0:n], start=(kc == 0), stop=(kc == 7)),
                        reads=SQ.r(kc, 0, n) + CB.r(1), writes=PS(bv))
                P.op('act', lambda e, n=n, bm=bm, mu=mu: e.activation(mu.v[:, 0, 0:n], bk(bm, 0, n), AF.Copy),
                     reads=PS(bm), writes=mu.r())
                P.op('dve', lambda e, n=n, mu=mu, t1=t1: e.tensor_tensor(t1.v[:, 0, 0:n], mu.v[:, 0, 0:n], mu.v[:, 0, 0:n], ALU.mult),
                     reads=mu.r(), writes=t1.r())
                P.op('dve', lambda e, n=n, bv=bv, t1=t1: e.tensor_tensor(t1.v[:, 0, 0:n], bk(bv, 0, n), t1.v[:, 0, 0:n], ALU.subtract),
                     reads=PS(bv) + t1.r(), writes=t1.r())
                P.op('act', lambda e, n=n, t1=t1: e.activation(t1.v[:, 0, 0:n], t1.v[:, 0, 0:n], AF.Sqrt, bias=epsc),
                     reads=t1.r() + EPSC.r(), writes=t1.r())
                P.op('dve', lambda e, n=n, t1=t1: e.reciprocal(t1.v[:, 0, 0:n], t1.v[:, 0, 0:n]), reads=t1.r(), writes=t1.r())
                for c in range(8):
                    sg = SG[c % 3]
                    P.op('dve', lambda e, c=c, lo=lo, hi=hi, n=n, mu=mu, sg=sg: e.tensor_tensor(
                        sg.v[:, 0, 0:n], CC.v[:, c, lo:hi], mu.v[:, 0, 0:n], ALU.subtract),
                        reads=CC.r(c, lo, hi) + mu.r(), writes=sg.r())
                    P.op('dve', lambda e, n=n, t1=t1, sg=sg: e.tensor_tensor(
                        sg.v[:, 0, 0:n], sg.v[:, 0, 0:n], t1.v[:, 0, 0:n], ALU.mult),
                        reads=sg.r() + t1.r(), writes=sg.r())
                    P.op('act', lambda e, c=c, lo=lo, hi=hi, n=n, sg=sg: e.activation(
                        XN.v[:, c, lo:hi], sg.v[:, 0, 0:n], AF.Silu, bias=vcol(('lnb', o_), c), scale=vcol(('lng', o_), c)),
                        reads=sg.r() + VEC.r(), writes=XN.r(c, lo, hi))
                for dch in range(8):
                    bo = vcol(('bpw2', o_), dch)
                    b = 4 + oc2[0] % 4
                    oc2[0] += 1
                    proj(hos[dch], XNS, b, lo, hi)
                    P.op('dve', lambda e, dch=dch, lo=lo, hi=hi, n=n, b=b, bo=bo: e.scalar_tensor_tensor(
                        X.v[:, dch, lo:hi], bk(b, 0, n), bo, X.v[:, dch, lo:hi], ALU.add, ALU.add),
                        reads=PS(b) + X.r(dch, lo, hi) + VEC.r(), writes=X.r(dch, lo, hi))

        if n_even > 0:
            even_setup()
        for l in range(depth):
            rmsnorm(('n1', l), FT)
            ffn(0, l)
            if l % 2 == 0:
                even_mixer(l)
            else:
                odd_mixer(l)
            rmsnorm(('n2', l), FT)
            ffn(1, l)
        P.op('sp', lambda e: e.dma_start(out=yT, in_=X.v), reads=X.r(), chan='out')
        P.emit()
    return nc


NCORES = 8


def make_in_maps(inp, cores):
    consts, c2, _, dstat = static_tables()
    f32 = lambda a: np.ascontiguousarray(np.asarray(a, np.float32))
    vecsT = build_vecs(inp)
    relb = np.concatenate([f32(inp['rel_bias']), np.ones((1, 8), np.float32)], 0)
    lamv = np.stack([f32(inp['lambda_q1']), f32(inp['lambda_k1']), f32(inp['lambda_q2']), f32(inp['lambda_k2'])], 1)
    shared = dict(vecsT=vecsT, consts=consts, cst2=c2, dstat=dstat, relb=relb, lamv=f32(lamv), sgrow=f32(inp['subln_gain']),
                  )
    ckf = f32(inp['cache_k']).reshape(2, 2560 * 128, 512)
    cvf = f32(inp['cache_v']).reshape(2, 2560 * 128, 512)
    for i in range(2):
        shared['cache_k%d' % i] = ckf[i]
        shared['cache_v%d' % i] = cvf[i]
    for k in ["ffn1_w_gate", "ffn1_w_up", "ffn1_w_down", "ffn2_w_gate", "ffn2_w_up", "ffn2_w_down",
              "w_in_even", "w_out_even", "w_pw1", "w_pw2"]:
        shared[k] = f32(inp[k])
    cw = f32(inp['conv_c_w'])
    cwp = np.zeros((2, 32, 1024), np.float32)
    cwp[:, :31] = cw
    t_ = cwp.reshape(2, 32, 8, 4, 32).transpose(0, 2, 4, 3, 1)
    shared['ccwrep'] = np.ascontiguousarray(np.broadcast_to(t_[:, :, None], (2, 8, 4, 32, 4, 32)).reshape(2, 8, 128, 4, 32))
    maps = []
    for core in cores:
        xp = f32(inp['x_prompt'][core])
        xs = f32(inp['x_sample'][4 * core:4 * core + 4, 0])
        xall = np.concatenate([xp, xs], 0)
        m = dict(shared)
        m['xT'] = np.ascontiguousarray(xall.reshape(NT, 8, 128).transpose(2, 1, 0))
        sb = f32(inp['state_conv_b'][:, 4 * core:4 * core + 4])
        m['scb'] = np.ascontiguousarray(sb.reshape(2, 4, 2, 4, 128).transpose(4, 0, 3, 2, 1))
        sc = f32(inp['state_conv_c'][:, 4 * core:4 * core + 4])
        m['scc'] = np.ascontiguousarray(sc.reshape(2, 4, 30, 8, 128).transpose(4, 0, 3, 1, 2))
        m['ptab'] = np.ascontiguousarray(np.asarray(inp['page_table'], np.int32)[4 * core:4 * core + 4].reshape(1, 256))
        maps.append(m)
    return maps


def assemble(results, ncores):
    f = np.float32
    y_p = np.zeros((ncores, 2048, 1024), f)
    y_s = np.zeros((4 * ncores, 1, 1024), f)
    nk_p = np.zeros((2, ncores, 2048, 16, 32), f)
    nv_p = np.zeros((2, ncores, 2048, 8, 64), f)
    nk_s = np.zeros((2, 4 * ncores, 1, 16, 32), f)
    nv_s = np.zeros((2, 4 * ncores, 1, 8, 64), f)
    cb_p = np.zeros((2, ncores, 2, 512), f)
    cb_s = np.zeros((2, 4 * ncores, 2, 512), f)
    cc_p = np.zeros((2, ncores, 30, 1024), f)
    cc_s = np.zeros((2, 4 * ncores, 30, 1024), f)
    for ci, r in enumerate(results):
        y = r['yT'].transpose(2, 1, 0).reshape(NT, 1024)
        y_p[ci] = y[:2048]
        y_s[4 * ci:4 * ci + 4, 0] = y[2048:]
        for e in range(2):
            k = r['nkT'][e].transpose(2, 1, 0).reshape(NT, 512)
            v = r['nvT'][e].transpose(2, 1, 0).reshape(NT, 512)
            nk_p[e, ci] = k[:2048].reshape(2048, 16, 32)
            nv_p[e, ci] = v[:2048].reshape(2048, 8, 64)
            nk_s[e, 4 * ci:4 * ci + 4, 0] = k[2048:].reshape(4, 16, 32)
            nv_s[e, 4 * ci:4 * ci + 4, 0] = v[2048:].reshape(4, 8, 64)
            cb_p[e, ci] = r['cbpT'][e].transpose(2, 1, 0).reshape(2, 512)
            cb_s[e, 4 * ci:4 * ci + 4, 0] = r['cbs0T'][e].transpose(2, 1, 0).reshape(4, 512)
            cb_s[e, 4 * ci:4 * ci + 4, 1] = r['cbs1T'][e].transpose(2, 1, 0).reshape(4, 512)
            cc_p[e, ci] = r['ccpT'][e].transpose(2, 1, 0).reshape(30, 1024)
            cc_s[e, 4 * ci:4 * ci + 4, 0:29] = r['ccsoT'][e].transpose(2, 3, 1, 0).reshape(4, 29, 1024)
            cc_s[e, 4 * ci:4 * ci + 4, 29] = r['ccsnT'][e].transpose(2, 1, 0).reshape(4, 1024)
    return (y_p, y_s, nk_p, nv_p, nk_s, nv_s, cb_p, cb_s, cc_p, cc_s)


def kernel(**inputs):
    nc = build_nc(DEPTH)
    maps = make_in_maps(inputs, list(range(NCORES)))
    res = run_bass_kernel_spmd(nc, maps, core_ids=list(range(NCORES)))
    return assemble(res.results, NCORES)
```

```python
import bisect
import contextlib
import math
import numpy as np
import concourse.bass as bass
import concourse.mybir as mybir
from concourse.bass_utils import run_bass_kernel_spmd

F32 = mybir.dt.float32
BF16 = mybir.dt.bfloat16
I32 = mybir.dt.int32
U8 = mybir.dt.uint8
AF = mybir.ActivationFunctionType
ALU = mybir.AluOpType
AX = mybir.AxisListType

D = 1024
NPROMPT = 2048
NSAMP = 4
NT = NPROMPT + NSAMP
DEPTH = 4
DFF = 2816
NFF = DFF // 128
GFF = 11
EPS = 1e-6
FT = [(342 * i, 342 * (i + 1)) for i in range(6)]
MT = [(0, 512), (512, 1024), (1024, 1536), (1536, 2048), (2048, 2052)]


class IMap:
    def __init__(self):
        self.starts = [0]
        self.data = {0: [1 << 40, None, []]}

    def _split(self, x):
        i = bisect.bisect_right(self.starts, x) - 1
        s = self.starts[i]
        e, w, r = self.data[s]
        if s == x or x >= e:
            return
        self.data[s] = [x, w, list(r)]
        self.data[x] = [e, w, list(r)]
        bisect.insort(self.starts, x)

    def access(self, a, b, oid, is_write, deps):
        self._split(a)
        self._split(b)
        i = bisect.bisect_left(self.starts, a)
        n = len(self.starts)
        while i < n and self.starts[i] < b:
            seg = self.data[self.starts[i]]
            if is_write:
                if seg[1] is not None:
                    deps.add((seg[1], 'waw'))
                for r in seg[2]:
                    deps.add((r, 'war'))
                seg[1] = oid
                seg[2] = []
            else:
                if seg[1] is not None:
                    deps.add((seg[1], 'raw'))
                seg[2].append(oid)
            i += 1


class Prog:
    COMPUTE = ('pe', 'act', 'dve', 'pool')
    ENGS = ('pe', 'act', 'dve', 'pool', 'sp')

    def __init__(self, nc):
        self.nc = nc
        self.ops = []
        self.buf = {}
        self.imaps = {}
        self.chan_last = {}
        self.chan_cnt = {}

    def _acc(self, k, oid, is_write, deps):
        if isinstance(k, tuple) and k and k[0] == 'iv':
            _, space, a, b = k
            self.imaps.setdefault(space, IMap()).access(a, b, oid, is_write, deps)
            return
        st = self.buf.setdefault(k, [None, []])
        if is_write:
            if st[0] is not None:
                deps.add((st[0], 'waw'))
            for r in st[1]:
                deps.add((r, 'war'))
            self.buf[k] = [oid, []]
        else:
            if st[0] is not None:
                deps.add((st[0], 'raw'))
            st[1].append(oid)

    def op(self, eng, fn, reads=(), writes=(), chan=None):
        oid = len(self.ops)
        deps = set()
        is_dma = chan is not None
        for k in reads:
            self._acc(k, oid, False, deps)
        for k in writes:
            self._acc(k, oid, True, deps)
        if is_dma and chan in self.chan_last:
            deps.add((self.chan_last[chan], 'chan'))
        dma_idx = None
        if is_dma:
            self.chan_last[chan] = oid
            dma_idx = self.chan_cnt.get(chan, 0) + 1
            self.chan_cnt[chan] = dma_idx
        fdeps = set()
        for (p, kind) in deps:
            if p == oid:
                continue
            po = self.ops[p]
            if po['chan'] is None and not is_dma and po['eng'] == eng:
                if eng == 'pe' or kind != 'raw':
                    continue
            fdeps.add(p)
        self.ops.append(dict(eng=eng, fn=fn, deps=fdeps, chan=chan, dma_idx=dma_idx, sig=False, sig_idx=None))
        return oid

    def emit(self):
        nc = self.nc
        ops = self.ops
        for o in ops:
            for p in o['deps']:
                ops[p]['sig'] = True
        cnt = {e: 0 for e in self.COMPUTE}
        for o in ops:
            if o['chan'] is None and o['sig']:
                cnt[o['eng']] += 1
                o['sig_idx'] = cnt[o['eng']]
        chans = sorted(self.chan_cnt.keys(), key=str)
        with contextlib.ExitStack() as es:
            sem = {}
            for e in self.COMPUTE:
                sem[('e', e)] = es.enter_context(nc.semaphore('s_' + e))
            for ci, c in enumerate(chans):
                sem[('c', c)] = es.enter_context(nc.semaphore('c%d' % ci))
            block = es.enter_context(nc.Block())
            by_eng = {e: [] for e in self.ENGS}
            for i, o in enumerate(ops):
                by_eng[o['eng']].append(i)

            def run(engname, engobj):
                waited = {}
                for i in by_eng[engname]:
                    o = ops[i]
                    need = {}
                    for p in o['deps']:
                        po = ops[p]
                        if po['chan'] is None:
                            k = ('e', po['eng'])
                            v = po['sig_idx']
                        else:
                            k = ('c', po['chan'])
                            v = 16 * po['dma_idx']
                        if need.get(k, 0) < v:
                            need[k] = v
                    for k, v in need.items():
                        if waited.get(k, 0) >= v:
                            continue
                        engobj.wait_ge(sem[k], v)
                        waited[k] = v
                    ins = o['fn'](engobj)
                    if o['chan'] is not None:
                        ins.then_inc(sem[('c', o['chan'])], 16)
                    elif o['sig']:
                        ins.then_inc(sem[('e', engname)], 1)
                if engname == 'sp':
                    for c in chans:
                        engobj.wait_ge(sem[('c', c)], 16 * self.chan_cnt[c])

            block.tensor(lambda e: run('pe', e))
            block.scalar(lambda e: run('act', e))
            block.vector(lambda e: run('dve', e))
            block.gpsimd(lambda e: run('pool', e))
            block.sync(lambda e: run('sp', e))


class Ten:
    def __init__(self, big, off, n0, n1, dt, esz):
        assert off % 32 == 0
        self.off, self.n0, self.n1, self.esz = off, n0, n1, esz
        self.nbytes = n0 * n1 * esz
        self.v = big[:, off:off + self.nbytes].bitcast(dt).rearrange("p (a b) -> p a b", b=n1)

    def r(self, i=None, lo=0, hi=None, i1=None):
        if hi is None:
            hi = self.n1
        if i is None:
            i, i1 = 0, self.n0
        elif i1 is None:
            i1 = i + 1
        if lo == 0 and hi == self.n1:
            return [('iv', 'sb', self.off + i * self.n1 * self.esz, self.off + i1 * self.n1 * self.esz)]
        return [('iv', 'sb', self.off + (j * self.n1 + lo) * self.esz, self.off + (j * self.n1 + hi) * self.esz)
                for j in range(i, i1)]


class Alloc:
    def __init__(self, big, limit):
        self.big, self.limit, self.cur = big, limit, 0

    def ten(self, n0, n1, dt):
        esz = {F32: 4, BF16: 2, I32: 4}[dt]
        t = Ten(self.big, self.cur, n0, n1, dt, esz)
        self.cur += (t.nbytes + 31) // 32 * 32
        assert self.cur <= self.limit, (self.cur, self.limit)
        return t

    def at(self, off, n0, n1, dt):
        esz = {F32: 4, BF16: 2, I32: 4}[dt]
        t = Ten(self.big, off, n0, n1, dt, esz)
        assert off + t.nbytes <= self.limit
        return t


NEG = -30000.0
SCALE = 32 ** -0.5
NTAB = 639
TS = 4
NPAGES = 64


def t5_bucket_np(n):
    n = np.asarray(n, np.int64)
    nn = np.maximum(n, 0)
    nf = np.maximum(nn, 1).astype(np.float32)
    large = 16 + (np.log(nf / np.float32(16)) / np.float32(math.log(128 / 16)) * np.float32(16)).astype(np.int32)
    large = np.minimum(large, 31)
    return np.where(nn < 16, nn, large).astype(np.int64)


def static_tables():
    consts = np.zeros((128, 7, 128), np.float32)
    consts[:, 0] = np.eye(128)
    consts[:, 1] = 1.0 / 1024
    consts[:, 2] = np.kron(np.eye(4), np.ones((32, 32))) / 32
    consts[:, 3] = np.eye(128)[::-1]
    consts[:, 4] = 1.0
    consts[:, 5, 0:64] = 1.0
    consts[:, 6, 64:128] = 1.0
    lay = {}
    cur = 0

    def add(name, n):
        nonlocal cur
        lay[name] = (cur, n)
        cur += n
    add('iota', 1); add('sgn0', 1); add('sgn1', 1); add('maskd', 512); add('pair', 8); add('maskp', 128); add('selc', 4); add('iota32', 32); add('negm', 1)
    c2 = np.zeros((128, cur), np.float32)
    c2[:, lay['iota'][0]] = np.arange(128)
    c2[:, lay['iota32'][0]:lay['iota32'][0] + 32] = np.arange(32)[None, :]
    c2[63, lay['negm'][0]] = NEG
    c2[127, lay['negm'][0]] = NEG
    for base in (0, 32):
        for hs in range(16):
            c2[base + hs, lay['sgn0'][0]] = 1.0 if hs % 2 == 0 else 0.0
            c2[base + hs, lay['sgn1'][0]] = -1.0 if hs % 2 == 1 else 0.0
            h = hs // 2
            c2[base + hs, lay['maskd'][0] + h * 64: lay['maskd'][0] + (h + 1) * 64] = 1.0
            c2[base + hs, lay['pair'][0] + h] = 1.0
    for h in range(8):
        r = h % 2
        c2[h, lay['maskp'][0] + r * 64: lay['maskp'][0] + (r + 1) * 64] = 1.0
        c2[h, lay['selc'][0] + h // 2] = 1.0
    ds_ = np.zeros((33, NTAB + 256), np.float32)
    for i in range(NTAB):
        n = i - 255
        if n < 0:
            ds_[32, i] = NEG
        else:
            ds_[int(t5_bucket_np(n)), i] += 1.0
            ds_[31, i] -= 1.0
    for p in range(128):
        ds_[int(t5_bucket_np(128 - p)), NTAB + p] += 1.0
        ds_[31, NTAB + p] -= 1.0
    ds_[0, NTAB + 128] += 1.0
    ds_[31, NTAB + 128] -= 1.0
    ds_[32, NTAB + 129:NTAB + 256] = NEG
    return consts, c2, lay, ds_


def vec_layout():
    lay = {}
    cur = 0

    def add(name, n):
        nonlocal cur
        lay[name] = (cur, n)
        cur += n
    for l in range(DEPTH):
        add(('n1', l), 8)
        add(('nm', l), 8)
        add(('n2', l), 8)
    for o in range(2):
        add(('bpw1', o), 16)
        add(('ccb', o), 8)
        add(('lng', o), 8)
        add(('lnb', o), 8)
        add(('bpw2', o), 8)
        add(('ccw', o), 31 * 8)
    for e in range(2):
        add(('cbw', e), 3 * 4)
        add(('qg', e), 1)
        add(('kg', e), 1)
    return lay, cur


def build_vecs(inp):
    lay, ncol = vec_layout()
    V = np.zeros((128, ncol), np.float32)

    def put(name, vec):
        c0, n = lay[name]
        V[:, c0:c0 + n] = np.asarray(vec, np.float32).reshape(n, 128).T
    for l in range(DEPTH):
        put(('n1', l), inp['norm_ffn1'][l])
        put(('nm', l), inp['norm_mix'][l])
        put(('n2', l), inp['norm_ffn2'][l])
    for o in range(2):
        put(('bpw1', o), inp['b_pw1'][o])
        put(('ccb', o), inp['conv_c_b'][o])
        put(('lng', o), inp['ln_c_gain'][o])
        put(('lnb', o), inp['ln_c_bias'][o])
        put(('bpw2', o), inp['b_pw2'][o])
        put(('ccw', o), np.asarray(inp['conv_c_w'][o]).reshape(-1))
    for e in range(2):
        put(('cbw', e), np.asarray(inp['conv_b_w'][e]).reshape(-1))
        put(('qg', e), np.tile(np.asarray(inp['q_norm_gain'][e]), 4))
        put(('kg', e), np.tile(np.asarray(inp['k_norm_gain'][e]), 4))
    return V


def build_nc(depth=DEPTH):
    nc = bass.Bass("TRN2", target_bir_lowering=False)
    lay, ncol = vec_layout()
    _, _, lay2, _ = static_tables()
    nc2 = sum(v[1] for v in lay2.values())
    n_even = (depth + 1) // 2
    n_odd = depth // 2

    def din(name, shape, dtype=F32):
        return nc.dram_tensor(name, list(shape), dtype, kind="ExternalInput").ap()

    def dout(name, shape, dtype=F32):
        return nc.dram_tensor(name, list(shape), dtype, kind="ExternalOutput").ap()

    xT = din("xT", [128, 8, NT])
    vecsT = din("vecsT", [128, ncol])
    consts = din("consts", [128, 7, 128])
    cst2 = din("cst2", [128, nc2])
    dstat = din("dstat", [33, NTAB + 256])
    relb = din("relb", [33, 8])
    lamv = din("lamv", [2, 4, 32])
    sgrow = din("sgrow", [2, 64])
    scb = din("scb", [128, 2, 4, 2, 4])
    scc = din("scc", [128, 2, 8, 4, 30])
    ptab = din("ptab", [1, NSAMP * NPAGES], I32)
    ccwrep = din("ccwrep", [2, 8, 128, 4, 32])
    cache_k = [din("cache_k%d" % i, [2560 * 128, 512]) for i in range(2)]
    cache_v = [din("cache_v%d" % i, [2560 * 128, 512]) for i in range(2)]
    NSTEP = 128 // TS
    wg = [din("ffn1_w_gate", [DEPTH, D, DFF]), din("ffn2_w_gate", [DEPTH, D, DFF])]
    wu = [din("ffn1_w_up", [DEPTH, D, DFF]), din("ffn2_w_up", [DEPTH, D, DFF])]
    wd = [din("ffn1_w_down", [DEPTH, DFF, D]), din("ffn2_w_down", [DEPTH, DFF, D])]
    w_in = din("w_in_even", [2, D, 3072])
    w_out = din("w_out_even", [2, D, D])
    w_pw1 = din("w_pw1", [2, D, 2048])
    w_pw2 = din("w_pw2", [2, D, D])
    yT = dout("yT", [128, 8, NT])
    nkT = dout("nkT", [2, 128, 4, NT])
    nvT = dout("nvT", [2, 128, 4, NT])
    cbpT = dout("cbpT", [2, 128, 4, 2])
    cbs0T = dout("cbs0T", [2, 128, 4, 4])
    cbs1T = dout("cbs1T", [2, 128, 4, 4])
    ccpT = dout("ccpT", [2, 128, 8, 30])
    ccsoT = dout("ccsoT", [2, 128, 8, 4, 29])
    ccsnT = dout("ccsnT", [2, 128, 8, 4])
    tabd = nc.dram_tensor("tabd", [8, NTAB], F32)

    P = Prog(nc)
    with contextlib.ExitStack() as es:
        LIMIT = 212800
        big = es.enter_context(nc.sbuf_tensor("big", [128, LIMIT], U8))
        PSA = es.enter_context(nc.psum_tensor("psa", [128, 4096], F32))
        A = Alloc(big, LIMIT)
        X = A.ten(8, NT, F32)
        XN = A.ten(8, NT, BF16)
        VEC = A.ten(1, ncol, F32)
        CF = A.ten(7, 128, F32)
        CB = A.ten(5, 128, BF16)
        C2 = A.ten(1, nc2, F32)
        RSTD = A.ten(1, NT, F32)
        EPSC = A.ten(1, 8, F32)
        LAMIN = A.ten(2, 128, F32)
        LAM = A.ten(2, 8, F32)
        SGR = A.ten(2, 64, F32)
        BDEC = A.ten(2, 8, F32)
        LAMC = A.ten(2, 1, F32)
        SCB = A.ten(2, 32, F32)
        IDX63 = A.ten(1, 8, I32)
        QS = A.ten(4, 4, F32)
        KS = A.ten(4, 4, F32)
        VS = A.ten(4, 4, F32)
        IDX2 = A.ten(2, 32, I32)
        NHS = 8
        HS = [A.ten(8, 128, BF16) for _ in range(NHS)]
        SG = [A.ten(1, 342, F32) for _ in range(3)]
        R0 = A.cur
        WD = A.ten(GFF, 1024, BF16)
        H = A.ten(GFF, NT, BF16)
        assert A.cur <= LIMIT
        RSIZE = LIMIT - R0
        SQH = A.at(H.off, 8, 512, BF16)

        def bk(b, n0=0, n1=512):
            return PSA[:, b * 512 + n0: b * 512 + n1]

        def PS(*bs):
            return [('ps', b) for b in bs]

        def vcol(name, j=0):
            return VEC.v[:, 0, lay[name][0] + j: lay[name][0] + j + 1]

        def c2v(name, p0, p1):
            c0, n = lay2[name]
            return C2.v[p0:p1, 0, c0:c0 + n]

        ident = CF.v[:, 0, :]
        ones_f = CF.v[:, 4, :]
        Jf = CF.v[:, 3, :]
        mean_b = CB.v[:, 1, :]
        blk_b = CB.v[:, 2, :]
        ident_b = CB.v[:, 0, :]
        ones_b = CB.v[:, 4, :]

        P.op('sp', lambda e: e.dma_start(out=X.v, in_=xT), writes=X.r(), chan='ld0')
        P.op('sp', lambda e: e.dma_start(out=VEC.v[:, 0, :], in_=vecsT), writes=VEC.r(), chan='ld')
        P.op('sp', lambda e: e.dma_start(out=CF.v, in_=consts), writes=CF.r(), chan='ld')
        P.op('sp', lambda e: e.dma_start(out=C2.v[:, 0, :], in_=cst2), writes=C2.r(), chan='ld')
        P.op('dve', lambda e: e.tensor_copy(CB.v, CF.v[:, 0:5, :]), reads=CF.r(), writes=CB.r())
        P.op('dve', lambda e: e.memset(EPSC.v[:, 0, :], EPS), writes=EPSC.r())
        epsc = EPSC.v[:, 0, 0:1]

        hs_ctr = [0]

        def load_col(w2d, col0):
            i = hs_ctr[0] % NHS
            hs_ctr[0] += 1
            src = w2d.rearrange("(kc p) f -> p kc f", p=128)[:, :, col0:col0 + 128]
            t = HS[i]
            P.op('pool', lambda e: e.dma_start(out=t.v, in_=src), writes=t.r(), chan=('hs', i))
            return t

        class WStream:
            def __init__(self, items, depth):
                self.items, self.i, self.q, self.depth = list(items), 0, [], depth

            def fill(self):
                while self.i < len(self.items) and len(self.q) < self.depth:
                    self.q.append(load_col(*self.items[self.i]))
                    self.i += 1

            def get(self):
                self.fill()
                h = self.q.pop(0)
                self.fill()
                return h

        def proj(hs, srcs, b, lo, hi):
            n = hi - lo
            for kc in range(8):
                t, row = srcs[kc]
                P.op('pe', lambda e, kc=kc, t=t, row=row: e.matmul(
                    bk(b, 0, n), hs.v[:, kc, :], t.v[:, row, lo:hi], start=(kc == 0), stop=(kc == 7)),
                    reads=hs.r(kc) + t.r(row, lo, hi), writes=PS(b))

        XNS = [(XN, kc) for kc in range(8)]

        nrm_ctr = [0]

        def rmsnorm(gname, tiles):
            SQ = SQH
            for (lo, hi) in tiles:
                n = hi - lo
                b = 6 + nrm_ctr[0] % 2
                nrm_ctr[0] += 1
                P.op('act', lambda e, lo=lo, hi=hi, n=n: e.activation(SQ.v[:, :, 0:n], X.v[:, :, lo:hi], AF.Square),
                     reads=X.r(None, lo, hi), writes=SQ.r(None, 0, n))
                for kc in range(8):
                    P.op('pe', lambda e, kc=kc, n=n, b=b: e.matmul(bk(b, 0, n), mean_b, SQ.v[:, kc, 0:n],
                                                                  start=(kc == 0), stop=(kc == 7)),
                         reads=SQ.r(kc, 0, n) + CB.r(1), writes=PS(b))
                P.op('act', lambda e, lo=lo, hi=hi, n=n, b=b: e.activation(
                    RSTD.v[:, 0, lo:hi], bk(b, 0, n), AF.Sqrt, bias=epsc),
                    reads=PS(b) + EPSC.r(), writes=RSTD.r(0, lo, hi))
                P.op('dve', lambda e, lo=lo, hi=hi: e.reciprocal(RSTD.v[:, 0, lo:hi], RSTD.v[:, 0, lo:hi]),
                     reads=RSTD.r(0, lo, hi), writes=RSTD.r(0, lo, hi))
                for c in range(8):
                    P.op('dve', lambda e, c=c, lo=lo, hi=hi: e.scalar_tensor_tensor(
                        XN.v[:, c, lo:hi], X.v[:, c, lo:hi], vcol(gname, c), RSTD.v[:, 0, lo:hi],
                        ALU.mult, ALU.mult),
                        reads=X.r(c, lo, hi) + RSTD.r(0, lo, hi) + VEC.r(), writes=XN.r(c, lo, hi))

        def ffn(which, l):
            gu_ctr = 0
            sgc = 0
            dn_ctr = 0
            pend = []
            nxt = [0]

            def prefetch(upto):
                while nxt[0] < min(upto, NFF):
                    f = nxt[0]
                    pend.append((load_col(wg[which][l], f * 128), load_col(wu[which][l], f * 128)))
                    nxt[0] += 1
            for g in range(2):
                prefetch(g * GFF + 2)
                for j in range(GFF):
                    f = g * GFF + j
                    src = wd[which][l][f * 128:(f + 1) * 128, :]
                    P.op('pool', lambda e, j=j, src=src: e.dma_start(out=WD.v[:, j, :], in_=src),
                         writes=WD.r(j), chan=('wd', j % 4))
                for j in range(GFF):
                    f = g * GFF + j
                    prefetch(f + 3)
                    hg, hu = pend.pop(0)
                    for (lo, hi) in FT:
                        n = hi - lo
                        bg = (gu_ctr % 3) * 2
                        bu = bg + 1
                        gu_ctr += 1
                        proj(hg, XNS, bg, lo, hi)
                        proj(hu, XNS, bu, lo, hi)
                        sg = SG[sgc % 3]
                        sgc += 1
                        P.op('act', lambda e, n=n, bg=bg, sg=sg: e.activation(sg.v[:, 0, 0:n], bk(bg, 0, n), AF.Silu),
                             reads=PS(bg), writes=sg.r())
                        P.op('dve', lambda e, n=n, bu=bu, sg=sg, j=j, lo=lo, hi=hi: e.tensor_tensor(
                            H.v[:, j, lo:hi], sg.v[:, 0, 0:n], bk(bu, 0, n), ALU.mult),
                            reads=sg.r() + PS(bu), writes=H.r(j, lo, hi))
                for (lo, hi) in FT:
                    n = hi - lo
                    for dch in range(8):
                        b = 6 + (dn_ctr % 2)
                        dn_ctr += 1
                        for j in range(GFF):
                            P.op('pe', lambda e, j=j, dch=dch, lo=lo, hi=hi, n=n, b=b: e.matmul(
                                bk(b, 0, n), WD.v[:, j, dch * 128:(dch + 1) * 128], H.v[:, j, lo:hi],
                                start=(j == 0), stop=(j == GFF - 1)),
                                reads=WD.r(j) + H.r(j, lo, hi), writes=PS(b))
                        P.op('dve', lambda e, dch=dch, lo=lo, hi=hi, n=n, b=b: e.scalar_tensor_tensor(
                            X.v[:, dch, lo:hi], bk(b, 0, n), 0.5, X.v[:, dch, lo:hi], ALU.mult, ALU.add),
                            reads=PS(b) + X.r(dch, lo, hi), writes=X.r(dch, lo, hi))

        def even_setup():
            B0 = Alloc(big, LIMIT)
            B0.cur = R0
            RB = B0.ten(1, 8, F32)
            DS = B0.ten(1, NTAB + 256, F32)
            TB = B0.ten(1, NTAB, F32)
            PTI = B0.ten(1, NSAMP * NPAGES, I32)
            PTF = B0.ten(1, NSAMP * NPAGES, F32)
            PRD = B0.ten(2, 32, F32)
            P63 = B0.ten(1, 8, F32)
            PT2I = B0.ten(1, 8, I32)
            PT2F = B0.ten(1, 8, F32)
            P2F = B0.ten(1, 32, F32)
            P.op('sp', lambda e: e.dma_start(out=RB.v[0:33, 0, :], in_=relb), writes=RB.r(), chan='ld')
            P.op('sp', lambda e: e.dma_start(out=DS.v[0:33, 0, :], in_=dstat), writes=DS.r(), chan='ld')
            for (c0, c1) in ((0, 512), (512, NTAB)):
                P.op('pe', lambda e, c0=c0, c1=c1: e.matmul(bk(0, 0, c1 - c0)[0:8, :], RB.v[0:33, 0, :], DS.v[0:33, 0, c0:c1],
                                                           start=True, stop=True),
                     reads=RB.r() + DS.r(), writes=PS(0))
                P.op('dve', lambda e, c0=c0, c1=c1: e.tensor_copy(TB.v[0:8, 0, c0:c1], bk(0, 0, c1 - c0)[0:8, :]),
                     reads=PS(0), writes=TB.r(0, c0, c1))
            P.op('sp', lambda e: e.dma_start(out=tabd.ap(), in_=TB.v[0:8, 0, :]), reads=TB.r(), writes=['tabd'], chan='ld')
            for k in range(2):
                P.op('pe', lambda e, k=k: e.matmul(bk(1, 0, 8), DS.v[0:33, 0, NTAB + 128 * k: NTAB + 128 * (k + 1)],
                                                  RB.v[0:33, 0, :], start=True, stop=True),
                     reads=RB.r() + DS.r(), writes=PS(1))
                P.op('dve', lambda e, k=k: e.tensor_copy(BDEC.v[:, k, :], bk(1, 0, 8)), reads=PS(1), writes=BDEC.r(k))
            P.op('sp', lambda e: e.dma_start(
                out=LAMIN.v, in_=bass.AP(lamv.tensor, 0, [[0, 128], [128, 2], [1, 128]])), writes=LAMIN.r(), chan='ld')
            P.op('sp', lambda e: e.dma_start(
                out=SGR.v, in_=bass.AP(sgrow.tensor, 0, [[0, 128], [64, 2], [1, 64]])), writes=SGR.r(), chan='ld')
            P.op('sp', lambda e: e.dma_start(out=SCB.v, in_=scb.rearrange("p e c k s -> p e (c k s)")),
                 writes=SCB.r(), chan='ld')
            P.op('sp', lambda e: e.dma_start(
                out=PTI.v[:, 0, :], in_=bass.AP(ptab.tensor, 0, [[0, 128], [1, NSAMP * NPAGES]])),
                writes=PTI.r(), chan='ld')
            P.op('dve', lambda e: e.tensor_copy(PTF.v, PTI.v), reads=PTI.r(), writes=PTF.r())
            P.op('dve', lambda e: e.tensor_scalar(
                P63.v[:, 0, 0:4], PTF.v[:, 0, NPAGES - 1:NSAMP * NPAGES:NPAGES], 128.0, c2v('iota', 0, 128), ALU.mult, ALU.add),
                reads=PTF.r() + C2.r(), writes=P63.r())
            P.op('dve', lambda e: e.tensor_copy(IDX63.v[:, 0, 0:4], P63.v[:, 0, 0:4]), reads=P63.r(), writes=IDX63.r())
            for a in range(2):
                P.op('sp', lambda e, a=a: e.dma_start(out=PT2I.v[:, 0, a:a + 1], in_=bass.AP(ptab.tensor, a * 128, [[1, 128], [1, 1]])),
                     writes=PT2I.r(), chan='ld')
            P.op('dve', lambda e: e.tensor_copy(PT2F.v[:, 0, 0:2], PT2I.v[:, 0, 0:2]), reads=PT2I.r(), writes=PT2F.r())
            for a in range(2):
                P.op('dve', lambda e, a=a: e.scalar_tensor_tensor(
                    P2F.v[:, 0, :], c2v('iota32', 0, 128), 1.0 / NSTEP, PT2F.v[:, 0, a:a + 1].to_broadcast([128, 32]), ALU.mult, ALU.add),
                    reads=PT2F.r() + C2.r(), writes=P2F.r())
                P.op('dve', lambda e: e.tensor_scalar(P2F.v[:, 0, :], P2F.v[:, 0, :], float(NSTEP), None, ALU.mult),
                     reads=P2F.r(), writes=P2F.r())
                P.op('dve', lambda e, a=a: e.tensor_copy(IDX2.v[:, a, :], P2F.v[:, 0, :]), reads=P2F.r(), writes=IDX2.r(a))
            for e_ in range(n_even):
                lam_init = 0.8 - 0.6 * math.exp(-0.3 * (2 * e_))
                for k in range(2):
                    P.op('dve', lambda e, e_=e_, k=k: e.tensor_tensor(
                        PRD.v[:, k, :], LAMIN.v[:, e_, 64 * k:64 * k + 32], LAMIN.v[:, e_, 64 * k + 32:64 * k + 64],
                        ALU.mult), reads=LAMIN.r(), writes=PRD.r(k))
                P.op('dve', lambda e, e_=e_: e.tensor_reduce(LAM.v[:, e_, 2:4], PRD.v, AX.X, ALU.add),
                     reads=PRD.r(), writes=LAM.r(e_))
                P.op('act', lambda e, e_=e_: e.activation(LAM.v[:, e_, 2:4], LAM.v[:, e_, 2:4], AF.Exp),
                     reads=LAM.r(e_), writes=LAM.r(e_))
                P.op('dve', lambda e, e_=e_: e.tensor_tensor(LAM.v[:, e_, 0:1], LAM.v[:, e_, 2:3], LAM.v[:, e_, 3:4],
                                                            ALU.subtract), reads=LAM.r(e_), writes=LAM.r(e_))
                P.op('dve', lambda e, e_=e_, li=lam_init: e.tensor_scalar(
                    LAM.v[:, e_, 0:1], LAM.v[:, e_, 0:1], li, None, ALU.add), reads=LAM.r(e_), writes=LAM.r(e_))
                P.op('dve', lambda e, e_=e_: e.tensor_scalar(LAM.v[:, e_, 1:2], LAM.v[:, e_, 0:1], -1.0, None, ALU.mult),
                     reads=LAM.r(e_), writes=LAM.r(e_))
                P.op('dve', lambda e, e_=e_, li=lam_init: e.tensor_scalar(SGR.v[:, e_, :], SGR.v[:, e_, :], 1.0 - li, None, ALU.mult),
                     reads=SGR.r(e_), writes=SGR.r(e_))
                P.op('dve', lambda e, e_=e_: e.scalar_tensor_tensor(
                    LAMC.v[0:48, e_, :], c2v('sgn1', 0, 48), LAM.v[0:48, e_, 0:1], c2v('sgn0', 0, 48), ALU.mult, ALU.add),
                    reads=LAM.r(e_) + C2.r(), writes=LAMC.r(e_))

        def even_mixer(l):
            e_ = l // 2
            lam_init = 0.8 - 0.6 * math.exp(-0.3 * l)
            B = Alloc(big, LIMIT)
            B.cur = R0
            ATT = B.ten(4, NT, BF16)
            CATB = B.ten(4, NT, BF16)
            S0 = B.cur
            QN = B.ten(1, NT, BF16)
            KN = B.ten(1, NT, BF16)
            VA = B.ten(16, 130, BF16)
            ECB = B.ten(2, 512, BF16)
            ZS = [B.ten(1, 512, F32) for _ in range(2)]
            SQB = [B.ten(1, 512, BF16) for _ in range(2)]
            RQ = [B.ten(1, 512, F32) for _ in range(2)]
            KF = [B.ten(1, 512, F32) for _ in range(2)]
            PT = [B.ten(2, 256, BF16) for _ in range(3)]
            E2 = B.at(KF[0].off, 2, 512, F32)
            assert KF[1].off == KF[0].off + 2048
            B2 = Alloc(big, SG[2].off + 1376)
            B2.cur = SG[0].off
            ATOK = [B2.ten(1, 256, F32) for _ in range(2)]
            OS = B2.ten(4, 130, F32)
            A1 = B.ten(2, 128, F32)
            RR = B.ten(1, 16, F32)
            S1 = B.cur
            w2d = w_in[e_]
            qg, kg = vcol(('qg', e_)), vcol(('kg', e_))
            nlam = LAM.v[:, e_, 1:2]

            items = []
            for c in range(4):
                items += [(w2d, c * 128), (w2d, (4 + c) * 128), (w2d, (8 + c) * 128)]
            for i in range(4):
                items += [(w2d, (16 + i) * 128), (w2d, (20 + i) * 128), (w2d, (12 + i) * 128)]
            for dch in range(8):
                items += [(w_out[e_], dch * 128)]
            ws = WStream(items, 4)
            ws.fill()
            rmsnorm(('nm', l), MT)
            P.op('pool', lambda e: e.memset(VA.v, 1.0), writes=VA.r())

            zc = [0]
            tc_ = [0]
            for c in range(4):
                hq = ws.get()
                hk = ws.get()
                hv = ws.get()
                srcE = bass.AP(tabd, 2 * c * NTAB, [[1, 128], [NTAB, 2], [1, 512]])
                P.op('sp', lambda e, srcE=srcE: e.dma_start(out=E2.v, in_=srcE), reads=['tabd'], writes=E2.r(), chan='e2')
                for hh in range(2):
                    P.op('pe', lambda e, hh=hh: e.matmul(bk(7), Jf, E2.v[:, hh, :], start=True, stop=True),
                         reads=E2.r(hh) + CF.r(3), writes=PS(7))
                    P.op('act', lambda e, hh=hh: e.activation(ECB.v[:, hh, :], bk(7), AF.Exp), reads=PS(7), writes=ECB.r(hh))
                tl = [(kind, hs, ti, lo, hi) for kind, hs in (('q', hq), ('k', hk), ('v', hv)) for ti, (lo, hi) in enumerate(MT)]
                tinfo = {}

                def st1(t):
                    kind, hs, ti, lo, hi = tl[t]
                    n = hi - lo
                    b = zc[0] % 4
                    zi = zc[0] % 2
                    zc[0] += 1
                    tinfo[t] = (b, zi)
                    proj(hs, XNS, b, lo, hi)
                    if kind in 'qk':
                        zs, sqb = ZS[zi], SQB[zi]
                        P.op('act', lambda e, n=n, b=b, zs=zs: e.activation(zs.v[:, 0, 0:n], bk(b, 0, n), AF.Copy),
                             reads=PS(b), writes=zs.r(0, 0, n))
                        P.op('act', lambda e, n=n, b=b, sqb=sqb: e.activation(sqb.v[:, 0, 0:n], bk(b, 0, n), AF.Square),
                             reads=PS(b), writes=sqb.r(0, 0, n))
                    else:
                        vf = KF[zi]
                        P.op('dve', lambda e, n=n, b=b, vf=vf: e.tensor_copy(vf.v[:, 0, 0:n], bk(b, 0, n)),
                             reads=PS(b), writes=vf.r(0, 0, n))
                        P.op('sp', lambda e, n=n, lo=lo, hi=hi, vf=vf, c=c: e.dma_start(
                            out=nvT[e_, :, c, lo:hi], in_=vf.v[:, 0, 0:n]), reads=vf.r(0, 0, n), chan=('ok', zi))

                def st2(t):
                    kind, hs, ti, lo, hi = tl[t]
                    n = hi - lo
                    b, zi = tinfo[t]
                    if kind in 'qk':
                        zs, sqb, rq = ZS[zi], SQB[zi], RQ[zi]
                        b2 = 4 + zi
                        P.op('pe', lambda e, n=n, b2=b2, sqb=sqb: e.matmul(bk(b2, 0, n), blk_b, sqb.v[:, 0, 0:n],
                                                                          start=True, stop=True),
                             reads=sqb.r(0, 0, n) + CB.r(2), writes=PS(b2))
                        P.op('act', lambda e, n=n, b2=b2, rq=rq: e.activation(rq.v[:, 0, 0:n], bk(b2, 0, n), AF.Ln, bias=epsc),
                             reads=PS(b2) + EPSC.r(), writes=rq.r(0, 0, n))
                        P.op('act', lambda e, n=n, rq=rq: e.activation(rq.v[:, 0, 0:n], rq.v[:, 0, 0:n], AF.Exp, scale=-0.5),
                             reads=rq.r(0, 0, n), writes=rq.r(0, 0, n))
                        if kind == 'q':
                            P.op('dve', lambda e, n=n, lo=lo, hi=hi, zs=zs, rq=rq: e.scalar_tensor_tensor(
                                QN.v[:, 0, lo:hi], zs.v[:, 0, 0:n], qg, rq.v[:, 0, 0:n], ALU.mult, ALU.mult),
                                reads=zs.r(0, 0, n) + rq.r(0, 0, n) + VEC.r(), writes=QN.r(0, lo, hi))
                            if ti == 4:
                                P.op('dve', lambda e, n=n, zs=zs, rq=rq, c=c: e.scalar_tensor_tensor(
                                    QS.v[:, c, :], zs.v[:, 0, 0:n], qg, rq.v[:, 0, 0:n], ALU.mult, ALU.mult),
                                    reads=zs.r(0, 0, n) + rq.r(0, 0, n) + VEC.r(), writes=QS.r(c))
                        else:
                            kf = KF[zi]
                            P.op('dve', lambda e, n=n, zs=zs, rq=rq, kf=kf: e.scalar_tensor_tensor(
                                kf.v[:, 0, 0:n], zs.v[:, 0, 0:n], kg, rq.v[:, 0, 0:n], ALU.mult, ALU.mult),
                                reads=zs.r(0, 0, n) + rq.r(0, 0, n) + VEC.r(), writes=kf.r(0, 0, n))
                            P.op('pool', lambda e, n=n, lo=lo, hi=hi, kf=kf: e.tensor_copy(KN.v[:, 0, lo:hi], kf.v[:, 0, 0:n]),
                                 reads=kf.r(0, 0, n), writes=KN.r(0, lo, hi))
                            P.op('sp', lambda e, n=n, lo=lo, hi=hi, kf=kf, c=c: e.dma_start(
                                out=nkT[e_, :, c, lo:hi], in_=kf.v[:, 0, 0:n]), reads=kf.r(0, 0, n), chan=('ok', zi))
                            if ti == 4:
                                P.op('pool', lambda e, n=n, kf=kf, c=c: e.tensor_copy(KS.v[:, c, :], kf.v[:, 0, 0:n]),
                                     reads=kf.r(0, 0, n), writes=KS.r(c))
                    else:
                        vf = KF[zi]
                        if ti == 4:
                            P.op('pool', lambda e, n=n, vf=vf, c=c: e.tensor_copy(VS.v[:, c, :], vf.v[:, 0, 0:n]),
                                 reads=vf.r(0, 0, n), writes=VS.r(c))
                        else:
                            for bi in range(4):
                                kb = ti * 4 + bi
                                tb = 6 + tc_[0] % 2
                                tc_[0] += 1
                                P.op('pe', lambda e, bi=bi, tb=tb, vf=vf: e.transpose(
                                    bk(tb, 0, 128), vf.v[:, 0, bi * 128:(bi + 1) * 128], ident),
                                    reads=vf.r(0, bi * 128, (bi + 1) * 128) + CF.r(0), writes=PS(tb))
                                P.op('dve', lambda e, kb=kb, tb=tb: e.tensor_copy(
                                    VA.v[:, kb, :].rearrange("p (h x) -> p h x", x=65)[:, :, 0:64],
                                    bk(tb, 0, 128).rearrange("p (h x) -> p h x", x=64)),
                                    reads=PS(tb), writes=VA.r(kb))

                st1(0)
                for t in range(len(tl)):
                    if t + 1 < len(tl):
                        st1(t + 1)
                    st2(t)
                steps = [(qb, hh, kb) for qb in range(8) for hh in range(2) for kb in range(2 * qb + 2)]
                nst = len(steps)
                first_flag = {}

                def emit_S(i):
                    qb, hh, kb = steps[i]
                    q0, k0 = qb * 256, kb * 128
                    d = q0 - k0
                    col_lo = max(0, -d)
                    ncol = 256 - col_lo
                    sb = 2 * (i % 3)
                    pt = PT[i % 3]
                    for sub in range(2):
                        j = 2 * hh + sub
                        P.op('pe', lambda e, j=j, sb=sb, sub=sub, k0=k0, q0=q0, col_lo=col_lo, ncol=ncol: e.matmul(
                            bk(sb + sub, 0, ncol), KN.v[32 * j:32 * j + 32, 0, k0:k0 + 128],
                            QN.v[32 * j:32 * j + 32, 0, q0 + col_lo:q0 + 256], start=True, stop=True,
                            tile_position=(32 * j, 0)),
                            reads=KN.r(0, k0, k0 + 128) + QN.r(0, q0 + col_lo, q0 + 256), writes=PS(sb + sub))
                    pspair = PSA[:, sb * 512:(sb + 2) * 512].rearrange("p (s n) -> p s n", n=512)[:, :, 0:ncol]
                    P.op('act', lambda e, pspair=pspair, pt=pt, ncol=ncol: e.activation(
                        pt.v[:, :, 0:ncol], pspair, AF.Exp, scale=SCALE),
                        reads=PS(sb, sb + 1), writes=pt.r(None, 0, ncol))
                    if d < 256:
                        w0 = col_lo + d + 128
                        P.op('dve', lambda e, pt=pt, hh=hh, w0=w0, ncol=ncol: e.tensor_tensor(
                            pt.v[:, :, 0:ncol], pt.v[:, :, 0:ncol],
                            ECB.v[:, hh:hh + 1, w0:w0 + ncol].to_broadcast([128, 2, ncol]), ALU.mult),
                            reads=pt.r(None, 0, ncol) + ECB.r(hh), writes=pt.r(None, 0, ncol))

                def emit_PV(i):
                    qb, hh, kb = steps[i]
                    q0, k0 = qb * 256, kb * 128
                    col_lo = max(0, k0 - q0)
                    pt = PT[i % 3]
                    ob = 6 + hh
                    for sub in range(2):
                        for qs in range(2):
                            if 128 * qs < col_lo:
                                continue
                            c0 = 128 * qs - col_lo
                            last_kb = 2 * qb + qs
                            st = not first_flag.get((qb, hh), False)
                            first_flag[(qb, hh)] = True
                            gi_ = sub * 2 + qs
                            P.op('pe', lambda e, sub=sub, gi_=gi_, c0=c0, pt=pt, kb=kb, hh=hh, ob=ob, st=st, last_kb=last_kb: e.matmul(
                                bk(ob, gi_ * 65, (gi_ + 1) * 65), pt.v[:, sub, c0:c0 + 128],
                                VA.v[:, kb, hh * 65:(hh + 1) * 65], start=st, stop=(kb == last_kb),
                                skip_group_check=True),
                                reads=pt.r(sub, c0, c0 + 128) + VA.r(kb), writes=PS(ob))

                def norm_A(qb):
                    osv = OS.v[:, :, :].rearrange("p a (q x) -> p a q x", x=65)
                    P.op('dve', lambda e: e.tensor_copy(
                        OS.v.rearrange("p (h s) x -> p h (s x)", s=2),
                        PSA[:, 6 * 512:8 * 512].rearrange("p (a n) -> p a n", n=512)[:, :, 0:260]),
                        reads=PS(6, 7), writes=OS.r())
                    P.op('dve', lambda e: e.reciprocal(RR.v[:, 0, 0:8].rearrange("p (a q) -> p a q", q=2), osv[:, :, :, 64]),
                         reads=OS.r(), writes=RR.r())
                    P.op('dve', lambda e: e.tensor_tensor(
                        osv[:, :, :, 0:64], osv[:, :, :, 0:64],
                        RR.v[:, 0, 0:8].rearrange("p (a q) -> p a q", q=2).unsqueeze(3).to_broadcast([128, 4, 2, 64]), ALU.mult),
                        reads=OS.r() + RR.r(), writes=OS.r())
                    os5 = OS.v[:, :, :].rearrange("p (h s) (q x) -> p h s q x", s=2, x=65)
                    a1v = A1.v[:, :, :].rearrange("p h (q x) -> p h q x", x=64)
                    for h2 in range(2):
                        P.op('dve', lambda e, h2=h2: e.scalar_tensor_tensor(
                            a1v[:, h2], os5[:, h2, 1, :, 0:64], nlam, os5[:, h2, 0, :, 0:64], ALU.mult, ALU.add),
                            reads=OS.r() + LAM.r(e_), writes=A1.r(h2))
                    sqv = OS.v[:, 0:2, 0:128].rearrange("p h (q x) -> p h q x", x=64)
                    P.op('dve', lambda e: e.tensor_tensor(sqv, a1v, a1v, ALU.mult), reads=A1.r(), writes=OS.r())
                    P.op('dve', lambda e: e.tensor_reduce(RR.v[:, 0, 8:12].rearrange("p (h q) -> p h q", q=2), sqv, AX.X, ALU.add),
                         reads=OS.r(), writes=RR.r())

                def norm_B(qb):
                    atok = ATOK[qb % 2]
                    a1v = A1.v[:, :, :].rearrange("p h (q x) -> p h q x", x=64)
                    P.op('act', lambda e: e.activation(RR.v[:, 0, 8:12], RR.v[:, 0, 8:12], AF.Ln, bias=epsc, scale=1.0 / 64),
                         reads=RR.r() + EPSC.r(), writes=RR.r())
                    P.op('act', lambda e: e.activation(RR.v[:, 0, 8:12], RR.v[:, 0, 8:12], AF.Exp, scale=-0.5),
                         reads=RR.r(), writes=RR.r())
                    P.op('dve', lambda e: e.tensor_tensor(
                        a1v, a1v, RR.v[:, 0, 8:12].rearrange("p (h q) -> p h q", q=2).unsqueeze(3).to_broadcast([128, 2, 2, 64]),
                        ALU.mult), reads=A1.r() + RR.r(), writes=A1.r())
                    av = atok.v[:, 0, :].rearrange("p (q h x) -> p h q x", q=2, h=2)
                    P.op('dve', lambda e, av=av: e.tensor_tensor(
                        av, a1v, SGR.v[:, e_:e_ + 1, :].unsqueeze(1).to_broadcast([128, 2, 2, 64]), ALU.mult),
                        reads=A1.r() + SGR.r(e_), writes=atok.r())

                def trans(qb):
                    atok = ATOK[qb % 2]
                    q0 = qb * 256
                    for qs in range(2):
                        tb = 2 * (qb % 3) + qs
                        P.op('pe', lambda e, qs=qs, tb=tb, atok=atok: e.transpose(
                            bk(tb, 0, 128), atok.v[:, 0, qs * 128:(qs + 1) * 128], ident),
                            reads=atok.r() + CF.r(0), writes=PS(tb))
                        P.op('dve', lambda e, qs=qs, tb=tb, c=c, q0=q0: e.tensor_copy(
                            ATT.v[:, c, q0 + 128 * qs:q0 + 128 * qs + 128], bk(tb, 0, 128)),
                            reads=PS(tb), writes=ATT.r(c, q0 + 128 * qs, q0 + 128 * qs + 128))

                deferred = []
                emit_S(0)
                emit_S(1)
                for i in range(nst):
                    if i + 2 < nst:
                        emit_S(i + 2)
                    emit_PV(i)
                    qb, hh, kb = steps[i]
                    if hh == 1 and kb == 2 * qb + 1:
                        norm_A(qb)
                        deferred.append((i + 2, norm_B, qb))
                        deferred.append((i + 5, trans, qb))
                    for item in list(deferred):
                        if item[0] <= i:
                            item[1](item[2])
                            deferred.remove(item)
                for item in sorted(deferred, key=lambda t: t[0]):
                    item[1](item[2])

            Dd = Alloc(big, LIMIT)
            Dd.cur = S0
            KP2 = [Dd.ten(TS, 512, BF16) for _ in range(3)]
            VP2 = [Dd.ten(TS, 512, BF16) for _ in range(3)]
            PRD = Dd.ten(TS, 512, BF16)
            SS2 = Dd.ten(TS, 16, F32)
            PD2 = [Dd.ten(TS, 48, BF16) for _ in range(3)]
            KP1 = Dd.at(KP2[0].off, 1, 512, BF16)
            VP1 = Dd.at(VP2[0].off, 1, 512, BF16)
            SS1 = Dd.ten(2, 16, F32)
            PD1 = Dd.ten(2, 16, BF16)
            DG = Dd.ten(4, 128, F32)
            DGB = Dd.ten(4, 128, F32)
            QB2 = Dd.ten(1, 512, BF16)
            QB = Dd.ten(1, 512, BF16)
            KSF = Dd.ten(1, 512, BF16)
            VSF = Dd.ten(1, 512, BF16)
            OM = Dd.at(DGB.off, 1, 512, F32)
            R1 = Dd.ten(1, 64, F32)
            AS = Dd.ten(1, 64, F32)
            AS2 = Dd.ten(1, 128, F32)
            RD = Dd.ten(1, 8, F32)
            assert Dd.cur <= LIMIT, (Dd.cur, LIMIT)
            ck1, cv1 = cache_k[e_], cache_v[e_]
            ck2 = cache_k[e_].rearrange("(r t) f -> r (t f)", t=TS)
            cv2 = cache_v[e_].rearrange("(r t) f -> r (t f)", t=TS)
            negm = c2v('negm', 0, 128)
            LA, LB = CF.v[:, 5, :], CF.v[:, 6, :]
            gctr = [0]

            def diag(dst, src, s):
                P.op('dve', lambda e: e.tensor_tensor(
                    dst.v, CF.v[:, 0:1, :].to_broadcast([128, 4, 128]),
                    src.v[:, :, s:s + 1].to_broadcast([128, 4, 128]), ALU.mult),
                    reads=CF.r(0) + src.r(), writes=dst.r())

            for pd_ in PD2:
                P.op('pool', lambda e, pd_=pd_: e.memset(pd_.v, 0.0), writes=pd_.r())
            for pair in range(2):
                sA, sB = 2 * pair, 2 * pair + 1
                diag(DG, QS, sA)
                diag(DGB, QS, sB)
                P.op('pe', lambda e: e.matmul(bk(0), LA, DG.v.rearrange("p a b -> p (a b)"), start=True, stop=False),
                     reads=DG.r() + CF.r(5), writes=PS(0))
                P.op('pe', lambda e: e.matmul(bk(0), LB, DGB.v.rearrange("p a b -> p (a b)"), start=False, stop=True),
                     reads=DGB.r() + CF.r(6), writes=PS(0))
                P.op('act', lambda e: e.activation(QB2.v[:, 0, :], bk(0), AF.Copy), reads=PS(0), writes=QB2.r())
                for g in range(NSTEP):
                    st_ = gctr[0] % 3
                    gctr[0] += 1
                    kp, vp, pd = KP2[st_], VP2[st_], PD2[st_]
                    P.op('pool', lambda e, g=g, kp=kp, pair=pair: e.indirect_dma_start(
                        out=kp.v.rearrange("p a b -> p (a b)"), out_offset=None, in_=ck2,
                        in_offset=bass.IndirectOffsetOnAxis(ap=IDX2.v[:, pair, g:g + 1], axis=0)),
                        reads=IDX2.r(pair), writes=kp.r(), chan=('kp', st_))
                    P.op('pool', lambda e, g=g, vp=vp, pair=pair: e.indirect_dma_start(
                        out=vp.v.rearrange("p a b -> p (a b)"), out_offset=None, in_=cv2,
                        in_offset=bass.IndirectOffsetOnAxis(ap=IDX2.v[:, pair, g:g + 1], axis=0)),
                        reads=IDX2.r(pair), writes=vp.r(), chan=('vp', st_))
                    P.op('dve', lambda e, kp=kp: e.tensor_tensor(
                        PRD.v, kp.v, QB2.v.to_broadcast([128, TS, 512]), ALU.mult),
                        reads=kp.r() + QB2.r(), writes=PRD.r())
                    P.op('dve', lambda e: e.tensor_reduce(
                        SS2.v.rearrange("p a b -> p (a b)"), PRD.v.rearrange("p a (h x) -> p (a h) x", x=32), AX.X, ALU.add),
                        reads=PRD.r(), writes=SS2.r())
                    for half in range(2):
                        p0, p1 = 64 * half, 64 * half + 64
                        P.op('act', lambda e, pd=pd, p0=p0, p1=p1, half=half: e.activation(
                            pd.v[p0:p1, :, 32 * half:32 * half + 16], SS2.v[p0:p1, :, :], AF.Exp,
                            bias=c2v('negm', p0, p1), scale=SCALE),
                            reads=SS2.r() + C2.r(), writes=pd.r())
                    for t in range(TS):
                        first = (g == 0 and t == 0)
                        P.op('pe', lambda e, t=t, pd=pd, vp=vp, first=first: e.matmul(
                            bk(3)[0:48, :], pd.v[:, t, :], vp.v[:, t, :], start=first, stop=False),
                            reads=pd.r(t) + vp.r(t), writes=PS(3))
                        P.op('pe', lambda e, t=t, pd=pd, first=first: e.matmul(
                            bk(5, 0, 1)[0:48, :], pd.v[:, t, :], ones_b[:, 0:1], start=first, stop=False),
                            reads=pd.r(t) + CB.r(4), writes=PS(5))
                for half, s in enumerate((sA, sB)):
                    ob_, db_ = 3, 5
                    r0, r1 = 32 * half, 32 * half + 16
                    for (src, dst, b) in ((QS, QB, 0), (KS, KSF, 1), (VS, VSF, 2)):
                        diag(DG, src, s)
                        P.op('pe', lambda e, b=b: e.matmul(bk(b), ones_f, DG.v.rearrange("p a b -> p (a b)"), start=True, stop=True),
                             reads=DG.r() + CF.r(4), writes=PS(b))
                        P.op('act', lambda e, dst=dst, b=b: e.activation(dst.v[:, 0, :], bk(b), AF.Copy), reads=PS(b), writes=dst.r())
                    P.op('pool', lambda e, s=s: e.indirect_dma_start(
                        out=KP1.v[:, 0, :], out_offset=None, in_=ck1,
                        in_offset=bass.IndirectOffsetOnAxis(ap=IDX63.v[:, 0, s:s + 1], axis=0)),
                        reads=IDX63.r(), writes=KP1.r(), chan='kp1')
                    P.op('pool', lambda e, s=s: e.indirect_dma_start(
                        out=VP1.v[:, 0, :], out_offset=None, in_=cv1,
                        in_offset=bass.IndirectOffsetOnAxis(ap=IDX63.v[:, 0, s:s + 1], axis=0)),
                        reads=IDX63.r(), writes=VP1.r(), chan='vp1')
                    for k, ksrc in enumerate((KP1, KSF)):
                        P.op('dve', lambda e, ksrc=ksrc: e.tensor_tensor(PRD.v[:, 0, :], ksrc.v[:, 0, :], QB.v[:, 0, :], ALU.mult),
                             reads=ksrc.r() + QB.r(), writes=PRD.r(0))
                        P.op('dve', lambda e, k=k: e.tensor_reduce(
                            SS1.v[:, k, :], PRD.v[:, 0, :].rearrange("p (h x) -> p h x", x=32), AX.X, ALU.add),
                            reads=PRD.r(0), writes=SS1.r(k))
                    P.op('dve', lambda e: e.scalar_tensor_tensor(
                        SS1.v.rearrange("p k (h x) -> p k h x", x=2)[:, 0], SS1.v.rearrange("p k (h x) -> p k h x", x=2)[:, 0], SCALE,
                        BDEC.v[:, 0, :].unsqueeze(2).to_broadcast([128, 8, 2]), ALU.mult, ALU.add),
                        reads=SS1.r(0) + BDEC.r(0), writes=SS1.r(0))
                    P.op('dve', lambda e: e.scalar_tensor_tensor(
                        SS1.v.rearrange("p k (h x) -> p k h x", x=2)[:, 1], SS1.v.rearrange("p k (h x) -> p k h x", x=2)[:, 1], SCALE,
                        BDEC.v[:, 1, :].unsqueeze(2).to_broadcast([128, 8, 2]), ALU.mult, ALU.add),
                        reads=SS1.r(1) + BDEC.r(1), writes=SS1.r(1))
                    P.op('act', lambda e: e.activation(PD1.v, SS1.v, AF.Exp), reads=SS1.r(), writes=PD1.r())
                    for k, vsrc in enumerate((VP1, VSF)):
                        P.op('pe', lambda e, k=k, vsrc=vsrc, ob_=ob_, r0=r0, r1=r1: e.matmul(
                            bk(ob_)[r0:r1, :], PD1.v[:, k, :], vsrc.v[:, 0, :], start=False, stop=(k == 1),
                            tile_position=(0, r0), skip_group_check=True),
                            reads=PD1.r(k) + vsrc.r(), writes=PS(ob_))
                        P.op('pe', lambda e, k=k, db_=db_, r0=r0, r1=r1: e.matmul(
                            bk(db_, 0, 1)[r0:r1, :], PD1.v[:, k, :], ones_b[:, 0:1], start=False, stop=(k == 1),
                            tile_position=(0, r0), skip_group_check=True),
                            reads=PD1.r(k) + CB.r(4), writes=PS(db_))
                    P.op('dve', lambda e, db_=db_, r0=r0, r1=r1: e.reciprocal(RD.v[r0:r1, 0, 0:1], bk(db_, 0, 1)[r0:r1, :]), reads=PS(db_), writes=RD.r())
                    P.op('dve', lambda e, r0=r0, r1=r1: e.tensor_tensor(RD.v[r0:r1, 0, 0:1], RD.v[r0:r1, 0, 0:1], LAMC.v[r0:r1, e_, :], ALU.mult),
                         reads=RD.r() + LAMC.r(e_), writes=RD.r())
                    P.op('dve', lambda e, ob_=ob_, r0=r0, r1=r1: e.tensor_tensor(OM.v[r0:r1, 0, :], bk(ob_)[r0:r1, :], c2v('maskd', r0, r1), ALU.mult),
                         reads=PS(ob_) + C2.r(), writes=OM.r())
                    P.op('dve', lambda e, r0=r0, r1=r1: e.tensor_reduce(
                        R1.v[r0:r1, 0, :], OM.v[r0:r1, 0, :].rearrange("p (h x) -> p x h", x=64), AX.X, ALU.add),
                        reads=OM.r(), writes=R1.r())
                    P.op('dve', lambda e, r0=r0, r1=r1: e.tensor_scalar(R1.v[r0:r1, 0, :], R1.v[r0:r1, 0, :], RD.v[r0:r1, 0, 0:1], None, ALU.mult),
                         reads=R1.r() + RD.r(), writes=R1.r())
                    P.op('pe', lambda e, r0=r0, r1=r1: e.matmul(bk(0, 0, 64)[0:8, :], c2v('pair', r0, r1), R1.v[r0:r1, 0, :], start=True, stop=True),
                         reads=R1.r() + C2.r(), writes=PS(0))
                    P.op('dve', lambda e: e.tensor_copy(AS.v[0:8, 0, :], bk(0, 0, 64)[0:8, :]), reads=PS(0), writes=AS.r())
                    P.op('dve', lambda e: e.tensor_tensor(R1.v[0:8, 0, :], AS.v[0:8, 0, :], AS.v[0:8, 0, :], ALU.mult),
                         reads=AS.r(), writes=R1.r())
                    P.op('dve', lambda e: e.tensor_reduce(RD.v[0:8, 0, 1:2], R1.v[0:8, 0, :], AX.X, ALU.add),
                         reads=R1.r(), writes=RD.r())
                    P.op('act', lambda e: e.activation(RD.v[0:8, 0, 1:2], RD.v[0:8, 0, 1:2], AF.Ln, bias=EPSC.v[0:8, 0, 0:1], scale=1.0 / 64),
                         reads=RD.r() + EPSC.r(), writes=RD.r())
                    P.op('act', lambda e: e.activation(RD.v[0:8, 0, 1:2], RD.v[0:8, 0, 1:2], AF.Exp, scale=-0.5),
                         reads=RD.r(), writes=RD.r())
                    P.op('dve', lambda e: e.tensor_scalar(AS.v[0:8, 0, :], AS.v[0:8, 0, :], RD.v[0:8, 0, 1:2], None,
                                                          ALU.mult), reads=AS.r() + RD.r(), writes=AS.r())
                    P.op('dve', lambda e: e.tensor_tensor(AS.v[0:8, 0, :], AS.v[0:8, 0, :], SGR.v[0:8, e_, :], ALU.mult),
                         reads=AS.r() + SGR.r(e_), writes=AS.r())
                    P.op('dve', lambda e: e.tensor_tensor(
                        AS2.v[0:8, 0, :].rearrange("p (r x) -> p r x", x=64), AS.v[0:8, 0:1, :].to_broadcast([8, 2, 64]),
                        c2v('maskp', 0, 8).rearrange("p (r x) -> p r x", x=64), ALU.mult),
                        reads=AS.r() + C2.r(), writes=AS2.r())
                    P.op('pe', lambda e: e.matmul(bk(1, 0, 4), AS2.v[0:8, 0, :], c2v('selc', 0, 8), start=True, stop=True),
                         reads=AS2.r() + C2.r(), writes=PS(1))
                    P.op('dve', lambda e, s=s: e.tensor_copy(ATT.v[:, :, NPROMPT + s], bk(1, 0, 4)),
                         reads=PS(1), writes=ATT.r(None, NPROMPT + s, NPROMPT + s + 1))

            Cc = Alloc(big, LIMIT)
            Cc.cur = S0
            UB = Cc.ten(1, 2 + NT, F32)
            GC = [Cc.ten(1, 512, F32) for _ in range(2)]
            CV = [Cc.ten(1, 512, F32) for _ in range(2)]
            assert Cc.cur <= LIMIT
            scbv = SCB.v[:, e_, :].rearrange("p (c k s) -> p c k s", c=4, k=2)
            cc_ = [0]
            for i in range(4):
                hgc = ws.get()
                hx = ws.get()
                hgb = ws.get()
                w0, w1, w2 = (vcol(('cbw', e_), k * 4 + i) for k in range(3))
                P.op('dve', lambda e: e.memset(UB.v[:, 0, 0:2], 0.0), writes=UB.r(0, 0, 2))
                for ti, (lo, hi) in enumerate(MT):
                    n = hi - lo
                    b1, b2, b3 = 0 + 3 * (cc_[0] % 2), 1 + 3 * (cc_[0] % 2), 2 + 3 * (cc_[0] % 2)
                    gc, cvt = GC[cc_[0] % 2], CV[cc_[0] % 2]
                    cc_[0] += 1
                    proj(hgc, XNS, b1, lo, hi)
                    proj(hx, XNS, b2, lo, hi)
                    proj(hgb, XNS, b3, lo, hi)
                    P.op('act', lambda e, n=n, b1=b1, gc=gc: e.activation(gc.v[:, 0, 0:n], bk(b1, 0, n), AF.Copy),
                         reads=PS(b1), writes=gc.r(0, 0, n))
                    P.op('dve', lambda e, n=n, b2=b2, gc=gc, lo=lo, hi=hi: e.tensor_tensor(
                        UB.v[:, 0, 2 + lo:2 + hi], gc.v[:, 0, 0:n], bk(b2, 0, n), ALU.mult),
                        reads=gc.r(0, 0, n) + PS(b2), writes=UB.r(0, 2 + lo, 2 + hi))
                    if ti < 4:
                        P.op('dve', lambda e, n=n, lo=lo, cvt=cvt, w0=w0: e.tensor_scalar(
                            cvt.v[:, 0, 0:n], UB.v[:, 0, lo:lo + n], w0, None, ALU.mult),
                            reads=UB.r(0, lo, lo + n) + VEC.r(), writes=cvt.r(0, 0, n))
                        P.op('dve', lambda e, n=n, lo=lo, cvt=cvt, w1=w1: e.scalar_tensor_tensor(
                            cvt.v[:, 0, 0:n], UB.v[:, 0, 1 + lo:1 + lo + n], w1, cvt.v[:, 0, 0:n], ALU.mult, ALU.add),
                            reads=UB.r(0, 1 + lo, 1 + lo + n) + VEC.r() + cvt.r(0, 0, n), writes=cvt.r(0, 0, n))
                        P.op('dve', lambda e, n=n, lo=lo, cvt=cvt, w2=w2: e.scalar_tensor_tensor(
                            cvt.v[:, 0, 0:n], UB.v[:, 0, 2 + lo:2 + lo + n], w2, cvt.v[:, 0, 0:n], ALU.mult, ALU.add),
                            reads=UB.r(0, 2 + lo, 2 + lo + n) + VEC.r() + cvt.r(0, 0, n), writes=cvt.r(0, 0, n))
                    else:
                        P.op('dve', lambda e, cvt=cvt, w0=w0, i=i: e.tensor_scalar(
                            cvt.v[:, 0, 0:4], scbv[:, i, 0, :], w0, None, ALU.mult),
                            reads=SCB.r(e_) + VEC.r(), writes=cvt.r(0, 0, 4))
                        P.op('dve', lambda e, cvt=cvt, w1=w1, i=i: e.scalar_tensor_tensor(
                            cvt.v[:, 0, 0:4], scbv[:, i, 1, :], w1, cvt.v[:, 0, 0:4], ALU.mult, ALU.add),
                            reads=SCB.r(e_) + VEC.r() + cvt.r(0, 0, 4), writes=cvt.r(0, 0, 4))
                        P.op('dve', lambda e, cvt=cvt, w2=w2: e.scalar_tensor_tensor(
                            cvt.v[:, 0, 0:4], UB.v[:, 0, 2 + NPROMPT:2 + NT], w2, cvt.v[:, 0, 0:4], ALU.mult, ALU.add),
                            reads=UB.r(0, 2 + NPROMPT, 2 + NT) + VEC.r() + cvt.r(0, 0, 4), writes=cvt.r(0, 0, 4))
                    P.op('dve', lambda e, n=n, b3=b3, cvt=cvt, lo=lo, hi=hi, i=i: e.tensor_tensor(
                        CATB.v[:, i, lo:hi], cvt.v[:, 0, 0:n], bk(b3, 0, n), ALU.mult),
                        reads=cvt.r(0, 0, n) + PS(b3), writes=CATB.r(i, lo, hi))
                P.op('sp', lambda e, i=i: e.dma_start(out=cbpT[e_, :, i, :], in_=UB.v[:, 0, NPROMPT:NPROMPT + 2]),
                     reads=UB.r(0, NPROMPT, NPROMPT + 2), chan='ocb')
                P.op('sp', lambda e, i=i: e.dma_start(out=cbs0T[e_, :, i, :], in_=scbv[:, i, 1, :]),
                     reads=SCB.r(e_), chan='ocb')
                P.op('sp', lambda e, i=i: e.dma_start(out=cbs1T[e_, :, i, :], in_=UB.v[:, 0, 2 + NPROMPT:2 + NT]),
                     reads=UB.r(0, 2 + NPROMPT, 2 + NT), chan='ocb')

            CATS = [(ATT, k) for k in range(4)] + [(CATB, k) for k in range(4)]
            oc2 = [0]
            for dch in range(8):
                ho = ws.get()
                for (lo, hi) in FT:
                    n = hi - lo
                    b = oc2[0] % 4
                    oc2[0] += 1
                    proj(ho, CATS, b, lo, hi)
                    P.op('dve', lambda e, dch=dch, lo=lo, hi=hi, n=n, b=b: e.tensor_tensor(
                        X.v[:, dch, lo:hi], bk(b, 0, n), X.v[:, dch, lo:hi], ALU.add),
                        reads=PS(b) + X.r(dch, lo, hi), writes=X.r(dch, lo, hi))

        def odd_mixer(l):
            o_ = l // 2
            B = Alloc(big, LIMIT)
            B.cur = R0
            CC = B.ten(8, NT, BF16)
            UO2 = [B.ten(1, 30 + NT, BF16) for _ in range(2)]
            UF2 = [B.ten(1, 34, F32) for _ in range(2)]
            UREP = B.ten(4, 30 + NT + 2, BF16)
            LW = B.ten(32, 32, BF16)
            WR = B.ten(4, 32, F32)
            SCS = B.ten(8, 120, F32)
            PRS = B.ten(4, 30, F32)
            CS4 = B.ten(1, 8, F32)
            B3 = Alloc(big, UREP.off + UREP.nbytes)
            B3.cur = UREP.off
            SQ = B3.ten(8, 342, BF16)
            MU = [B3.ten(1, 342, F32) for _ in range(2)]
            T1 = [B3.ten(1, 342, F32) for _ in range(2)]
            items = []
            for i in range(8):
                items += [(w_pw1[o_], i * 128), (w_pw1[o_], (8 + i) * 128)]
            for dch in range(8):
                items += [(w_pw2[o_], dch * 128)]
            ws = WStream(items, 4)
            ws.fill()
            rmsnorm(('nm', l), FT)
            P.op('sp', lambda e: e.dma_start(out=SCS.v, in_=scc[:, o_].rearrange("p c s k -> p c (s k)")),
                 writes=SCS.r(), chan='ld')
            for UO in UO2:
                P.op('pool', lambda e, UO=UO: e.memset(UO.v[:, 0, 0:30], 0.0), writes=UO.r(0, 0, 30))
            ccw0 = lay[('ccw', o_)][0]
            pc = [0]

            def stageA(i):
                UO, UF = UO2[i % 2], UF2[i % 2]
                ha = ws.get()
                hb = ws.get()
                ba_, bb_ = vcol(('bpw1', o_), i), vcol(('bpw1', o_), 8 + i)
                for ti, (lo, hi) in enumerate(FT):
                    n = hi - lo
                    b1 = 2 * (pc[0] % 2)
                    b2 = b1 + 1
                    sg = SG[pc[0] % 3]
                    pc[0] += 1
                    proj(ha, XNS, b1, lo, hi)
                    proj(hb, XNS, b2, lo, hi)
                    P.op('act', lambda e, n=n, b2=b2, sg=sg, bb_=bb_: e.activation(sg.v[:, 0, 0:n], bk(b2, 0, n), AF.Sigmoid, bias=bb_),
                         reads=PS(b2) + VEC.r(), writes=sg.r())
                    P.op('dve', lambda e, n=n, b1=b1, sg=sg, lo=lo, hi=hi, ba_=ba_, UO=UO: e.scalar_tensor_tensor(
                        UO.v[:, 0, 30 + lo:30 + hi], bk(b1, 0, n), ba_, sg.v[:, 0, 0:n], ALU.add, ALU.mult),
                        reads=PS(b1) + sg.r() + VEC.r(), writes=UO.r(0, 30 + lo, 30 + hi))
                    if ti == 5:
                        P.op('dve', lambda e, n=n, b1=b1, sg=sg, ba_=ba_, UF=UF: e.scalar_tensor_tensor(
                            UF.v[:, 0, :], bk(b1, n - 34, n), ba_, sg.v[:, 0, n - 34:n], ALU.add, ALU.mult),
                            reads=PS(b1) + sg.r() + VEC.r(), writes=UF.r())

            def stageR(i):
                UO = UO2[i % 2]
                for r in range(4):
                    for cg in range(4):
                        P.op('sp', lambda e, r=r, cg=cg, UO=UO: e.dma_start(
                            out=UREP.v[32 * r:32 * r + 32, cg, 0:2076], in_=UO.v[32 * cg:32 * cg + 32, 0, r:r + 2076]),
                            reads=UO.r(0, r, r + 2076), writes=[('urep', cg, r)], chan=('ur', (r * 4 + cg) % 8))

            def stageB(i):
                UO, UF = UO2[i % 2], UF2[i % 2]
                wv = VEC.v[:, 0, ccw0 + i: ccw0 + 248: 8]
                P.op('sp', lambda e, i=i: e.dma_start(out=WR.v, in_=ccwrep[o_, i]), writes=WR.r(), chan='wr')
                for r in range(4):
                    P.op('dve', lambda e, r=r: e.tensor_tensor(
                        LW.v[32 * r:32 * r + 32, :, :],
                        CB.v[32 * r:32 * r + 32, 0:1, 32 * r:32 * r + 32].to_broadcast([32, 32, 32]),
                        WR.v[32 * r:32 * r + 32, :, r::4].rearrange("p c j -> p (c j)").unsqueeze(2).to_broadcast([32, 32, 32]),
                        ALU.mult), reads=CB.r(0) + WR.r(), writes=LW.r())
                cb_ = vcol(('ccb', o_), i)
                for blk in range(4):
                    t0 = blk * 512
                    b = 4 + (pc[0] % 2)
                    pc[0] += 1
                    for jg in range(8):
                        for cg in range(4):
                            P.op('pe', lambda e, jg=jg, cg=cg, t0=t0, b=b: e.matmul(
                                bk(b)[32 * cg:32 * cg + 32, :], LW.v[:, cg * 8 + jg, :],
                                UREP.v[:, cg, t0 + 4 * jg:t0 + 4 * jg + 512], start=(jg == 0), stop=(jg == 7),
                                tile_position=(0, 32 * cg), skip_group_check=True),
                                reads=LW.r(cg * 8 + jg) + [('urep', cg, r_) for r_ in range(4)], writes=PS(b))
                    P.op('act', lambda e, t0=t0, b=b, i=i, cb_=cb_: e.activation(
                        CC.v[:, i, t0:t0 + 512], bk(b), AF.Identity, bias=cb_),
                        reads=PS(b) + VEC.r(), writes=CC.r(i, t0, t0 + 512))
                P.op('dve', lambda e, i=i, wv=wv: e.tensor_tensor(
                    PRS.v, SCS.v[:, i, :].rearrange("p (s k) -> p s k", k=30),
                    wv[:, 0:30].unsqueeze(1).to_broadcast([128, 4, 30]), ALU.mult),
                    reads=SCS.r(i) + VEC.r(), writes=PRS.r())
                P.op('dve', lambda e: e.tensor_reduce(CS4.v[:, 0, 0:4], PRS.v, AX.X, ALU.add), reads=PRS.r(), writes=CS4.r())
                P.op('dve', lambda e, wv=wv, UF=UF: e.scalar_tensor_tensor(
                    CS4.v[:, 0, 0:4], UF.v[:, 0, 30:34], wv[:, 30:31], CS4.v[:, 0, 0:4], ALU.mult, ALU.add),
                    reads=UF.r() + VEC.r() + CS4.r(), writes=CS4.r())
                P.op('dve', lambda e, i=i, cb_=cb_: e.tensor_scalar(
                    CC.v[:, i, NPROMPT:NT], CS4.v[:, 0, 0:4], cb_, None, ALU.add),
                    reads=CS4.r() + VEC.r(), writes=CC.r(i, NPROMPT, NT))
                P.op('sp', lambda e, i=i, UF=UF: e.dma_start(out=ccpT[o_, :, i, :], in_=UF.v[:, 0, 0:30]), reads=UF.r(), chan='occ')
                P.op('sp', lambda e, i=i: e.dma_start(
                    out=ccsoT[o_, :, i, :, :], in_=SCS.v[:, i, :].rearrange("p (s k) -> p s k", k=30)[:, :, 1:30]),
                    reads=SCS.r(i), chan='occ')
                P.op('sp', lambda e, i=i, UF=UF: e.dma_start(out=ccsnT[o_, :, i, :], in_=UF.v[:, 0, 30:34]),
                     reads=UF.r(), chan='occ')

            stageA(0)
            stageR(0)
            for i in range(8):
                if i + 1 < 8:
                    stageA(i + 1)
                stageB(i)
                if i + 1 < 8:
                    stageR(i + 1)
            lc = [0]
            hos = [ws.get() for _ in range(8)]
            oc2 = [0]
            for (lo, hi) in FT:
                n = hi - lo
                bm, bv = 0 + 2 * (lc[0] % 2), 1 + 2 * (lc[0] % 2)
                mu, t1 = MU[lc[0] % 2], T1[lc[0] % 2]
                lc[0] += 1
                P.op('act', lambda e, lo=lo, hi=hi, n=n: e.activation(SQ.v[:, :, 0:n], CC.v[:, :, lo:hi], AF.Square),
                     reads=CC.r(None, lo, hi), writes=SQ.r(None, 0, n))
                for kc in range(8):
                    P.op('pe', lambda e, kc=kc, n=n, bm=bm, lo=lo, hi=hi: e.matmul(
                        bk(bm, 0, n), mean_b, CC.v[:, kc, lo:hi], start=(kc == 0), stop=(kc == 7)),
                        reads=CC.r(kc, lo, hi) + CB.r(1), writes=PS(bm))
                for kc in range(8):
                    P.op('pe', lambda e, kc=kc, n=n, bv=bv: e.matmul(
                        bk(bv, 0, n), mean_b, SQ.v[:, kc, 0:n], start=(kc == 0), stop=(kc == 7)),
                        reads=SQ.r(kc, 0, n) + CB.r(1), writes=PS(bv))
                P.op('act', lambda e, n=n, bm=bm, mu=mu: e.activation(mu.v[:, 0, 0:n], bk(bm, 0, n), AF.Copy),
                     reads=PS(bm), writes=mu.r())
                P.op('dve', lambda e, n=n, mu=mu, t1=t1: e.tensor_tensor(t1.v[:, 0, 0:n], mu.v[:, 0, 0:n], mu.v[:, 0, 0:n], ALU.mult),
                     reads=mu.r(), writes=t1.r())
                P.op('dve', lambda e, n=n, bv=bv, t1=t1: e.tensor_tensor(t1.v[:, 0, 0:n], bk(bv, 0, n), t1.v[:, 0, 0:n], ALU.subtract),
                     reads=PS(bv) + t1.r(), writes=t1.r())
                P.op('act', lambda e, n=n, t1=t1: e.activation(t1.v[:, 0, 0:n], t1.v[:, 0, 0:n], AF.Sqrt, bias=epsc),
                     reads=t1.r() + EPSC.r(), writes=t1.r())
                P.op('dve', lambda e, n=n, t1=t1: e.reciprocal(t1.v[:, 0, 0:n], t1.v[:, 0, 0:n]), reads=t1.r(), writes=t1.r())
                for c in range(8):
                    sg = SG[c % 3]
                    P.op('dve', lambda e, c=c, lo=lo, hi=hi, n=n, mu=mu, sg=sg: e.tensor_tensor(
                        sg.v[:, 0, 0:n], CC.v[:, c, lo:hi], mu.v[:, 0, 0:n], ALU.subtract),
                        reads=CC.r(c, lo, hi) + mu.r(), writes=sg.r())
                    P.op('dve', lambda e, n=n, t1=t1, sg=sg: e.tensor_tensor(
                        sg.v[:, 0, 0:n], sg.v[:, 0, 0:n], t1.v[:, 0, 0:n], ALU.mult),
                        reads=sg.r() + t1.r(), writes=sg.r())
                    P.op('act', lambda e, c=c, lo=lo, hi=hi, n=n, sg=sg: e.activation(
                        XN.v[:, c, lo:hi], sg.v[:, 0, 0:n], AF.Silu, bias=vcol(('lnb', o_), c), scale=vcol(('lng', o_), c)),
                        reads=sg.r() + VEC.r(), writes=XN.r(c, lo, hi))
                for dch in range(8):
                    bo = vcol(('bpw2', o_), dch)
                    b = 4 + oc2[0] % 4
                    oc2[0] += 1
                    proj(hos[dch], XNS, b, lo, hi)
                    P.op('dve', lambda e, dch=dch, lo=lo, hi=hi, n=n, b=b, bo=bo: e.scalar_tensor_tensor(
                        X.v[:, dch, lo:hi], bk(b, 0, n), bo, X.v[:, dch, lo:hi], ALU.add, ALU.add),
                        reads=PS(b) + X.r(dch, lo, hi) + VEC.r(), writes=X.r(dch, lo, hi))

        if n_even > 0:
            even_setup()
        for l in range(depth):
            rmsnorm(('n1', l), FT)
            ffn(0, l)
            if l % 2 == 0:
                even_mixer(l)
            else:
                odd_mixer(l)
            rmsnorm(('n2', l), FT)
            ffn(1, l)
        P.op('sp', lambda e: e.dma_start(out=yT, in_=X.v), reads=X.r(), chan='out')
        P.emit()
    return nc


NCORES = 8


def make_in_maps(inp, cores):
    consts, c2, _, dstat = static_tables()
    f32 = lambda a: np.ascontiguousarray(np.asarray(a, np.float32))
    vecsT = build_vecs(inp)
    relb = np.concatenate([f32(inp['rel_bias']), np.ones((1, 8), np.float32)], 0)
    lamv = np.stack([f32(inp['lambda_q1']), f32(inp['lambda_k1']), f32(inp['lambda_q2']), f32(inp['lambda_k2'])], 1)
    shared = dict(vecsT=vecsT, consts=consts, cst2=c2, dstat=dstat, relb=relb, lamv=f32(lamv), sgrow=f32(inp['subln_gain']),
                  )
    ckf = f32(inp['cache_k']).reshape(2, 2560 * 128, 512)
    cvf = f32(inp['cache_v']).reshape(2, 2560 * 128, 512)
    for i in range(2):
        shared['cache_k%d' % i] = ckf[i]
        shared['cache_v%d' % i] = cvf[i]
    for k in ["ffn1_w_gate", "ffn1_w_up", "ffn1_w_down", "ffn2_w_gate", "ffn2_w_up", "ffn2_w_down",
              "w_in_even", "w_out_even", "w_pw1", "w_pw2"]:
        shared[k] = f32(inp[k])
    cw = f32(inp['conv_c_w'])
    cwp = np.zeros((2, 32, 1024), np.float32)
    cwp[:, :31] = cw
    t_ = cwp.reshape(2, 32, 8, 4, 32).transpose(0, 2, 4, 3, 1)
    shared['ccwrep'] = np.ascontiguousarray(np.broadcast_to(t_[:, :, None], (2, 8, 4, 32, 4, 32)).reshape(2, 8, 128, 4, 32))
    maps = []
    for core in cores:
        xp = f32(inp['x_prompt'][core])
        xs = f32(inp['x_sample'][4 * core:4 * core + 4, 0])
        xall = np.concatenate([xp, xs], 0)
        m = dict(shared)
        m['xT'] = np.ascontiguousarray(xall.reshape(NT, 8, 128).transpose(2, 1, 0))
        sb = f32(inp['state_conv_b'][:, 4 * core:4 * core + 4])
        m['scb'] = np.ascontiguousarray(sb.reshape(2, 4, 2, 4, 128).transpose(4, 0, 3, 2, 1))
        sc = f32(inp['state_conv_c'][:, 4 * core:4 * core + 4])
        m['scc'] = np.ascontiguousarray(sc.reshape(2, 4, 30, 8, 128).transpose(4, 0, 3, 1, 2))
        m['ptab'] = np.ascontiguousarray(np.asarray(inp['page_table'], np.int32)[4 * core:4 * core + 4].reshape(1, 256))
        maps.append(m)
    return maps


def assemble(results, ncores):
    f = np.float32
    y_p = np.zeros((ncores, 2048, 1024), f)
    y_s = np.zeros((4 * ncores, 1, 1024), f)
    nk_p = np.zeros((2, ncores, 2048, 16, 32), f)
    nv_p = np.zeros((2, ncores, 2048, 8, 64), f)
    nk_s = np.zeros((2, 4 * ncores, 1, 16, 32), f)
    nv_s = np.zeros((2, 4 * ncores, 1, 8, 64), f)
    cb_p = np.zeros((2, ncores, 2, 512), f)
    cb_s = np.zeros((2, 4 * ncores, 2, 512), f)
    cc_p = np.zeros((2, ncores, 30, 1024), f)
    cc_s = np.zeros((2, 4 * ncores, 30, 1024), f)
    for ci, r in enumerate(results):
        y = r['yT'].transpose(2, 1, 0).reshape(NT, 1024)
        y_p[ci] = y[:2048]
        y_s[4 * ci:4 * ci + 4, 0] = y[2048:]
        for e in range(2):
            k = r['nkT'][e].transpose(2, 1, 0).reshape(NT, 512)
            v = r['nvT'][e].transpose(2, 1, 0).reshape(NT, 512)
            nk_p[e, ci] = k[:2048].reshape(2048, 16, 32)
            nv_p[e, ci] = v[:2048].reshape(2048, 8, 64)
            nk_s[e, 4 * ci:4 * ci + 4, 0] = k[2048:].reshape(4, 16, 32)
            nv_s[e, 4 * ci:4 * ci + 4, 0] = v[2048:].reshape(4, 8, 64)
            cb_p[e, ci] = r['cbpT'][e].transpose(2, 1, 0).reshape(2, 512)
            cb_s[e, 4 * ci:4 * ci + 4, 0] = r['cbs0T'][e].transpose(2, 1, 0).reshape(4, 512)
            cb_s[e, 4 * ci:4 * ci + 4, 1] = r['cbs1T'][e].transpose(2, 1, 0).reshape(4, 512)
            cc_p[e, ci] = r['ccpT'][e].transpose(2, 1, 0).reshape(30, 1024)
            cc_s[e, 4 * ci:4 * ci + 4, 0:29] = r['ccsoT'][e].transpose(2, 3, 1, 0).reshape(4, 29, 1024)
            cc_s[e, 4 * ci:4 * ci + 4, 29] = r['ccsnT'][e].transpose(2, 1, 0).reshape(4, 1024)
    return (y_p, y_s, nk_p, nv_p, nk_s, nv_s, cb_p, cb_s, cc_p, cc_s)


def kernel(**inputs):
    nc = build_nc(DEPTH)
    maps = make_in_maps(inputs, list(range(NCORES)))
    res = run_bass_kernel_spmd(nc, maps, core_ids=list(range(NCORES)))
    return assemble(res.results, NCORES)
```
